# Optimizing a Trainium2 kernel written in Bass

```python
import math
import jax
import jax.numpy as jnp
from jax import lax
import numpy as np

D_MODEL = 1024
BATCH = 4
SEQ = 4096
DEPTH = 4

N_META = 16
NORM_EPS = 1e-6
CHUNK = 128
CONV_K = 4
N_BRANCH = 4

RWKV_WIDTH = D_MODEL
RWKV_HEAD = 64
RWKV_HEADS = RWKV_WIDTH // RWKV_HEAD
RWKV_W_RANK = 64
RWKV_A_RANK = 64
RWKV_G_RANK = 128
RWKV_V_RANK = 32
RWKV_LN_EPS = 64e-5

SSM_WIDTH = D_MODEL
SSM_HEAD = 64
SSM_HEADS = SSM_WIDTH // SSM_HEAD
SSM_GROUPS = 4
SSM_STATE = 128

RET_HEADS = 8
RET_QK_WIDTH = D_MODEL // 2
RET_V_WIDTH = D_MODEL
RET_QK_HEAD = RET_QK_WIDTH // RET_HEADS
RET_V_HEAD = RET_V_WIDTH // RET_HEADS
ROPE_BASE = 10000.0

LRU_WIDTH = D_MODEL
LRU_BLOCKS = 8
LRU_BLOCK = LRU_WIDTH // LRU_BLOCKS
LRU_C = 8.0

FFN_HIDDEN = -(-8 * D_MODEL // (3 * 256)) * 256

RWKV_SIZES = (RWKV_WIDTH, RWKV_WIDTH, RWKV_WIDTH, RWKV_W_RANK, RWKV_A_RANK, RWKV_G_RANK)
SSM_XBC = SSM_WIDTH + 2 * SSM_GROUPS * SSM_STATE
SSM_SIZES = (SSM_WIDTH, SSM_XBC, SSM_HEADS)
RET_SIZES = (RET_QK_WIDTH, RET_QK_WIDTH, RET_V_WIDTH, RET_V_WIDTH)
LRU_SIZES = (LRU_WIDTH, LRU_WIDTH)
GROUP_SIZES = (sum(RWKV_SIZES), sum(SSM_SIZES), sum(RET_SIZES), sum(LRU_SIZES), N_BRANCH * D_MODEL)
IN_WIDTH = sum(GROUP_SIZES)

kernel_name = "hybrid_rwkv7_ssd_retention_rglru_trunk"


def _split(p, sizes):
    return jnp.split(p, np.cumsum(sizes)[:-1].tolist(), axis=-1)


def rms_norm(x, w, eps=NORM_EPS):
    x32 = x.astype(jnp.float32)
    y = x32 * lax.rsqrt(jnp.mean(x32 * x32, axis=-1, keepdims=True) + eps)
    return (y * w.astype(jnp.float32)).astype(x.dtype)


def causal_conv(x, w, b):
    y = lax.conv_general_dilated(x, w[:, None, :].astype(x.dtype), window_strides=(1,),
                                 padding=[(CONV_K - 1, 0)], dimension_numbers=('NWC', 'WIO', 'NWC'),
                                 feature_group_count=x.shape[-1])
    return y + b


def token_shift(p):
    return jnp.pad(p, ((0, 0), (1, 0), (0, 0)))[:, :-1]


def pad_front(t, n):
    return jnp.pad(t, ((0, 0), (n, 0)) + ((0, 0),) * (t.ndim - 2))


def segsum(a):
    n = a.shape[-1]
    cs = jnp.cumsum(a, axis=-1)
    diff = cs[..., :, None] - cs[..., None, :]
    return jnp.where(jnp.tril(jnp.ones((n, n), bool)), diff, -jnp.inf)


def rope(x, pos):
    half = x.shape[-1] // 2
    freq = jnp.power(ROPE_BASE, -jnp.arange(half, dtype=jnp.float32) / half)
    ang = pos[:, None] * freq[None, :]
    cos, sin = jnp.cos(ang)[None, :, None, :], jnp.sin(ang)[None, :, None, :]
    x1, x2 = x[..., :half], x[..., half:]
    return jnp.concatenate([x1 * cos - x2 * sin, x2 * cos + x1 * sin], axis=-1)


def rwkv7_mix(p, mu, w2, a2, g2, vec, r_k, v_first, vres):
    p = p.astype(jnp.float32)
    bsz, t_len, _ = p.shape
    p = p + (token_shift(p) - p) * mu
    r, k, v, wd, ad, gd = _split(p, RWKV_SIZES)
    w0, a0, k_k, k_a, ln_w, ln_b = vec
    w_log = -jax.nn.softplus(-(w0 + jnp.tanh(wd) @ w2)) - 0.5
    decay = jnp.exp(-jnp.exp(w_log))
    a = jax.nn.sigmoid(a0 + ad @ a2)
    g = jax.nn.sigmoid(gd) @ g2
    v_own = v
    if vres is not None:
        v0, v1, v2 = vres
        v = v + (v_first - v) * jax.nn.sigmoid(v0 + (v @ v1) @ v2)

    def hd(t):
        return t.reshape(bsz, t_len, RWKV_HEADS, RWKV_HEAD)

    kk = hd(k * k_k)
    kk = kk / jnp.maximum(jnp.sqrt(jnp.sum(kk * kk, axis=-1, keepdims=True)), 1e-12)
    k = k * (1.0 + (a - 1.0) * k_a)
    r4, w4, k4, v4, a4 = hd(r), hd(decay), hd(k), hd(v), hd(a)

    def step(s, inp):
        r_t, w_t, k_t, v_t, a_t, b_t = inp
        sa = jnp.einsum('bhvk,bhk->bhv', s, a_t)
        s = s * w_t[:, :, None, :] + sa[..., None] * b_t[:, :, None, :] + v_t[..., None] * k_t[:, :, None, :]
        return s, jnp.einsum('bhvk,bhk->bhv', s, r_t)

    s0 = jnp.zeros((bsz, RWKV_HEADS, RWKV_HEAD, RWKV_HEAD), jnp.float32)
    xs = tuple(jnp.moveaxis(t, 1, 0) for t in (r4, w4, k4, v4, -kk, kk * a4))
    _, y = lax.scan(step, s0, xs)
    y = jnp.moveaxis(y, 0, 1)
    mean = jnp.mean(y, axis=-1, keepdims=True)
    var = jnp.mean(jnp.square(y - mean), axis=-1, keepdims=True)
    y = ((y - mean) * lax.rsqrt(var + RWKV_LN_EPS)).reshape(bsz, t_len, RWKV_WIDTH) * ln_w + ln_b
    bonus = jnp.sum(r4 * k4 * r_k, axis=-1, keepdims=True) * v4
    return (y + bonus.reshape(bsz, t_len, RWKV_WIDTH)) * g, v_own


def ssd_mix(p, conv_w, conv_b, dt_bias, a_log, d_skip, norm_w):
    p = p.astype(jnp.float32)
    bsz, t_len, _ = p.shape
    G, E, P, N, L = SSM_GROUPS, SSM_HEADS // SSM_GROUPS, SSM_HEAD, SSM_STATE, CHUNK
    z, xbc, dt = _split(p, SSM_SIZES)
    xbc = jax.nn.silu(causal_conv(xbc, conv_w, conv_b))
    xs, bm, cm = _split(xbc, (SSM_WIDTH, G * N, G * N))
    dt = jax.nn.softplus(dt + dt_bias)
    A = -jnp.exp(a_log.astype(jnp.float32))
    pad = (-t_len) % L
    nc = (t_len + pad) // L

    def chunk(t, *tail):
        return pad_front(t, pad).reshape(bsz, nc, L, *tail)

    x6 = chunk(xs, G, E, P)
    dt5 = chunk(dt, G, E)
    bc = chunk(bm, G, N)
    cc = chunk(cm, G, N)
    dA = jnp.transpose(dt5 * A.reshape(G, E), (0, 3, 4, 1, 2))
    a_cum = jnp.cumsum(dA, axis=-1)
    lmat = jnp.exp(segsum(dA))
    xdt = x6 * dt5[..., None]
    cb = jnp.einsum('bclgn,bcsgn->bcgls', cc, bc)
    y_diag = jnp.einsum('bcgls,bgecls,bcsgep->bclgep', cb, lmat, xdt)
    decay_states = jnp.exp(a_cum[..., -1:] - a_cum)
    states = jnp.einsum('bclgn,bgecl,bclgep->bcgepn', bc, decay_states, xdt)
    states = jnp.concatenate([jnp.zeros_like(states[:, :1]), states], axis=1)
    chunk_decay = jnp.exp(segsum(jnp.pad(a_cum[..., -1], ((0, 0), (0, 0), (0, 0), (1, 0)))))
    states = jnp.einsum('bgezc,bcgepn->bzgepn', chunk_decay, states)[:, :-1]
    y_off = jnp.einsum('bclgn,bcgepn,bgecl->bclgep', cc, states, jnp.exp(a_cum))
    y = y_diag + y_off + x6 * d_skip.reshape(G, E)[..., None]
    y = y.reshape(bsz, nc * L, SSM_WIDTH)[:, pad:] * jax.nn.silu(z)
    yg = y.reshape(bsz, t_len, G, SSM_WIDTH // G)
    yg = yg * lax.rsqrt(jnp.mean(yg * yg, axis=-1, keepdims=True) + 1e-5)
    return yg.reshape(bsz, t_len, SSM_WIDTH) * norm_w


def retention_mix(p):
    p = p.astype(jnp.float32)
    bsz, t_len, _ = p.shape
    H, DK, DV, L = RET_HEADS, RET_QK_HEAD, RET_V_HEAD, CHUNK
    q, k, v, g = _split(p, RET_SIZES)
    pos = jnp.arange(t_len, dtype=jnp.float32)
    q = rope(q.reshape(bsz, t_len, H, DK), pos)
    k = rope(k.reshape(bsz, t_len, H, DK), pos) * (DK ** -0.5)
    v = v.reshape(bsz, t_len, H, DV)
    pad = (-t_len) % L
    nc = (t_len + pad) // L
    qc = pad_front(q, pad).reshape(bsz, nc, L, H, DK)
    kc = pad_front(k, pad).reshape(bsz, nc, L, H, DK)
    vc = pad_front(v, pad).reshape(bsz, nc, L, H, DV)
    log_g = jnp.log1p(-jnp.exp2(-5.0 - jnp.arange(H, dtype=jnp.float32)))
    idx = jnp.arange(L, dtype=jnp.float32)
    rel = idx[:, None] - idx[None, :]
    causal = rel >= 0
    intra = jnp.where(causal, jnp.exp(jnp.where(causal, rel, 0.0)[None] * log_g[:, None, None]), 0.0)
    scores = jnp.einsum('bclhd,bcshd->bchls', qc, kc) * intra
    y_in = jnp.einsum('bchls,bcshe->bclhe', scores, vc)
    k_dec = jnp.exp((L - 1.0 - idx)[None, :] * log_g[:, None])
    q_dec = jnp.exp((idx + 1.0)[None, :] * log_g[:, None])
    chunk_dec = jnp.exp(L * log_g)[None, :, None, None]
    kv = jnp.einsum('bclhd,hl,bclhe->bchde', kc, k_dec, vc)

    def step(r_state, kv_c):
        return chunk_dec * r_state + kv_c, r_state

    _, r_prev = lax.scan(step, jnp.zeros((bsz, H, DK, DV), jnp.float32), jnp.moveaxis(kv, 1, 0))
    y_x = jnp.einsum('bclhd,hl,cbhde->bclhe', qc, q_dec, r_prev)
    y = (y_in + y_x).reshape(bsz, nc * L, H, DV)[:, pad:]
    y = y * lax.rsqrt(jnp.mean(y * y, axis=-1, keepdims=True) + NORM_EPS)
    return jax.nn.silu(g) * y.reshape(bsz, t_len, RET_V_WIDTH)


def rglru_mix(p, conv_w, conv_b, w_gate, b_gate, lam):
    p = p.astype(jnp.float32)
    bsz, t_len, _ = p.shape
    y_in, x_in = _split(p, LRU_SIZES)
    gate = jax.nn.gelu(y_in)
    xc = causal_conv(x_in, conv_w, conv_b)
    xb = xc.reshape(bsz, t_len, LRU_BLOCKS, LRU_BLOCK)
    gates = jnp.einsum('btnc,knce->kbtne', xb, w_gate).reshape(2, bsz, t_len, LRU_WIDTH) + b_gate[:, None, None, :]
    r_gate = jax.nn.sigmoid(gates[0])
    i_gate = jax.nn.sigmoid(gates[1])
    log_a = -LRU_C * r_gate * jax.nn.softplus(-lam)
    a = jnp.exp(log_a)
    u = jnp.sqrt(-jnp.expm1(2.0 * log_a)) * (i_gate * xc)

    def combine(left, right):
        a1, b1 = left
        a2, b2 = right
        return a1 * a2, a2 * b1 + b2

    _, h = lax.associative_scan(combine, (a, u), axis=1)
    return h * gate


def setup_inputs(seed: int = 0) -> dict:
    key = jax.random.key(seed)
    keys = iter(jax.random.split(key, 64))
    f32 = jnp.float32

    def nrm(shape, scale):
        return jax.random.normal(next(keys), shape, f32) * scale

    def uni(shape, lo, hi):
        return jax.random.uniform(next(keys), shape, f32, lo, hi)

    L = DEPTH
    W = RWKV_WIDTH
    dt0 = jnp.exp(uni((L, SSM_HEADS), math.log(1e-3), math.log(1e-1)))
    a_pow = uni((L, LRU_WIDTH), 0.9, 0.999) ** (1.0 / LRU_C)
    rwkv_vec = jnp.stack([uni((L, W), -6.0, -1.0), nrm((L, W), 0.1), 0.85 + nrm((L, W), 0.02),
                          1.0 + nrm((L, W), 0.02), 1.0 + nrm((L, W), 0.02), nrm((L, W), 0.02)], axis=1)
    return {
        "x": nrm((BATCH, SEQ, D_MODEL), 1.0),
        "meta": nrm((N_META, D_MODEL), 1.0),
        "norm_mix": 1.0 + nrm((L, D_MODEL), 0.02),
        "norm_ffn": 1.0 + nrm((L, D_MODEL), 0.02),
        "w_in": nrm((L, D_MODEL, IN_WIDTH), D_MODEL ** -0.5),
        "rwkv_mu": uni((L, sum(RWKV_SIZES)), 0.0, 1.0),
        "rwkv_w2": nrm((L, RWKV_W_RANK, W), 0.1),
        "rwkv_a2": nrm((L, RWKV_A_RANK, W), RWKV_A_RANK ** -0.5),
        "rwkv_g2": nrm((L, RWKV_G_RANK, W), RWKV_G_RANK ** -0.5),
        "rwkv_vec": rwkv_vec,
        "rwkv_rk": nrm((L, RWKV_HEADS, RWKV_HEAD), 0.1),
        "rwkv_v0": nrm((L - 1, W), 0.1),
        "rwkv_v1": nrm((L - 1, W, RWKV_V_RANK), W ** -0.5),
        "rwkv_v2": nrm((L - 1, RWKV_V_RANK, W), RWKV_V_RANK ** -0.5),
        "ssm_conv_w": nrm((L, CONV_K, SSM_XBC), CONV_K ** -0.5),
        "ssm_conv_b": nrm((L, SSM_XBC), 0.02),
        "ssm_dt_bias": dt0 + jnp.log(-jnp.expm1(-dt0)),
        "ssm_a_log": jnp.log(uni((L, SSM_HEADS), 1.0, 16.0)),
        "ssm_d": 1.0 + nrm((L, SSM_HEADS), 0.02),
        "ssm_norm": 1.0 + nrm((L, SSM_WIDTH), 0.02),
        "lru_conv_w": nrm((L, CONV_K, LRU_WIDTH), CONV_K ** -0.5),
        "lru_conv_b": nrm((L, LRU_WIDTH), 0.02),
        "lru_w_gate": nrm((L, 2, LRU_BLOCKS, LRU_BLOCK, LRU_BLOCK), LRU_BLOCK ** -0.5),
        "lru_b_gate": nrm((L, 2, LRU_WIDTH), 0.02),
        "lru_lambda": jnp.log(a_pow) - jnp.log1p(-a_pow),
        "w_branch": nrm((L, N_BRANCH, D_MODEL, D_MODEL), D_MODEL ** -0.5),
        "w_out": nrm((L, D_MODEL, D_MODEL), D_MODEL ** -0.5),
        "w_ffn_in": nrm((L, D_MODEL, 2 * FFN_HIDDEN), D_MODEL ** -0.5),
        "w_ffn_out": nrm((L, FFN_HIDDEN, D_MODEL), FFN_HIDDEN ** -0.5),
        "final_norm": 1.0 + nrm((D_MODEL,), 0.02),
    }


def reference(x, meta, norm_mix, norm_ffn, w_in, rwkv_mu, rwkv_w2, rwkv_a2, rwkv_g2, rwkv_vec, rwkv_rk,
              rwkv_v0, rwkv_v1, rwkv_v2, ssm_conv_w, ssm_conv_b, ssm_dt_bias, ssm_a_log, ssm_d, ssm_norm,
              lru_conv_w, lru_conv_b, lru_w_gate, lru_b_gate, lru_lambda, w_branch, w_out, w_ffn_in,
              w_ffn_out, final_norm):
    bsz = x.shape[0]
    s = jnp.concatenate([jnp.broadcast_to(meta[None].astype(x.dtype), (bsz, N_META, D_MODEL)), x], axis=1)
    t_len = s.shape[1]
    v_first = None
    for l in range(DEPTH):
        hn = rms_norm(s, norm_mix[l])
        p_a, p_b, p_c, p_d, p_g = _split(hn @ w_in[l], GROUP_SIZES)
        vres = None if l == 0 else (rwkv_v0[l - 1], rwkv_v1[l - 1], rwkv_v2[l - 1])
        y_a, v_a = rwkv7_mix(p_a, rwkv_mu[l], rwkv_w2[l], rwkv_a2[l], rwkv_g2[l], rwkv_vec[l], rwkv_rk[l],
                             v_first, vres)
        if l == 0:
            v_first = v_a
        y_b = ssd_mix(p_b, ssm_conv_w[l], ssm_conv_b[l], ssm_dt_bias[l], ssm_a_log[l], ssm_d[l], ssm_norm[l])
        y_c = retention_mix(p_c)
        y_d = rglru_mix(p_d, lru_conv_w[l], lru_conv_b[l], lru_w_gate[l], lru_b_gate[l], lru_lambda[l])
        ys = jnp.stack([y_a, y_b, y_c, y_d], axis=2).astype(s.dtype)
        zb = jnp.einsum('btnc,ncd->btnd', ys, w_branch[l])
        gate = jax.nn.sigmoid(p_g.reshape(bsz, t_len, N_BRANCH, D_MODEL))
        s = s + jnp.sum(gate * zb, axis=2) @ w_out[l]
        hn = rms_norm(s, norm_ffn[l])
        g_in, u_in = jnp.split(hn @ w_ffn_in[l], 2, axis=-1)
        s = s + (jax.nn.silu(g_in) * u_in) @ w_ffn_out[l]
    return rms_norm(s, final_norm)[:, N_META:]
```

```python
import numpy as np
import concourse.bass as bass
import concourse.mybir as mybir

F32 = mybir.dt.float32
BF16 = mybir.dt.bfloat16
AF = mybir.ActivationFunctionType
ALU = mybir.AluOpType
AX = mybir.AxisListType

COMPUTE = ("pe", "act", "dve", "pool")
ALLENG = ("pe", "act", "dve", "pool", "sp")
EPOCH = 30000
NDMASEM = 24
SAME_ENGINE_SYNC = True


class View:
    __slots__ = ("t", "ap", "p0", "p1")

    def __init__(self, t, ap, p0, p1):
        self.t, self.ap, self.p0, self.p1 = t, ap, p0, p1

    def re(self, fn):
        return View(self.t, fn(self.ap), self.p0, self.p1)

    def __getitem__(self, idx):
        return View(self.t, self.ap[idx], self.p0, self.p1)


class Tile:
    def __init__(self, name, h, nparts=1):
        self.name, self.h, self.nparts = name, h, nparts
        self.psum = False
        self.lastw = [None] * nparts
        self.readers = [[] for _ in range(nparts)]

    def __getitem__(self, idx):
        ap = self.h[idx]
        if self.nparts == 1:
            return View(self, ap, 0, 1)
        i = idx[1] if isinstance(idx, tuple) and len(idx) > 1 else slice(None)
        if isinstance(i, int):
            return View(self, ap, i, i + 1)
        p0, p1, _ = i.indices(self.nparts)
        return View(self, ap, p0, p1)

    def all(self):
        return self[:]


class Op:
    __slots__ = ("eng", "fn", "deps", "dma", "pos", "gid", "sig", "signaler", "waits", "dpos")


class Prog:
    def __init__(self, nc):
        self.nc = nc
        self.ops = []
        self.eng_ops = {e: [] for e in ALLENG}
        self.dma_ops = {e: [] for e in ALLENG}
        self.ntile = 0
        self.fence_dma = {}

    def sb(self, name, shape, dtype=F32, parts=1):
        self.ntile += 1
        h = self.nc.alloc_sbuf_tensor(f"{name}_{self.ntile}", list(shape), dtype)
        return Tile(name, h, parts)

    def ps(self, name, shape, dtype=F32, parts=1):
        self.ntile += 1
        h = self.nc.alloc_psum_tensor(f"{name}_{self.ntile}", list(shape), dtype)
        t = Tile(name, h, parts)
        t.psum = True
        return t

    def dram(self, name, shape, dtype=F32, kind="Internal", parts=1):
        h = self.nc.dram_tensor(name, list(shape), dtype, kind=kind).ap()
        return Tile(name, h, parts)

    def add(self, eng, fn, reads=(), writes=(), dma=False):
        op = Op()
        op.eng, op.fn, op.dma = eng, fn, dma
        op.gid = len(self.ops)
        op.sig = None
        op.signaler = False
        deps = set()
        for v in reads:
            if v is None:
                continue
            t = v.t
            for p in range(v.p0, v.p1):
                if t.lastw[p] is not None:
                    deps.add(t.lastw[p])
                if t.psum:
                    for r in t.readers[p]:
                        if r.eng != eng:
                            deps.add(r)
        for v in writes:
            t = v.t
            for p in range(v.p0, v.p1):
                if t.lastw[p] is not None:
                    deps.add(t.lastw[p])
                for r in t.readers[p]:
                    deps.add(r)
        for v in reads:
            if v is None:
                continue
            t = v.t
            for p in range(v.p0, v.p1):
                t.readers[p].append(op)
        for v in writes:
            t = v.t
            for p in range(v.p0, v.p1):
                t.lastw[p] = op
                t.readers[p] = []
        deps.discard(op)
        if dma:
            lst = self.dma_ops[eng]
            op.dpos = len(lst)
            if op.dpos >= NDMASEM:
                deps.add(lst[op.dpos - NDMASEM])
            lst.append(op)
        op.deps = deps
        op.pos = len(self.eng_ops[eng])
        self.eng_ops[eng].append(op)
        self.ops.append(op)
        return op

    def fence(self):
        lastc = [self.eng_ops[e][-1] for e in ALLENG if self.eng_ops[e] and not self.eng_ops[e][-1].dma]
        lastc = []
        for e in ALLENG:
            for op in reversed(self.eng_ops[e]):
                if not op.dma:
                    lastc.append(op)
                    break
        dmas = []
        for e in ALLENG:
            dmas += self.dma_ops[e][self.fence_dma.get(e, 0):]
            self.fence_dma[e] = len(self.dma_ops[e])
        for e in ALLENG:
            if not self.eng_ops[e]:
                continue
            op = self.add(e, None, [], [])
            op.deps |= set(lastc) | set(dmas)
            op.deps.discard(op)

    def mm(self, out, lhsT, rhs, start=True, stop=True, **kw):
        return self.add("pe", lambda e: e.matmul(out.ap, lhsT.ap, rhs.ap, start=start, stop=stop, **kw),
                        [lhsT, rhs] + ([] if start else [out]), [out])

    def tr(self, out, in_, ident):
        return self.add("pe", lambda e: e.transpose(out.ap, in_.ap, ident.ap), [in_, ident], [out])

    def act(self, out, in_, func, bias=None, scale=None, accum=None, eng="act"):
        def fn(e):
            kw = {}
            if bias is not None:
                kw["bias"] = bias.ap if isinstance(bias, View) else bias
            if scale is not None:
                kw["scale"] = scale.ap if isinstance(scale, View) else scale
            if accum is not None:
                kw["accum_out"] = accum.ap
            return e.activation(out.ap, in_.ap, func, **kw)
        rd = [in_] + [x for x in (bias, scale) if isinstance(x, View)]
        wr = [out] + ([accum] if accum is not None else [])
        return self.add(eng, fn, rd, wr)

    def tt(self, out, a, b, op, eng="dve"):
        return self.add(eng, lambda e: e.tensor_tensor(out.ap, a.ap, b.ap, op), [a, b], [out])

    def ts(self, out, a, s1, op0, s2=None, op1=None, accum=None, eng="dve"):
        def fn(e):
            kw = {}
            if accum is not None:
                kw["accum_out"] = accum.ap
            return e.tensor_scalar(out.ap, a.ap, s1.ap if isinstance(s1, View) else s1,
                                   (s2.ap if isinstance(s2, View) else s2), op0,
                                   op1 if op1 is not None else ALU.bypass, **kw)
        rd = [a] + [x for x in (s1, s2) if isinstance(x, View)]
        wr = [out] + ([accum] if accum is not None else [])
        return self.add(eng, fn, rd, wr)

    def stt(self, out, a, s, b, op0, op1, eng="dve"):
        return self.add(eng, lambda e: e.scalar_tensor_tensor(out.ap, a.ap, s.ap if isinstance(s, View) else s,
                                                              b.ap, op0, op1),
                        [a, b] + ([s] if isinstance(s, View) else []), [out])

    def copy(self, out, in_, eng="dve"):
        if eng == "act":
            return self.add(eng, lambda e: e.copy(out.ap, in_.ap), [in_], [out])
        return self.add(eng, lambda e: e.tensor_copy(out.ap, in_.ap), [in_], [out])

    def memset(self, out, val, eng="dve"):
        return self.add(eng, lambda e: e.memset(out.ap, val), [], [out])

    def red(self, out, in_, op=ALU.add, axis=AX.X, eng="dve"):
        return self.add(eng, lambda e: e.tensor_reduce(out.ap, in_.ap, axis, op), [in_], [out])

    def scan(self, out, d0, d1, init, op0=ALU.mult, op1=ALU.add):
        return self.add("dve", lambda e: e.tensor_tensor_scan(out.ap, d0.ap, d1.ap,
                                                              init.ap if isinstance(init, View) else init, op0, op1),
                        [d0, d1] + ([init] if isinstance(init, View) else []), [out])

    def recip(self, out, in_):
        return self.add("dve", lambda e: e.reciprocal(out.ap, in_.ap), [in_], [out])

    def dma(self, out, in_, eng="sp", **kw):
        return self.add(eng, lambda e: e.dma_start(out=out.ap, in_=in_.ap, **kw), [in_], [out], dma=True)

    def emit(self):
        nc = self.nc
        seen = {f: {e: -1 for e in COMPUTE + ("sp",)} for f in ALLENG}
        seen_dma = {f: set() for f in ALLENG}
        for op in self.ops:
            f = op.eng
            best = {}
            dwaits = []
            for d in op.deps:
                if d.dma:
                    if d.gid in seen_dma[f]:
                        continue
                    seen_dma[f].add(d.gid)
                    dwaits.append(d)
                    d.signaler = True
                else:
                    if d.eng == f and not SAME_ENGINE_SYNC:
                        continue
                    if seen[f][d.eng] >= d.pos:
                        continue
                    if d.eng not in best or best[d.eng].pos < d.pos:
                        best[d.eng] = d
            for e, d in best.items():
                seen[f][e] = d.pos
                d.signaler = True
            op.waits = list(best.values()) + dwaits
        nsig = {e: 0 for e in ALLENG}
        for e in ALLENG:
            for op in self.eng_ops[e]:
                if op.dma:
                    op.sig = ("dma", e, op.dpos % NDMASEM, 16 * (op.dpos // NDMASEM + 1))
                elif op.signaler:
                    n = nsig[e]
                    op.sig = ("cmp", e, n // EPOCH, n % EPOCH + 1)
                    nsig[e] = n + 1
        sems = {}
        stack = []

        def getsem(key):
            if key not in sems:
                cm = nc.semaphore("s_%s_%s_%d" % key)
                sems[key] = cm.__enter__()
                stack.append(cm)
            return sems[key]

        for op in self.ops:
            if op.sig is not None:
                getsem(op.sig[:3])
        engobj = {"pe": "tensor", "act": "scalar", "dve": "vector", "pool": "gpsimd", "sp": "sync"}
        self.final_dma = {e: list(self.dma_ops[e]) for e in ALLENG}

        with nc.Block() as block:
            for e in ALLENG:
                ops = self.eng_ops[e]
                if not ops:
                    continue

                def body(eng, ops=ops, e=e):
                    for op in ops:
                        for d in op.waits:
                            eng.wait_ge(sems[d.sig[:3]], d.sig[3])
                        if op.fn is None:
                            if op.sig is None:
                                continue
                            ins = eng.nop()
                        else:
                            ins = op.fn(eng)
                        if op.sig is not None:
                            ins.then_inc(sems[op.sig[:3]], 16 if op.dma else 1)
                    lst = self.dma_ops[e]
                    done = set()
                    for op in reversed(lst):
                        k = op.sig[:3]
                        if k in done:
                            continue
                        done.add(k)
                        eng.wait_ge(sems[k], op.sig[3])

                getattr(block, engobj[e])(body)
        for cm in reversed(stack):
            cm.__exit__(None, None, None)
        return nc

from concourse.bass_utils import run_bass_kernel_spmd

D = 1024
NMETA = 16
PAD = 112
OFF_A, OFF_B, OFF_C, OFF_D, OFF_G, IN_W = 0, 3328, 6416, 9488, 11536, 15632
FFH = 2816
C_NM, C_NF, C_MU, C_W0, C_A0, C_KK, C_KA, C_RK, C_V0 = 0, 8, 16, 42, 50, 58, 66, 74, 82
C_SCW, C_SCB, C_LCW, C_LCB, C_BG, C_LAM, C_LNW, C_LNB, NFM = 90, 154, 170, 202, 210, 226, 234, 242, 250
R_LNW, R_LNB, R_SSMN, R_DTB, R_ALOG, R_SD, NROW = 0, 1024, 2048, 3072, 3088, 3104, 3120


class K:
    pass


def build(nchunks, depth, nch_group, enable=("a", "b", "c", "d"), arena_kb=82):
    nc = bass.Bass("TRN2", target_bir_lowering=False)
    P = Prog(nc)
    TP = nchunks * 128
    L = depth
    GMAX = nch_group * 128
    k = K()
    xin = P.dram("xin", [TP, D], F32, kind="ExternalInput")
    out = P.dram("out", [TP - 128, D], F32, kind="ExternalOutput")
    w_in = P.dram("w_in", [L, D, IN_W], F32, kind="ExternalInput")
    w_qksw = P.dram("w_qksw", [L, D, 1024], F32, kind="ExternalInput")
    w_branch = P.dram("w_branch", [L, 4, D, D], F32, kind="ExternalInput")
    w_out = P.dram("w_out", [L, D, D], F32, kind="ExternalInput")
    w_ffn_in = P.dram("w_ffn_in", [L, D, 2 * FFH], F32, kind="ExternalInput")
    w_ffn_out = P.dram("w_ffn_out", [L, FFH, D], F32, kind="ExternalInput")
    fmpack = P.dram("fmpack", [128, L * NFM + 8], F32, kind="ExternalInput")
    rowpack = P.dram("rowpack", [L, NROW], F32, kind="ExternalInput")
    lru_wg = P.dram("lru_wg", [L, 2, 8, 128, 128], F32, kind="ExternalInput")
    consts = P.dram("consts", [128, 1280], F32, kind="ExternalInput")
    ropetab = P.dram("ropetab", [128, 2, TP], F32, kind="ExternalInput")
    rettab = P.dram("rettab", [128, 1548], F32, kind="ExternalInput")
    rw_w2 = P.dram("rw_w2", [L, 64, D], F32, kind="ExternalInput")
    rw_a2 = P.dram("rw_a2", [L, 64, D], F32, kind="ExternalInput")
    rw_g2 = P.dram("rw_g2", [L, 128, D], F32, kind="ExternalInput")
    rw_v1 = P.dram("rw_v1", [max(L - 1, 1), D, 32], F32, kind="ExternalInput")
    rw_v2 = P.dram("rw_v2", [max(L - 1, 1), 32, D], F32, kind="ExternalInput")

    cst = P.sb("cst", [128, 1280], F32)
    P.dma(cst[:], consts[:])
    ident = cst[:, 0:128]
    identb_t = P.sb("identb", [128, 128], BF16)
    P.copy(identb_t[:], ident)
    identb = identb_t[:]
    onesb_t = P.sb("onesb", [128, 128], BF16)
    P.memset(onesb_t[:], 1.0)
    onesb = onesb_t[:]
    fm = P.sb("fm", [128, L * NFM + 8], F32)
    P.dma(fm[:], fmpack[:])

    def fmc(l, off, n=1):
        return fm[:, l * NFM + off: l * NFM + off + n]

    psA = [P.ps(f"psA{i}", [128, 512]) for i in range(4)]
    psB = [P.ps(f"psB{i}", [128, 512]) for i in range(3)]
    psT = P.ps("psT", [128, 1024], BF16)
    cnt = {"A": 0, "B": 0, "ev": 0, "w": 0}

    def nextA():
        cnt["A"] += 1
        return psA[cnt["A"] % 4]

    def nextB():
        cnt["B"] += 1
        return psB[cnt["B"] % 3]

    def ev_eng():
        cnt["ev"] += 1
        return "act" if cnt["ev"] % 2 else "dve"

    s = P.sb("s", [128, 8, GMAX], F32, parts=8)
    hn = P.sb("hn", [128, 8, GMAX], BF16, parts=8)
    rstd = P.sb("rstd", [128, GMAX], F32)
    ys = [P.sb(f"ys{n}", [128, 8, GMAX], BF16, parts=8) for n in range(4)]
    mfm = P.sb("mfm", [128, 8, GMAX], BF16, parts=8)
    NW = 3
    wbuf = [P.sb(f"wbuf{i}", [128, 8, 512], BF16) for i in range(NW)]
    arena_t = P.sb("arena", [128, arena_kb * 256], F32)
    ar = {"off": 0}

    def ar_reset():
        ar["off"] = 0

    def ar_alloc(name, shape, dtype=F32, parts=1):
        n = 1
        for d_ in shape[1:]:
            n *= d_
        nf = n if dtype == F32 else (n + 1) // 2
        nf = (nf + 7) // 8 * 8
        off = ar["off"]
        assert off + nf <= arena_kb * 256, (name, off, nf)
        ar["off"] = off + nf
        ap = arena_t.h[:, off:off + nf]
        if dtype != F32:
            ap = ap.bitcast(dtype)[:, 0:n]
        else:
            ap = ap[:, 0:n]
        if len(shape) == 3:
            ap = ap.rearrange("p (a b) -> p a b", a=shape[1])
        return Tile(name, ap, parts)

    def rowload(dst, l, off, n):
        P.dma(dst, View(rowpack, rowpack.h[l:l + 1, off:off + n].partition_broadcast(128), 0, 1))
    tmpf = [P.sb(f"tmpf{i}", [128, GMAX], F32) for i in range(8)]
    tcnt = [0]

    def tmp():
        tcnt[0] += 1
        return tmpf[tcnt[0] % 8]

    def wload(dram_view):
        cnt["w"] += 1
        wt = wbuf[cnt["w"] % NW]
        kt, n = dram_view.ap.shape[1], dram_view.ap.shape[2]
        P.dma(wt[:, 0:kt, 0:n], dram_view, eng="pool")
        return wt

    def wview(t, l_idx, c0, ncols, r0=0, nrows=1024):
        ap = t.h[l_idx + (slice(r0, r0 + nrows), slice(c0, c0 + ncols))]
        ap = ap.rearrange("(kt p) n -> p kt n", p=128)
        return View(t, ap, 0, 1)

    ar_reset()
    sq = ar_alloc("sq", [128, 8, GMAX], F32, parts=8)
    xtm = [ar_alloc(f"xtm{i}", [128, D], F32) for i in range(2)]
    macc = [ar_alloc(f"macc{i}", [128, GMAX], F32) for i in range(4)]

    def rmsnorm(G, wcol, outt, eps=1e-6):
        for kt in range(8):
            P.act(sq[:, kt, 0:G], s[:, kt, 0:G], AF.Square)
        pp = nextA()
        for kt in range(8):
            P.mm(pp[:, 0:G], onesb_f[:], sq[:, kt, 0:G], start=(kt == 0), stop=(kt == 7))
        P.act(rstd[:, 0:G], pp[:, 0:G], AF.Ln, scale=1.0 / D, bias=eps_t[:, 0:1])
        P.act(rstd[:, 0:G], rstd[:, 0:G], AF.Exp, scale=-0.5)
        for kt in range(8):
            P.stt(outt[:, kt, 0:G], s[:, kt, 0:G], wcol[:, kt:kt + 1], rstd[:, 0:G], ALU.mult, ALU.mult)

    onesf_t = P.sb("onesf", [128, 128], F32)
    P.memset(onesf_t[:], 1.0)
    onesb_f = onesf_t
    eps_t = P.sb("eps", [128, 2], F32)
    P.memset(eps_t[:, 0:1], 1e-6)
    P.memset(eps_t[:, 1:2], 1.0)

    def proj_fm(src, KT, wt_fn, ncols, G, evac):
        nblk = (ncols + 511) // 512
        for b in range(nblk):
            nb = min(512, ncols - b * 512)
            wt = wt_fn(b * 512, nb)
            for jj in range(nb // 128):
                pp = nextA()
                for kt in range(KT):
                    P.mm(pp[:, 0:G], wt[:, kt, jj * 128:(jj + 1) * 128], src[:, kt, 0:G],
                         start=(kt == 0), stop=(kt == KT - 1))
                evac(b * 4 + jj, pp[:, 0:G])

    def proj_tm(src, KT, wt_fn, ncols, G, evac):
        nblk = (ncols + 511) // 512
        for b in range(nblk):
            nb = min(512, ncols - b * 512)
            wt = wt_fn(b * 512, nb)
            for c in range(G // 128):
                pp = nextA()
                for kt in range(KT):
                    P.mm(pp[:, 0:nb], src[:, kt, c * 128:(c + 1) * 128], wt[:, kt, 0:nb],
                         start=(kt == 0), stop=(kt == KT - 1))
                evac(c, b, pp[:, 0:nb], nb)

    k.__dict__.update(locals())

    lru_tail = [P.sb(f"lru_tail{l}", [128, 8, 3], F32) for l in range(L)]
    lru_h = [P.sb(f"lru_h{l}", [128, 8], F32) for l in range(L)]
    lru_nsp = [P.sb(f"lru_nsp{l}", [128, 8], F32) for l in range(L)]
    ar_reset()
    lru_wgs = ar_alloc("lru_wgs", [128, 16, 128], BF16)
    lru_xcb = ar_alloc("lru_xcb", [128, GMAX], BF16)
    pin = ar_alloc("lru_pin", [128, 16, GMAX + 4], F32, parts=16)
    for l in range(L):
        P.memset(lru_tail[l][:], 0.0)
        P.memset(lru_h[l][:], 0.0)
        P.act(lru_nsp[l][:], fmc(l, C_LAM, 8), AF.Exp, scale=-1.0)
        P.act(lru_nsp[l][:], lru_nsp[l][:], AF.Ln, bias=eps_t[:, 1:2])
        P.ts(lru_nsp[l][:], lru_nsp[l][:], -8.0, ALU.mult)

    def gelu_tanh(outv, xv, G):
        t1 = tmp()
        P.act(t1[:, 0:G], xv, AF.Square)
        P.ts(t1[:, 0:G], t1[:, 0:G], 0.044715, ALU.mult, 1.0, ALU.add)
        P.tt(t1[:, 0:G], t1[:, 0:G], xv, ALU.mult)
        P.act(t1[:, 0:G], t1[:, 0:G], AF.Sigmoid, scale=1.5957691216057308)
        P.tt(outv, t1[:, 0:G], xv, ALU.mult)

    def lru_mix(l, g, G):
        def wt_fn(c0, nb):
            return wload(wview(w_in, (l,), OFF_D + c0, nb))

        def evac(j, pp):
            if j < 8:
                P.copy(pin[:, j, 0:G], pp, eng=ev_eng())
            else:
                P.copy(pin[:, j, 3:3 + G], pp, eng=ev_eng())
        proj_fm(hn, 8, wt_fn, 2048, G, evac)
        P.dma(lru_wgs[:], View(lru_wg, lru_wg.h[l].rearrange("k n c e -> c (k n) e"), 0, 1), eng="pool")
        for n in range(8):
            xe = pin[:, 8 + n, :]
            P.copy(xe[:, 0:3], lru_tail[l][:, n, :])
            xc = tmp()
            lw = C_LCW + n * 4
            P.ts(xc[:, 0:G], xe[:, 0:G], fmc(l, lw), ALU.mult, fmc(l, C_LCB + n), ALU.add)
            for j in range(1, 4):
                P.stt(xc[:, 0:G], xe[:, j:j + G], fmc(l, lw + j), xc[:, 0:G], ALU.mult, ALU.add)
            P.copy(lru_tail[l][:, n, :], xe[:, G:G + 3])
            xcb16 = lru_xcb
            P.copy(xcb16[:, 0:G], xc[:, 0:G])
            gates = []
            for kk_ in range(2):
                pp = nextB()
                P.mm(pp[:, 0:G], lru_wgs[:, kk_ * 8 + n, :], xcb16[:, 0:G])
                gt = tmp()
                P.act(gt[:, 0:G], pp[:, 0:G], AF.Sigmoid, bias=fmc(l, C_BG + kk_ * 8 + n))
                gates.append(gt)
            rg, ig = gates
            la = tmp()
            P.ts(la[:, 0:G], rg[:, 0:G], lru_nsp[l][:, n:n + 1], ALU.mult)
            av = tmp()
            P.act(av[:, 0:G], la[:, 0:G], AF.Exp)
            P.act(la[:, 0:G], la[:, 0:G], AF.Exp, scale=2.0)
            P.ts(la[:, 0:G], la[:, 0:G], -1.0, ALU.mult, 1.0, ALU.add)
            P.act(la[:, 0:G], la[:, 0:G], AF.Sqrt)
            P.tt(ig[:, 0:G], ig[:, 0:G], xc[:, 0:G], ALU.mult)
            P.tt(ig[:, 0:G], ig[:, 0:G], la[:, 0:G], ALU.mult)
            if g == 0:
                P.memset(ig[:, 0:PAD], 0.0)
            P.scan(rg[:, 0:G], av[:, 0:G], ig[:, 0:G], lru_h[l][:, n:n + 1])
            P.copy(lru_h[l][:, n:n + 1], rg[:, G - 1:G])
            gelu_tanh(la[:, 0:G], pin[:, n, 0:G], G)
            P.tt(ys[3][:, n, 0:G], rg[:, 0:G], la[:, 0:G], ALU.mult)


    def zero_y(n, G):
        for kt in range(8):
            P.memset(ys[n][:, kt, 0:G], 0.0, eng="pool")

    mixers = {"a": None, "b": None, "c": None, "d": lru_mix}
    k.__dict__.update(locals())
    for nm, modfn in MIXER_BUILDERS.items():
        ar_reset()
        mixers[nm] = modfn(k)

    def layer(l, g, G):
        rmsnorm(G, fmc(l, C_NM, 8), hn)
        for n, nm in enumerate("abcd"):
            if nm in enable and mixers[nm] is not None:
                P.fence()
                mixers[nm](l, g, G)
            else:
                zero_y(n, G)
        P.fence()
        for db in range(2):
            for n in range(4):
                wb = wload(wview(w_branch, (l, n), db * 512, 512))
                wg = wload(wview(w_in, (l,), OFF_G + n * 1024 + db * 512, 512))
                for dj in range(4):
                    pz = nextA()
                    for kt in range(8):
                        P.mm(pz[:, 0:G], wb[:, kt, dj * 128:(dj + 1) * 128], ys[n][:, kt, 0:G],
                             start=(kt == 0), stop=(kt == 7))
                    pg = nextA()
                    for kt in range(8):
                        P.mm(pg[:, 0:G], wg[:, kt, dj * 128:(dj + 1) * 128], hn[:, kt, 0:G],
                             start=(kt == 0), stop=(kt == 7))
                    sg = tmp()
                    P.act(sg[:, 0:G], pg[:, 0:G], AF.Sigmoid)
                    if n == 0:
                        P.tt(macc[dj][:, 0:G], sg[:, 0:G], pz[:, 0:G], ALU.mult)
                    else:
                        P.tt(sg[:, 0:G], sg[:, 0:G], pz[:, 0:G], ALU.mult)
                        P.tt(macc[dj][:, 0:G], macc[dj][:, 0:G], sg[:, 0:G], ALU.add, eng="pool")
            for dj in range(4):
                P.copy(mfm[:, db * 4 + dj, 0:G], macc[dj][:, 0:G])

        def evac_out(j, pp):
            P.tt(s[:, j, 0:G], s[:, j, 0:G], pp, ALU.add)
        proj_fm(mfm, 8, lambda c0, nb: wload(wview(w_out, (l,), c0, nb)), 1024, G, evac_out)
        rmsnorm(G, fmc(l, C_NF, 8), hn)
        for hb in range(6):
            nb = min(512, FFH - hb * 512)
            wg_ = wload(wview(w_ffn_in, (l,), hb * 512, nb))
            wu_ = wload(wview(w_ffn_in, (l,), FFH + hb * 512, nb))
            for jj in range(nb // 128):
                pg = nextA()
                pu = nextA()
                for kt in range(8):
                    P.mm(pg[:, 0:G], wg_[:, kt, jj * 128:(jj + 1) * 128], hn[:, kt, 0:G], start=(kt == 0), stop=(kt == 7))
                for kt in range(8):
                    P.mm(pu[:, 0:G], wu_[:, kt, jj * 128:(jj + 1) * 128], hn[:, kt, 0:G], start=(kt == 0), stop=(kt == 7))
                sg = tmp()
                P.act(sg[:, 0:G], pg[:, 0:G], AF.Silu)
                ht_ = hb * 4 + jj
                P.tt(ys[ht_ // 8][:, ht_ % 8, 0:G], sg[:, 0:G], pu[:, 0:G], ALU.mult)
        for db in range(2):
            accs = [psA[i] for i in range(4)]
            for hb3, (h0, nh) in enumerate(((0, 8), (8, 8), (16, 6))):
                wt = wload(wview(w_ffn_out, (l,), db * 512, 512, r0=h0 * 128, nrows=nh * 128))
                for dj in range(4):
                    for hh in range(nh):
                        ht = h0 + hh
                        P.mm(accs[dj][:, 0:G], wt[:, hh, dj * 128:(dj + 1) * 128], ys[ht // 8][:, ht % 8, 0:G],
                             start=(ht == 0), stop=(ht == 21))
            for dj in range(4):
                P.tt(s[:, db * 4 + dj, 0:G], s[:, db * 4 + dj, 0:G], accs[dj][:, 0:G], ALU.add)
        if g == 0:
            for kt in range(8):
                P.memset(s[:, kt, 0:PAD], 0.0, eng="pool")

    ngroups = (nchunks + nch_group - 1) // nch_group
    otm = xtm
    fin = sq
    for g in range(ngroups):
        c0 = g * nch_group
        c1 = min(nchunks, c0 + nch_group)
        G = (c1 - c0) * 128
        P.fence()
        for c in range(c0, c1):
            xt = xtm[c % 2]
            P.dma(xt[:], xin[c * 128:(c + 1) * 128, :])
            for kt in range(8):
                pp = nextB()
                P.tr(pp[:, 0:128], xt[:, kt * 128:(kt + 1) * 128], ident)
                P.copy(s[:, kt, (c - c0) * 128:(c - c0 + 1) * 128], pp[:, 0:128], eng=ev_eng())
        for l in range(L):
            layer(l, g, G)
        P.fence()
        rmsnorm(G, fm[:, L * NFM:L * NFM + 8], fin)
        for c in range(c0, c1):
            if c == 0:
                continue
            ot = otm[c % 2]
            for kt in range(8):
                pp = nextB()
                P.tr(pp[:, 0:128], fin[:, kt, (c - c0) * 128:(c - c0 + 1) * 128], ident)
                P.copy(ot[:, kt * 128:(kt + 1) * 128], pp[:, 0:128], eng=ev_eng())
            P.dma(out[(c - 1) * 128:c * 128, :], ot[:])
    if BUILD_ONLY:
        return nc
    P.emit()
    return nc


BUILD_ONLY = False
MIXER_BUILDERS = {}


def _fmcol(v):
    v = np.asarray(v, np.float32).reshape(-1, 128)
    return np.ascontiguousarray(v.T)


def prep_shared(inp, L, TP):
    f32 = np.float32
    sh = {}
    sh["w_in"] = np.ascontiguousarray(inp["w_in"][:L], f32)
    qk = inp["w_in"][:L, :, OFF_C:OFF_C + 1024].reshape(L, D, 16, 2, 32)
    sh["w_qksw"] = np.ascontiguousarray(qk[:, :, :, ::-1, :].reshape(L, D, 1024), f32)
    sh["w_branch"] = np.ascontiguousarray(inp["w_branch"][:L], f32)
    sh["w_out"] = np.ascontiguousarray(inp["w_out"][:L], f32)
    sh["w_ffn_in"] = np.ascontiguousarray(inp["w_ffn_in"][:L], f32)
    sh["w_ffn_out"] = np.ascontiguousarray(inp["w_ffn_out"][:L], f32)
    cols = []
    for l in range(L):
        vec = inp["rwkv_vec"][l]
        v0 = inp["rwkv_v0"][l - 1] if l > 0 else np.zeros(D, f32)
        parts = [_fmcol(inp["norm_mix"][l]), _fmcol(inp["norm_ffn"][l]), _fmcol(inp["rwkv_mu"][l]),
                 _fmcol(vec[0]), _fmcol(vec[1]), _fmcol(vec[2]), _fmcol(vec[3]),
                 _fmcol(inp["rwkv_rk"][l].reshape(-1)), _fmcol(v0)]
        scw = inp["ssm_conv_w"][l]
        parts.append(np.ascontiguousarray(scw.reshape(4, 16, 128).transpose(2, 1, 0)).reshape(128, 64))
        parts.append(_fmcol(inp["ssm_conv_b"][l]))
        lcw = inp["lru_conv_w"][l]
        parts.append(np.ascontiguousarray(lcw.reshape(4, 8, 128).transpose(2, 1, 0)).reshape(128, 32))
        parts.append(_fmcol(inp["lru_conv_b"][l]))
        parts.append(_fmcol(inp["lru_b_gate"][l].reshape(-1)))
        parts.append(_fmcol(inp["lru_lambda"][l]))
        parts.append(_fmcol(vec[4]))
        parts.append(_fmcol(vec[5]))
        cols.append(np.concatenate(parts, axis=1))
    cols.append(_fmcol(inp["final_norm"]))
    sh["fmpack"] = np.ascontiguousarray(np.concatenate(cols, axis=1), f32)
    assert sh["fmpack"].shape[1] == L * NFM + 8
    rows = []
    for l in range(L):
        vec = inp["rwkv_vec"][l]
        rows.append(np.concatenate([vec[4], vec[5], inp["ssm_norm"][l], inp["ssm_dt_bias"][l],
                                    inp["ssm_a_log"][l], inp["ssm_d"][l]]))
    sh["rowpack"] = np.ascontiguousarray(np.stack(rows), f32)
    sh["lru_wg"] = np.ascontiguousarray(inp["lru_w_gate"][:L], f32)
    i = np.arange(128)
    ident = np.eye(128, dtype=f32)
    triu = (i[:, None] <= i[None, :]).astype(f32)
    mstr = (i[:, None] > i[None, :]).astype(f32)
    spare = np.zeros((128, 128), f32)
    spare[:, 0] = (i >= PAD)
    sut = (i[:, None] < i[None, :]).astype(f32)
    blk = ((i[:, None] // 64) == (i[None, :] // 64)).astype(f32)
    sh["consts"] = np.ascontiguousarray(np.concatenate([ident, triu, mstr, spare, sut, blk, sut, triu, sut, triu], axis=1))
    pos = (np.arange(TP) - PAD).astype(f32)
    half = 32
    freq = np.power(np.float32(10000.0), -np.arange(half, dtype=f32) / half).astype(f32)
    ang = pos[None, :] * freq[:, None]
    cosf = np.cos(ang).astype(f32)
    sinf = np.sin(ang).astype(f32)
    ch = np.arange(128) % 64
    cos_t = cosf[ch % 32]
    sin_t = np.where((ch < 32)[:, None], -sinf[ch % 32], sinf[ch % 32])
    sh["ropetab"] = np.ascontiguousarray(np.stack([cos_t, sin_t], axis=1), f32)
    H = 8
    log_g = np.log1p(-np.exp2(-5.0 - np.arange(H, dtype=f32))).astype(f32)
    idx = np.arange(128, dtype=f32)
    rel = idx[None, :] - idx[:, None]
    intraT = np.where(rel >= 0, np.exp(np.where(rel >= 0, rel, 0.0)[None] * log_g[:, None, None]), 0.0)
    qdec = np.exp((idx + 1.0)[None, :] * log_g[:, None])
    kdec = np.exp((127.0 - idx)[None, :] * log_g[:, None])
    cdec = np.exp(128.0 * log_g)
    rt = np.zeros((128, 1548), f32)
    rt[:, 0:1024] = intraT.transpose(1, 0, 2).reshape(128, 1024)
    pp_ = np.arange(128) // 64
    for j in range(4):
        rt[:, 1024 + j * 128:1024 + (j + 1) * 128] = qdec[2 * j + pp_, :]
        rt[:, 1544 + j] = cdec[2 * j + pp_]
    rt[:, 1536:1544] = kdec.T
    sh["rettab"] = rt
    sh["rw_w2"] = np.ascontiguousarray(inp["rwkv_w2"][:L], f32)
    sh["rw_a2"] = np.ascontiguousarray(inp["rwkv_a2"][:L], f32)
    sh["rw_g2"] = np.ascontiguousarray(inp["rwkv_g2"][:L], f32)
    n1 = max(L - 1, 1)
    sh["rw_v1"] = np.ascontiguousarray(inp["rwkv_v1"][:n1], f32)
    sh["rw_v2"] = np.ascontiguousarray(inp["rwkv_v2"][:n1], f32)
    return sh


def run_module(inp, depth, nch_group, enable=("a", "b", "c", "d"), ncores=None, arena_kb=82):
    x = np.asarray(inp["x"], np.float32)
    B, S, _ = x.shape
    T = S + NMETA
    assert (T + PAD) % 128 == 0
    TP = T + PAD
    nchunks = TP // 128
    nc = build(nchunks, depth, nch_group, enable, arena_kb)
    sh = prep_shared(inp, depth, TP)
    meta = np.asarray(inp["meta"], np.float32)
    ncores = ncores or B
    in_maps = []
    for c in range(ncores):
        b = c % B
        xin = np.concatenate([np.zeros((PAD, D), np.float32), meta, x[b]], axis=0)
        m = dict(sh)
        m["xin"] = np.ascontiguousarray(xin)
        in_maps.append(m)
    res = run_bass_kernel_spmd(nc, in_maps, core_ids=list(range(ncores)))
    outs = [res.results[b]["out"] for b in range(B)]
    return np.stack(outs, axis=0)


def kernel(**inputs):
    return run_module(inputs, 4, NCH_GROUP, ncores=4, arena_kb=82)


NCH_GROUP = 2

RT_INTRA, RT_Q, RT_KDEC, RT_C, RT_N = 0, 1024, 1536, 1544, 1548


def build_ret(k):
    P, L, GMAX, NCHG = k.P, k.L, k.GMAX, k.nch_group
    hn, ys, w_in, w_qksw = k.hn, k.ys, k.w_in, k.w_qksw
    A = k.ar_alloc
    rt = A("rt", [128, RT_N], F32)
    rope_c = P.sb("rope_c", [128, GMAX], F32)
    rope_s = P.sb("rope_s", [128, GMAX], F32)
    pin = A("ret_pin", [128, 16, GMAX], F32, parts=16)
    qrT = A("qrT", [128, 8, GMAX], BF16, parts=8)
    qdT = A("qdT", [128, 4, GMAX], BF16, parts=4)
    vtm = A("ret_vtm", [128, NCHG, 1024], BF16, parts=NCHG)
    gtm = A("ret_gtm", [128, NCHG, 1024], F32, parts=NCHG)
    ktm = A("ret_ktm", [128, NCHG, 512], BF16, parts=NCHG)
    scm = [A(f"ret_scm{i}", [128, 8, 128], BF16) for i in range(2)]
    ysb = [A(f"ret_ysb{i}", [128, 8, 128], F32) for i in range(2)]
    ysq = A("ret_ysq", [128, 8, 128], F32)
    ssq = P.sb("ret_ssq", [128, 8], F32)
    Rst = [P.sb(f"ret_R{l}", [128, 4, 128], F32) for l in range(L)]
    Rb = P.sb("ret_Rb", [128, 4, 128], BF16)
    for l in range(L):
        P.memset(Rst[l][:], 0.0)

    def mix(l, g, G):
        c0 = g * NCHG
        nchk = G // 128
        P.dma(rt[:], k.rettab[:])
        if l == 0:
            P.dma(rope_c[:, 0:G], k.ropetab[:, 0, c0 * 128:c0 * 128 + G])
            P.dma(rope_s[:, 0:G], k.ropetab[:, 1, c0 * 128:c0 * 128 + G])

        def ev1(j, pp):
            P.copy(pin[:, j, 0:G], pp, eng=k.ev_eng())

        def ev2(j, pp):
            P.copy(pin[:, 8 + j, 0:G], pp, eng=k.ev_eng())
        k.proj_fm(hn, 8, lambda c, nb: k.wload(k.wview(w_in, (l,), OFF_C + c, nb)), 1024, G, ev1)
        k.proj_fm(hn, 8, lambda c, nb: k.wload(k.wview(w_qksw, (l,), c, nb)), 1024, G, ev2)
        for j in range(8):
            cc, ss = rope_c, rope_s
            t1 = k.tmp()
            t2 = k.tmp()
            P.tt(t1[:, 0:G], pin[:, j, 0:G], cc[:, 0:G], ALU.mult)
            P.tt(t2[:, 0:G], pin[:, 8 + j, 0:G], ss[:, 0:G], ALU.mult, eng="pool")
            if j < 4:
                P.tt(t1[:, 0:G], t1[:, 0:G], t2[:, 0:G], ALU.add)
                P.copy(qrT[:, j, 0:G], t1[:, 0:G], eng="act")
                for c in range(nchk):
                    P.tt(qdT[:, j, c * 128:(c + 1) * 128], t1[:, c * 128:(c + 1) * 128],
                         rt[:, RT_Q + j * 128:RT_Q + (j + 1) * 128], ALU.mult)
            else:
                P.tt(t1[:, 0:G], t1[:, 0:G], t2[:, 0:G], ALU.add)
                P.act(qrT[:, j, 0:G], t1[:, 0:G], AF.Copy, scale=0.125)

        def ev3(c, b, pp, nb):
            if b < 2:
                P.copy(vtm[:, c, b * 512:(b + 1) * 512], pp, eng=k.ev_eng())
            else:
                P.act(gtm[:, c, (b - 2) * 512:(b - 1) * 512], pp, AF.Silu)
        k.proj_tm(hn, 8, lambda c, nb: k.wload(k.wview(w_in, (l,), OFF_C + 1024 + c, nb)), 2048, G, ev3)

        for c in range(nchk):
            cs = slice(c * 128, (c + 1) * 128)
            for j in range(4):
                pt = k.psT
                P.tr(pt[:, j * 128:(j + 1) * 128], qrT[:, 4 + j, cs], k.identb)
            for h in range(8):
                P.ts(ktm[:, c, h * 64:(h + 1) * 64], k.psT[:, h * 64:(h + 1) * 64],
                     rt[:, RT_KDEC + h:RT_KDEC + h + 1], ALU.mult, eng="dve")
            sc = scm[c % 2]
            for hb in range(2):
                pp = k.nextB()
                for hh in range(4):
                    h = hb * 4 + hh
                    j, pb = h // 2, 64 * (h % 2)
                    P.mm(pp[:, hh * 128:(hh + 1) * 128], qrT[pb:pb + 64, 4 + j, cs], qrT[pb:pb + 64, j, cs])
                P.tt(sc[:, hb * 4:(hb + 1) * 4, :], pp[:, :].re(lambda a: a.rearrange("p (h l) -> p h l", h=4)),
                     rt[:, RT_INTRA + hb * 512:RT_INTRA + (hb + 1) * 512].re(
                         lambda a: a.rearrange("p (h l) -> p h l", h=4)), ALU.mult)
            P.copy(Rb[:], Rst[l][:], eng="act")
            yb = ysb[c % 2]
            for hb in range(2):
                pp = k.nextB()
                for hh in range(4):
                    h = hb * 4 + hh
                    j, pb = h // 2, 64 * (h % 2)
                    P.mm(pp[:, hh * 128:(hh + 1) * 128], sc[:, h, :], vtm[:, c, h * 128:(h + 1) * 128],
                         start=True, stop=False)
                    P.mm(pp[:, hh * 128:(hh + 1) * 128], qdT[pb:pb + 64, j, cs], Rb[pb:pb + 64, j, :],
                         start=False, stop=True)
                P.copy(yb[:, hb * 4:(hb + 1) * 4, :], pp[:, :].re(lambda a: a.rearrange("p (h l) -> p h l", h=4)),
                       eng="act")
            for j in range(4):
                pp = k.nextB()
                for hp in range(2):
                    h = 2 * j + hp
                    P.mm(pp[:, hp * 128:(hp + 1) * 128], ktm[:, c, j * 128:(j + 1) * 128],
                         vtm[:, c, h * 128:(h + 1) * 128])
                for hp in range(2):
                    pb = 64 * hp
                    P.stt(Rst[l][pb:pb + 64, j, :], Rst[l][pb:pb + 64, j, :], rt[pb:pb + 64, RT_C + j:RT_C + j + 1],
                          pp[pb:pb + 64, hp * 128:(hp + 1) * 128], ALU.mult, ALU.add)
            P.tt(ysq[:], yb[:], yb[:], ALU.mult, eng="pool")
            P.red(ssq[:], ysq[:])
            P.ts(ssq[:], ssq[:], 1.0 / 128, ALU.mult, 1e-6, ALU.add)
            P.act(ssq[:], ssq[:], AF.Ln)
            P.act(ssq[:], ssq[:], AF.Exp, scale=-0.5)
            P.tt(yb[:], yb[:], ssq[:].re(lambda a: a.unsqueeze(2).broadcast_to([128, 8, 128])), ALU.mult)
            P.tt(yb[:], yb[:], gtm[:, c, :].re(lambda a: a.rearrange("p (h e) -> p h e", h=8)), ALU.mult)
            for kt in range(8):
                pp = k.nextB()
                P.tr(pp[:, 0:128], yb[:, kt, :], k.ident)
                P.copy(ys[2][:, kt, cs], pp[:, 0:128], eng=k.ev_eng())
    return mix


MIXER_BUILDERS["c"] = build_ret

def build_ssd(k):
    P, L, GMAX, NCHG = k.P, k.L, k.GMAX, k.nch_group
    hn, ys, w_in = k.hn, k.ys, k.w_in
    A = k.ar_alloc
    triU = k.cst[:, 128:256]
    mstr = k.cst[:, 256:384]
    xe = A("ssd_xe", [128, 16, GMAX + 4], F32, parts=16)
    ztm = A("ssd_ztm", [128, NCHG, 1024], F32, parts=NCHG)
    dtm = A("ssd_dtm", [128, NCHG, 16], F32, parts=NCHG)
    rowb = A("ssd_rowb", [128, 1072], F32)
    xsf = A("ssd_xsf", [128, 8, GMAX], F32, parts=8)
    bfm = A("ssd_bfm", [128, 4, GMAX], BF16, parts=4)
    cfm = A("ssd_cfm", [128, 4, GMAX], BF16, parts=4)
    btm = A("ssd_btm", [128, 4, 128], BF16)
    xs_tm = A("ssd_xstm", [128, 16, 64], F32)
    xdt = A("ssd_xdt", [128, 16, 64], BF16)
    xdt2 = A("ssd_xdt2", [128, 16, 64], BF16)
    rhsL = A("ssd_rhsL", [128, 16, 128], F32)
    Lm = A("ssd_Lm", [128, 16, 128], F32)
    Wm = A("ssd_Wm", [128, 16, 128], BF16)
    cbm = A("ssd_cbm", [128, 4, 128], F32)
    yt = A("ssd_yt", [128, 16, 64], F32)
    yt2 = A("ssd_yt2", [128, 16, 64], F32)
    sm = P.sb("ssd_small", [128, 8, 16], F32)
    ssq = P.sb("ssd_ssq", [128, 4], F32)
    St = [P.sb(f"ssd_S{l}", [128, 16, 64], F32) for l in range(L)]
    Sb = P.sb("ssd_Sb", [128, 16, 64], BF16)
    tails = [P.sb(f"ssd_tail{l}", [128, 16, 3], F32) for l in range(L)]
    for l in range(L):
        P.memset(St[l][:], 0.0)
        P.memset(tails[l][:], 0.0)

    def bc(v, n):
        return v.re(lambda a: a.unsqueeze(2).broadcast_to([128, a.shape[1], n]))

    def mix(l, g, G):
        nchk = G // 128
        k.rowload(rowb[:], l, R_SSMN, 1072)
        nw_b = rowb[:, 0:1024]
        dtb_b = rowb[:, 1024:1040]
        alog_b = rowb[:, 1040:1056]
        dsk_b = rowb[:, 1056:1072]
        P.act(sm[:, 6, :], alog_b, AF.Exp)
        P.ts(sm[:, 6, :], sm[:, 6, :], -1.0, ALU.mult)

        def ev_z(c, b, pp, nb):
            P.act(ztm[:, c, b * 512:(b + 1) * 512], pp, AF.Silu)
        k.proj_tm(hn, 8, lambda c, nb: k.wload(k.wview(w_in, (l,), OFF_B + c, nb)), 1024, G, ev_z)

        def ev_x(j, pp):
            P.copy(xe[:, j, 3:3 + G], pp, eng=k.ev_eng())
        k.proj_fm(hn, 8, lambda c, nb: k.wload(k.wview(w_in, (l,), OFF_B + 1024 + c, nb)), 2048, G, ev_x)

        def ev_dt(c, b, pp, nb):
            P.tt(dtm[:, c, :], pp, dtb_b, ALU.add)
        k.proj_tm(hn, 8, lambda c, nb: k.wload(k.wview(w_in, (l,), OFF_B + 3072 + c, nb)), 16, G, ev_dt)
        for j in range(16):
            P.copy(xe[:, j, 0:3], tails[l][:, j, :])
            xc = k.tmp()
            cw = C_SCW + j * 4
            P.ts(xc[:, 0:G], xe[:, j, 0:G], k.fmc(l, cw), ALU.mult, k.fmc(l, C_SCB + j), ALU.add)
            for t in range(1, 4):
                P.stt(xc[:, 0:G], xe[:, j, t:t + G], k.fmc(l, cw + t), xc[:, 0:G], ALU.mult, ALU.add)
            P.copy(tails[l][:, j, :], xe[:, j, G:G + 3])
            if j < 8:
                P.act(xsf[:, j, 0:G], xc[:, 0:G], AF.Silu)
            elif j < 12:
                P.act(bfm[:, j - 8, 0:G], xc[:, 0:G], AF.Silu)
            else:
                P.act(cfm[:, j - 12, 0:G], xc[:, 0:G], AF.Silu)

        for c in range(nchk):
            cs = slice(c * 128, (c + 1) * 128)
            dt = sm[:, 0, :]
            dA = sm[:, 1, :]
            ea = sm[:, 2, :]
            dec = sm[:, 3, :]
            etot = sm[:, 4, :]
            dtdec = sm[:, 5, :]
            P.act(dt, dtm[:, c, :], AF.Exp)
            P.act(dt, dt, AF.Ln, bias=k.eps_t[:, 1:2])
            if g == 0 and c == 0:
                P.ts(dt, dt, k.cst[:, 384:385], ALU.mult)
            P.tt(dA, dt, sm[:, 6, :], ALU.mult)
            pc = k.nextB()
            P.mm(pc[:, 0:16], triU, dA)
            P.mm(pc[:, 16:32], k.onesf_t[:], dA)
            P.act(ea, pc[:, 0:16], AF.Exp)
            P.copy(sm[:, 7, :], pc[:, 0:16])
            P.tt(dec, pc[:, 16:32], sm[:, 7, :], ALU.subtract)
            P.act(dec, dec, AF.Exp)
            P.act(etot, pc[:, 16:32], AF.Exp)
            P.tt(dtdec, dt, dec, ALU.mult)
            for hb in range(2):
                pp = k.nextA()
                for jj in range(4):
                    P.tr(pp[:, jj * 128:(jj + 1) * 128], xsf[:, hb * 4 + jj, cs], k.ident)
                P.copy(xs_tm[:, hb * 8:(hb + 1) * 8, :], pp[:, :].re(lambda a: a.rearrange("p (h e) -> p h e", h=8)),
                       eng="act")
            P.tt(xdt[:], xs_tm[:], bc(dt, 64), ALU.mult)
            P.tt(xdt2[:], xs_tm[:], bc(dtdec, 64), ALU.mult, eng="pool")
            for gi in range(4):
                P.tr(k.psT[:, gi * 128:(gi + 1) * 128], bfm[:, gi, cs], k.identb)
            P.copy(btm[:], k.psT[:, 0:512].re(lambda a: a.rearrange("p (g n) -> p g n", g=4)), eng="act")
            P.tt(rhsL[:], triU.re(lambda a: a.unsqueeze(1).broadcast_to([128, 16, 128])), bc(dA, 128), ALU.mult)
            pcb = k.nextB()
            for gi in range(4):
                P.mm(pcb[:, gi * 128:(gi + 1) * 128], bfm[:, gi, cs], cfm[:, gi, cs])
            P.tt(cbm[:], pcb[:, :].re(lambda a: a.rearrange("p (g l) -> p g l", g=4)),
                 triU.re(lambda a: a.unsqueeze(1).broadcast_to([128, 4, 128])), ALU.mult)
            for gi in range(4):
                pl = k.nextA()
                P.mm(pl[:, :], mstr, rhsL[:, gi * 4:(gi + 1) * 4, :].re(lambda a: a.rearrange("p h l -> p (h l)")))
                P.act(Lm[:, gi * 4:(gi + 1) * 4, :], pl[:, :].re(lambda a: a.rearrange("p (h l) -> p h l", h=4)), AF.Exp)
                P.tt(Wm[:, gi * 4:(gi + 1) * 4, :], Lm[:, gi * 4:(gi + 1) * 4, :],
                     cbm[:, gi, :].re(lambda a: a.unsqueeze(1).broadcast_to([128, 4, 128])), ALU.mult)
            P.copy(Sb[:], St[l][:], eng="act")
            for hb in range(2):
                py = k.nextA()
                for hh in range(8):
                    h = hb * 8 + hh
                    P.mm(py[:, hh * 64:(hh + 1) * 64], Wm[:, h, :], xdt[:, h, :])
                po = k.nextB()
                for gg in range(2):
                    gi = hb * 2 + gg
                    P.mm(po[:, gg * 256:(gg + 1) * 256], cfm[:, gi, cs],
                         Sb[:, gi * 4:(gi + 1) * 4, :].re(lambda a: a.rearrange("p h e -> p (h e)")))
                hs = slice(hb * 8, (hb + 1) * 8)
                P.tt(yt[:, hs, :], po[:, :].re(lambda a: a.rearrange("p (h e) -> p h e", h=8)),
                     bc(sm[:, 2, hb * 8:(hb + 1) * 8], 64), ALU.mult)
                P.tt(yt[:, hs, :], yt[:, hs, :], py[:, :].re(lambda a: a.rearrange("p (h e) -> p h e", h=8)), ALU.add)
            P.tt(yt2[:], xs_tm[:], bc(dsk_b, 64), ALU.mult, eng="pool")
            P.tt(yt[:], yt[:], yt2[:], ALU.add)
            P.tt(yt[:], yt[:], ztm[:, c, :].re(lambda a: a.rearrange("p (h e) -> p h e", h=16)), ALU.mult)
            for hb in range(2):
                pst = k.nextB()
                for gg in range(2):
                    gi = hb * 2 + gg
                    P.mm(pst[:, gg * 256:(gg + 1) * 256], btm[:, gi, :],
                         xdt2[:, gi * 4:(gi + 1) * 4, :].re(lambda a: a.rearrange("p h e -> p (h e)")))
                hs = slice(hb * 8, (hb + 1) * 8)
                P.tt(St[l][:, hs, :], St[l][:, hs, :], bc(sm[:, 4, hb * 8:(hb + 1) * 8], 64), ALU.mult)
                P.tt(St[l][:, hs, :], St[l][:, hs, :],
                     pst[:, :].re(lambda a: a.rearrange("p (h e) -> p h e", h=8)), ALU.add)
            P.tt(yt2[:], yt[:], yt[:], ALU.mult, eng="pool")
            P.red(ssq[:], yt2[:].re(lambda a: a.rearrange("p (g e) n -> p g (e n)", g=4)))
            P.ts(ssq[:], ssq[:], 1.0 / 256, ALU.mult, 1e-5, ALU.add)
            P.act(ssq[:], ssq[:], AF.Ln)
            P.act(ssq[:], ssq[:], AF.Exp, scale=-0.5)
            ytg = yt[:].re(lambda a: a.rearrange("p (g e) n -> p g (e n)", g=4))
            P.tt(ytg, ytg, bc(ssq[:], 256), ALU.mult)
            ytf = yt[:].re(lambda a: a.rearrange("p h e -> p (h e)"))
            P.tt(ytf, ytf, nw_b, ALU.mult)
            for kt in range(8):
                pp = k.nextA()
                P.tr(pp[:, 0:128], ytf[:, kt * 128:(kt + 1) * 128], k.ident)
                P.copy(ys[1][:, kt, cs], pp[:, 0:128], eng=k.ev_eng())
    return mix


MIXER_BUILDERS["b"] = build_ssd

def build_rwkv(k):
    P, L, GMAX, NCHG = k.P, k.L, k.GMAX, k.nch_group
    hn, ys, w_in = k.hn, k.ys, k.w_in
    A = k.ar_alloc
    cst = k.cst
    ident, identb = k.ident, k.identb
    mstr = cst[:, 256:384]
    blk = cst[:, 640:768]
    mask4 = cst[:, 768:1280]
    pa = A("rw_pa", [128, 26, GMAX + 1], F32, parts=26)
    w2s = A("rw_w2s", [128, 1024], BF16)
    a2s = A("rw_a2s", [128, 1024], BF16)
    g2s = A("rw_g2s", [128, 1024], BF16)
    v1s = A("rw_v1s", [128, 8, 32], BF16)
    v2s = A("rw_v2s", [128, 1024], BF16)
    twb = A("rw_twb", [128, 128], BF16)
    adb = A("rw_adb", [128, 128], BF16)
    sgb = A("rw_sgb", [128, 128], BF16)
    bt3 = [A(f"rw_bt3{i}", [128, 3, 128], BF16) for i in range(2)]
    gfm = A("rw_gfm", [128, 8, 128], BF16, parts=8)
    bonus = A("rw_bonus", [128, 8, 128], BF16, parts=8)
    off_alias = k.ar["off"]
    vb16 = A("rw_vb16", [128, 8, GMAX], BF16, parts=8)
    vvb = A("rw_vvb", [128, GMAX], BF16)
    k.ar["off"] = off_alias
    ar = A("rw_ar", [128, 8, 256], BF16, parts=8)
    bet = A("rw_bet", [128, 8, 128], BF16, parts=8)
    ktl = A("rw_ktl", [128, 8, 128], BF16, parts=8)
    tm3 = A("rw_tm3", [128, 3, 1024], BF16)
    btm, ktm, vtm = tm3[:, 0, :], tm3[:, 1, :], tm3[:, 2, :]
    AT = [A(f"rw_AT{i}", [128, 4, 512], BF16, parts=4) for i in range(1)] * 2
    P0 = [A(f"rw_P0{i}", [128, 4, 128], BF16) for i in range(1)] * 2
    QP = [A(f"rw_QP{i}", [128, 4, 256], BF16) for i in range(2)]
    Xb = [A(f"rw_Xb{i}", [128, 4, 64], BF16) for i in range(2)]
    Ub = A("rw_Ub", [128, 16, 64], BF16)
    ysb = A("rw_ysb", [128, 16, 64], F32)
    ysq = A("rw_ysq", [128, 16, 64], F32)
    elast = P.sb("rw_elast", [128, 8], F32)
    st4 = P.sb("rw_st4", [128, 4, 16], F32)
    nw0 = P.sb("rw_nw0", [128, 8], F32)
    cm = P.sb("rw_cm", [128, 2], F32)
    P.memset(cm[:, 0:1], -0.5)
    P.memset(cm[:, 1:2], 1e-24)
    vfirst = P.sb("rw_vfirst", [128, 8, GMAX], F32, parts=8)
    Hst = [P.sb(f"rw_H{l}", [128, 8, 64], F32) for l in range(L)]
    H0b = P.sb("rw_H0b", [128, 8, 64], BF16)
    shtail = [P.sb(f"rw_sht{l}", [128, 26], F32) for l in range(L)]
    for l in range(L):
        P.memset(Hst[l][:], 0.0)
        P.memset(shtail[l][:], 0.0)

    def bc(v, n):
        return v.re(lambda a: a.unsqueeze(2).broadcast_to([128, a.shape[1], n]))

    rtp = [A(f"rw_tmp{i}", [128, 128], F32) for i in range(14)]
    rtc = [0]

    def ctmp():
        rtc[0] += 1
        return rtp[rtc[0] % 14]

    def mix(l, g, G):
        nchk = G // 128
        fmc = k.fmc

        def ev(j, pp):
            P.copy(pa[:, j, 1:1 + G], pp, eng=k.ev_eng())
        k.proj_fm(hn, 8, lambda c, nb: k.wload(k.wview(w_in, (l,), OFF_A + c, nb)), 3328, G, ev)
        P.dma(w2s[0:64, :], View(k.rw_w2, k.rw_w2.h[l], 0, 1), eng="pool")
        P.dma(a2s[64:128, :], View(k.rw_a2, k.rw_a2.h[l], 0, 1), eng="pool")
        P.dma(g2s[:, :], View(k.rw_g2, k.rw_g2.h[l], 0, 1), eng="pool")
        if l > 0:
            P.dma(v1s[:], View(k.rw_v1, k.rw_v1.h[l - 1].rearrange("(kt p) r -> p kt r", p=128), 0, 1), eng="pool")
            P.dma(v2s[0:32, :], View(k.rw_v2, k.rw_v2.h[l - 1], 0, 1), eng="pool")
        P.ts(nw0[:], fmc(l, C_W0, 8), -1.0, ALU.mult)
        P.copy(pa[:, :, 0], shtail[l][:])
        P.copy(shtail[l][:], pa[:, :, G])
        for j in range(26):
            d = k.tmp()
            P.tt(d[:, 0:G], pa[:, j, 0:G], pa[:, j, 1:G + 1], ALU.subtract)
            P.stt(pa[:, j, 1:G + 1], d[:, 0:G], fmc(l, C_MU + j), pa[:, j, 1:G + 1], ALU.mult, ALU.add)
        if l == 0:
            for j in range(8):
                P.copy(vfirst[:, j, 0:G], pa[:, 16 + j, 1:G + 1], eng=k.ev_eng())
        else:
            for j in range(8):
                P.copy(vb16[:, j, 0:G], pa[:, 16 + j, 1:G + 1], eng=k.ev_eng())
            pv = k.nextA()
            for kt in range(8):
                P.mm(pv[0:32, 0:G], v1s[:, kt, :], vb16[:, kt, 0:G], start=(kt == 0), stop=(kt == 7))
            P.copy(vvb[0:32, 0:G], pv[0:32, 0:G])
            for j in range(8):
                pp = k.nextA()
                P.mm(pp[:, 0:G], v2s[0:32, j * 128:(j + 1) * 128], vvb[0:32, 0:G])
                sg = k.tmp()
                P.act(sg[:, 0:G], pp[:, 0:G], AF.Sigmoid, bias=fmc(l, C_V0 + j))
                d = k.tmp()
                vcur = pa[:, 16 + j, 1:G + 1]
                P.tt(d[:, 0:G], vfirst[:, j, 0:G], vcur, ALU.subtract)
                P.tt(d[:, 0:G], d[:, 0:G], sg[:, 0:G], ALU.mult)
                P.tt(vcur, vcur, d[:, 0:G], ALU.add)

        P.fence()
        for c in range(nchk):
            cs = slice(c * 128, (c + 1) * 128)
            c1 = slice(1 + c * 128, 1 + (c + 1) * 128)
            P.act(twb[0:64, :], pa[0:64, 24, c1], AF.Tanh)
            P.copy(adb[64:128, :], pa[64:128, 24, c1])
            P.act(sgb[:, :], pa[:, 25, c1], AF.Sigmoid)
            for j in range(8):
                js = slice(j * 128, (j + 1) * 128)
                ed_j, av_j, kk_j = ctmp()[:, 0:128], ctmp()[:, 0:128], ctmp()[:, 0:128]
                b3 = bt3[j % 2]
                r_, k_, v_ = pa[:, j, c1], pa[:, 8 + j, c1], pa[:, 16 + j, c1]
                pw = k.nextA()
                P.mm(pw[:, 0:128], w2s[0:64, js], twb[0:64, :])
                P.mm(pw[:, 128:256], a2s[64:128, js], adb[64:128, :])
                P.mm(pw[:, 256:384], g2s[:, js], sgb[:, :])
                t1 = ctmp()
                P.act(t1[:, 0:128], pw[:, 0:128], AF.Exp, scale=-1.0, bias=nw0[:, j:j + 1])
                P.act(t1[:, 0:128], t1[:, 0:128], AF.Ln, bias=k.eps_t[:, 1:2])
                P.act(ed_j, t1[:, 0:128], AF.Exp, scale=-1.0, bias=cm[:, 0:1])
                P.act(av_j, pw[:, 128:256], AF.Sigmoid, bias=fmc(l, C_A0 + j))
                P.copy(gfm[:, j, :], pw[:, 256:384])
                kq = ctmp()
                P.ts(kq[:, 0:128], k_, fmc(l, C_KK + j), ALU.mult)
                sq = ctmp()
                P.tt(sq[:, 0:128], kq[:, 0:128], kq[:, 0:128], ALU.mult, eng="pool")
                pn = k.nextB()
                P.mm(pn[:, 0:128], blk, sq[:, 0:128])
                rn = ctmp()
                P.act(rn[:, 0:128], pn[:, 0:128], AF.Ln, bias=cm[:, 1:2])
                P.act(rn[:, 0:128], rn[:, 0:128], AF.Exp, scale=-0.5)
                P.tt(kk_j, kq[:, 0:128], rn[:, 0:128], ALU.mult)
                t2 = ctmp()
                P.ts(t2[:, 0:128], av_j, -1.0, ALU.add, fmc(l, C_KA + j), ALU.mult)
                P.stt(k_, t2[:, 0:128], 1.0, k_, ALU.add, ALU.mult)
                pr = ctmp()
                P.stt(pr[:, 0:128], r_, fmc(l, C_RK + j), k_, ALU.mult, ALU.mult)
                P.mm(pn[:, 128:256], blk, pr[:, 0:128])
                P.tt(bonus[:, j, :], pn[:, 128:256], v_, ALU.mult)
                P.copy(b3[:, 2, :], v_, eng="act")
                cw = ctmp()
                P.scan(cw[:, 0:128], k.onesf_t[:, 0:128], ed_j, 0.0)
                e_in = ctmp()
                P.act(e_in[:, 0:128], cw[:, 0:128], AF.Exp, scale=-1.0)
                P.copy(elast[:, j:j + 1], e_in[:, 127:128])
                e_inv = ctmp()
                P.act(e_inv[:, 0:128], cw[:, 0:128], AF.Exp)
                e_ex = ctmp()
                P.tt(e_ex[:, 0:128], cw[:, 0:128], ed_j, ALU.subtract)
                P.act(e_ex[:, 0:128], e_ex[:, 0:128], AF.Exp, scale=-1.0)
                P.stt(ar[:, j, 0:128], kk_j, -1.0, e_ex[:, 0:128], ALU.mult, ALU.mult)
                P.tt(ar[:, j, 128:256], r_, e_in[:, 0:128], ALU.mult)
                P.tt(t2[:, 0:128], kk_j, av_j, ALU.mult, eng="pool")
                P.tt(bet[:, j, :], t2[:, 0:128], e_inv[:, 0:128], ALU.mult)
                P.tt(ktl[:, j, :], k_, e_inv[:, 0:128], ALU.mult)
                P.ts(b3[:, 0, :], bet[:, j, :], elast[:, j:j + 1], ALU.mult)
                P.ts(b3[:, 1, :], ktl[:, j, :], elast[:, j:j + 1], ALU.mult)
                for q in range(3):
                    P.tr(k.psT[:, q * 128:(q + 1) * 128], b3[:, q, :], identb)
                P.copy(tm3[:, :, js], k.psT[:, 0:384].re(lambda a: a.rearrange("p (q c) -> p q c", q=3)),
                       eng=k.ev_eng())
            P.copy(H0b[:], Hst[l][:])
            for hb in range(4):
                at = AT[hb % 2]
                p0 = P0[hb % 2]
                pp0 = k.nextB()
                for hh in range(4):
                    h = hb * 4 + hh
                    j, pb = h // 2, 64 * (h % 2)
                    b1 = k.nextA()
                    P.mm(b1[:, 0:256], bet[pb:pb + 64, j, :], ar[pb:pb + 64, j, :])
                    P.mm(b1[:, 256:512], ktl[pb:pb + 64, j, :], ar[pb:pb + 64, j, :])
                    P.tt(at[:, hh, :], b1[:, :], mask4, ALU.mult)
                    P.mm(pp0[:, hh * 128:(hh + 1) * 128], ar[pb:pb + 64, j, 0:128], bet[pb:pb + 64, j, :])
                P.tt(p0[:], pp0[:, :].re(lambda a: a.rearrange("p (h s) -> p h s", h=4)),
                     mstr.re(lambda a: a.unsqueeze(1).broadcast_to([128, 4, 128])), ALU.mult)
                px = k.nextB()
                for hh in range(4):
                    h = hb * 4 + hh
                    j, pb = h // 2, 64 * (h % 2)
                    P.mm(px[:, hh * 64:(hh + 1) * 64], ar[pb:pb + 64, j, 0:128], H0b[pb:pb + 64, j, :],
                         start=True, stop=False)
                    P.mm(px[:, hh * 64:(hh + 1) * 64], at[:, hh, 256:384], vtm[:, h * 64:(h + 1) * 64],
                         start=False, stop=True)
                xcur = Xb[0]
                P.copy(xcur[:], px[:, 0:256].re(lambda a: a.rearrange("p (h v) -> p h v", h=4)), eng="act")
                for step in range(7):
                    def Q(hh):
                        return at[:, hh, 0:128] if step == 0 else QP[(step - 1) % 2][:, hh, 0:128]

                    def Pm(hh):
                        return p0[:, hh, :] if step == 0 else QP[(step - 1) % 2][:, hh, 128:256]
                    px = k.nextB()
                    for hh in range(4):
                        P.mm(px[:, hh * 64:(hh + 1) * 64], identb, xcur[:, hh, :], start=True, stop=False)
                        P.mm(px[:, hh * 64:(hh + 1) * 64], Q(hh), xcur[:, hh, :], start=False, stop=True)
                    if step < 6:
                        xn = Xb[(step + 1) % 2]
                        P.copy(xn[:], px[:, 0:256].re(lambda a: a.rearrange("p (h v) -> p h v", h=4)), eng="act")
                        xcur = xn
                        qn = QP[step % 2]
                        for half in range(2):
                            pq = k.nextA()
                            for h2 in range(2):
                                hh = half * 2 + h2
                                P.mm(pq[:, h2 * 256:h2 * 256 + 128], Pm(hh), Q(hh))
                                P.mm(pq[:, h2 * 256 + 128:h2 * 256 + 256], Q(hh), Pm(hh))
                            P.copy(qn[:, half * 2:half * 2 + 2, :],
                                   pq[:, :].re(lambda a: a.rearrange("p (h s) -> p h s", h=2)), eng="dve")
                    else:
                        P.copy(Ub[:, hb * 4:(hb + 1) * 4, :],
                               px[:, 0:256].re(lambda a: a.rearrange("p (h v) -> p h v", h=4)), eng="act")
                py = k.nextB()
                for hh in range(4):
                    h = hb * 4 + hh
                    j, pb = h // 2, 64 * (h % 2)
                    P.mm(py[:, hh * 64:(hh + 1) * 64], ar[pb:pb + 64, j, 128:256], H0b[pb:pb + 64, j, :],
                         start=True, stop=False)
                    P.mm(py[:, hh * 64:(hh + 1) * 64], at[:, hh, 128:256], Ub[:, h, :], start=False, stop=False)
                    P.mm(py[:, hh * 64:(hh + 1) * 64], at[:, hh, 384:512], vtm[:, h * 64:(h + 1) * 64],
                         start=False, stop=True)
                P.copy(ysb[:, hb * 4:(hb + 1) * 4, :], py[:, 0:256].re(lambda a: a.rearrange("p (h v) -> p h v", h=4)),
                       eng="dve")
            for j in range(8):
                ph = k.nextA()
                P.mm(ph[:, 0:128], btm[:, j * 128:(j + 1) * 128],
                     Ub[:, 2 * j:2 * j + 2, :].re(lambda a: a.rearrange("p h v -> p (h v)")), start=True, stop=False)
                P.mm(ph[:, 0:128], ktm[:, j * 128:(j + 1) * 128], vtm[:, j * 128:(j + 1) * 128], start=False, stop=True)
                for hp in range(2):
                    pb = 64 * hp
                    P.stt(Hst[l][pb:pb + 64, j, :], Hst[l][pb:pb + 64, j, :], elast[pb:pb + 64, j:j + 1],
                          ph[pb:pb + 64, hp * 64:(hp + 1) * 64], ALU.mult, ALU.add)
            s1, s2, mean, var = st4[:, 0, :], st4[:, 1, :], st4[:, 2, :], st4[:, 3, :]
            P.red(s1, ysb[:])
            P.tt(ysq[:], ysb[:], ysb[:], ALU.mult, eng="pool")
            P.red(s2, ysq[:])
            P.ts(mean, s1, 1.0 / 64, ALU.mult)
            P.tt(var, mean, mean, ALU.mult)
            P.stt(var, s2, 1.0 / 64, var, ALU.mult, ALU.subtract)
            P.ts(var, var, 64e-5, ALU.add)
            P.act(var, var, AF.Ln)
            P.act(var, var, AF.Exp, scale=-0.5)
            P.tt(ysb[:], ysb[:], bc(mean, 64), ALU.subtract)
            P.tt(ysb[:], ysb[:], bc(var, 64), ALU.mult)
            for kt in range(8):
                pp = k.nextA()
                P.tr(pp[:, 0:128], ysb[:, 2 * kt:2 * kt + 2, :].re(lambda a: a.rearrange("p h v -> p (h v)")), ident)
                t3 = ctmp()
                P.ts(t3[:, 0:128], pp[:, 0:128], fmc(l, C_LNW + kt), ALU.mult, fmc(l, C_LNB + kt), ALU.add)
                P.tt(t3[:, 0:128], t3[:, 0:128], bonus[:, kt, :], ALU.add)
                P.tt(ys[0][:, kt, cs], t3[:, 0:128], gfm[:, kt, :], ALU.mult)
    return mix


MIXER_BUILDERS["a"] = build_rwkv
```

```python
import numpy as np
import concourse.bass as bass
import concourse.mybir as mybir

F32 = mybir.dt.float32
BF16 = mybir.dt.bfloat16
AF = mybir.ActivationFunctionType
ALU = mybir.AluOpType
AX = mybir.AxisListType

COMPUTE = ("pe", "act", "dve", "pool")
ALLENG = ("pe", "act", "dve", "pool", "sp")
EPOCH = 30000
NDMASEM = 24
SAME_ENGINE_SYNC = True


class View:
    __slots__ = ("t", "ap", "p0", "p1")

    def __init__(self, t, ap, p0, p1):
        self.t, self.ap, self.p0, self.p1 = t, ap, p0, p1

    def re(self, fn):
        return View(self.t, fn(self.ap), self.p0, self.p1)

    def __getitem__(self, idx):
        return View(self.t, self.ap[idx], self.p0, self.p1)


class Tile:
    def __init__(self, name, h, nparts=1):
        self.name, self.h, self.nparts = name, h, nparts
        self.psum = False
        self.lastw = [None] * nparts
        self.readers = [[] for _ in range(nparts)]

    def __getitem__(self, idx):
        ap = self.h[idx]
        if self.nparts == 1:
            return View(self, ap, 0, 1)
        i = idx[1] if isinstance(idx, tuple) and len(idx) > 1 else slice(None)
        if isinstance(i, int):
            return View(self, ap, i, i + 1)
        p0, p1, _ = i.indices(self.nparts)
        return View(self, ap, p0, p1)

    def all(self):
        return self[:]


class Op:
    __slots__ = ("eng", "fn", "deps", "dma", "pos", "gid", "sig", "signaler", "waits", "dpos", "rg")


class Prog:
    def __init__(self, nc):
        self.nc = nc
        self.ops = []
        self.eng_ops = {e: [] for e in ALLENG}
        self.dma_ops = {e: [] for e in ALLENG}
        self.ntile = 0
        self.fence_dma = {}

    def sb(self, name, shape, dtype=F32, parts=1):
        self.ntile += 1
        h = self.nc.alloc_sbuf_tensor(f"{name}_{self.ntile}", list(shape), dtype)
        return Tile(name, h, parts)

    def ps(self, name, shape, dtype=F32, parts=1):
        self.ntile += 1
        h = self.nc.alloc_psum_tensor(f"{name}_{self.ntile}", list(shape), dtype)
        t = Tile(name, h, parts)
        t.psum = True
        return t

    def dram(self, name, shape, dtype=F32, kind="Internal", parts=1):
        h = self.nc.dram_tensor(name, list(shape), dtype, kind=kind).ap()
        return Tile(name, h, parts)

    def add(self, eng, fn, reads=(), writes=(), dma=False):
        op = Op()
        op.eng, op.fn, op.dma = eng, fn, dma
        op.gid = len(self.ops)
        op.rg = None
        op.sig = None
        op.signaler = False
        deps = set()
        for v in reads:
            if v is None:
                continue
            t = v.t
            for p in range(v.p0, v.p1):
                if t.lastw[p] is not None:
                    deps.add(t.lastw[p])
                if t.psum:
                    for r in t.readers[p]:
                        if r.eng != eng:
                            deps.add(r)
        for v in writes:
            t = v.t
            for p in range(v.p0, v.p1):
                if t.lastw[p] is not None:
                    deps.add(t.lastw[p])
                for r in t.readers[p]:
                    deps.add(r)
        for v in reads:
            if v is None:
                continue
            t = v.t
            for p in range(v.p0, v.p1):
                t.readers[p].append(op)
        for v in writes:
            t = v.t
            for p in range(v.p0, v.p1):
                t.lastw[p] = op
                t.readers[p] = []
        deps.discard(op)
        if dma:
            lst = self.dma_ops[eng]
            op.dpos = len(lst)
            if op.dpos >= NDMASEM:
                deps.add(lst[op.dpos - NDMASEM])
            lst.append(op)
        op.deps = deps
        op.pos = len(self.eng_ops[eng])
        self.eng_ops[eng].append(op)
        self.ops.append(op)
        return op

    def fence(self):
        lastc = [self.eng_ops[e][-1] for e in ALLENG if self.eng_ops[e] and not self.eng_ops[e][-1].dma]
        lastc = []
        for e in ALLENG:
            for op in reversed(self.eng_ops[e]):
                if not op.dma:
                    lastc.append(op)
                    break
        dmas = []
        for e in ALLENG:
            dmas += self.dma_ops[e][self.fence_dma.get(e, 0):]
            self.fence_dma[e] = len(self.dma_ops[e])
        for e in ALLENG:
            if not self.eng_ops[e]:
                continue
            op = self.add(e, None, [], [])
            op.deps |= set(lastc) | set(dmas)
            op.deps.discard(op)

    def mm(self, out, lhsT, rhs, start=True, stop=True, **kw):
        op = self.add("pe", lambda e: e.matmul(out.ap, lhsT.ap, rhs.ap, start=start, stop=stop, **kw),
                      [lhsT, rhs] + ([] if start else [out]), [out])
        op.rg = (lhsT.ap.base_partition(), lhsT.ap.shape[0], out.ap.base_partition())
        return op

    def tr(self, out, in_, ident):
        op = self.add("pe", lambda e: e.transpose(out.ap, in_.ap, ident.ap), [in_, ident], [out])
        op.rg = (in_.ap.base_partition(), in_.ap.shape[0], out.ap.base_partition())
        return op

    def act(self, out, in_, func, bias=None, scale=None, accum=None, eng="act"):
        def fn(e):
            kw = {}
            if bias is not None:
                kw["bias"] = bias.ap if isinstance(bias, View) else bias
            if scale is not None:
                kw["scale"] = scale.ap if isinstance(scale, View) else scale
            if accum is not None:
                kw["accum_out"] = accum.ap
            return e.activation(out.ap, in_.ap, func, **kw)
        rd = [in_] + [x for x in (bias, scale) if isinstance(x, View)]
        wr = [out] + ([accum] if accum is not None else [])
        return self.add(eng, fn, rd, wr)

    def tt(self, out, a, b, op, eng="dve"):
        return self.add(eng, lambda e: e.tensor_tensor(out.ap, a.ap, b.ap, op), [a, b], [out])

    def ts(self, out, a, s1, op0, s2=None, op1=None, accum=None, eng="dve"):
        def fn(e):
            kw = {}
            if accum is not None:
                kw["accum_out"] = accum.ap
            return e.tensor_scalar(out.ap, a.ap, s1.ap if isinstance(s1, View) else s1,
                                   (s2.ap if isinstance(s2, View) else s2), op0,
                                   op1 if op1 is not None else ALU.bypass, **kw)
        rd = [a] + [x for x in (s1, s2) if isinstance(x, View)]
        wr = [out] + ([accum] if accum is not None else [])
        return self.add(eng, fn, rd, wr)

    def stt(self, out, a, s, b, op0, op1, eng="dve"):
        return self.add(eng, lambda e: e.scalar_tensor_tensor(out.ap, a.ap, s.ap if isinstance(s, View) else s,
                                                              b.ap, op0, op1),
                        [a, b] + ([s] if isinstance(s, View) else []), [out])

    def copy(self, out, in_, eng="dve"):
        if eng == "act":
            return self.add(eng, lambda e: e.copy(out.ap, in_.ap), [in_], [out])
        return self.add(eng, lambda e: e.tensor_copy(out.ap, in_.ap), [in_], [out])

    def memset(self, out, val, eng="dve"):
        return self.add(eng, lambda e: e.memset(out.ap, val), [], [out])

    def red(self, out, in_, op=ALU.add, axis=AX.X, eng="dve"):
        return self.add(eng, lambda e: e.tensor_reduce(out.ap, in_.ap, axis, op), [in_], [out])

    def scan(self, out, d0, d1, init, op0=ALU.mult, op1=ALU.add):
        return self.add("dve", lambda e: e.tensor_tensor_scan(out.ap, d0.ap, d1.ap,
                                                              init.ap if isinstance(init, View) else init, op0, op1),
                        [d0, d1] + ([init] if isinstance(init, View) else []), [out])

    def recip(self, out, in_):
        return self.add("dve", lambda e: e.reciprocal(out.ap, in_.ap), [in_], [out])

    def dma(self, out, in_, eng="sp", **kw):
        return self.add(eng, lambda e: e.dma_start(out=out.ap, in_=in_.ap, **kw), [in_], [out], dma=True)

    def emit(self):
        nc = self.nc
        seen = {f: {e: -1 for e in COMPUTE + ("sp",)} for f in ALLENG}
        seen_dma = {f: set() for f in ALLENG}
        for op in self.ops:
            f = op.eng
            best = {}
            dwaits = []
            for d in op.deps:
                if d.dma:
                    if d.gid in seen_dma[f]:
                        continue
                    seen_dma[f].add(d.gid)
                    dwaits.append(d)
                    d.signaler = True
                else:
                    if d.eng == f and f != "pe" and not SAME_ENGINE_SYNC:
                        continue
                    if d.eng == "pe" and f == "pe" and d.rg is not None and d.rg == op.rg:
                        continue
                    if seen[f][d.eng] >= d.pos:
                        continue
                    if d.eng not in best or best[d.eng].pos < d.pos:
                        best[d.eng] = d
            for e, d in best.items():
                seen[f][e] = d.pos
                d.signaler = True
            op.waits = list(best.values()) + dwaits
        nsig = {e: 0 for e in ALLENG}
        for e in ALLENG:
            for op in self.eng_ops[e]:
                if op.dma:
                    op.sig = ("dma", e, op.dpos % NDMASEM, 16 * (op.dpos // NDMASEM + 1))
                elif op.signaler:
                    n = nsig[e]
                    op.sig = ("cmp", e, n // EPOCH, n % EPOCH + 1)
                    nsig[e] = n + 1
        sems = {}
        stack = []

        def getsem(key):
            if key not in sems:
                cm = nc.semaphore("s_%s_%s_%d" % key)
                sems[key] = cm.__enter__()
                stack.append(cm)
            return sems[key]

        for op in self.ops:
            if op.sig is not None:
                getsem(op.sig[:3])
        engobj = {"pe": "tensor", "act": "scalar", "dve": "vector", "pool": "gpsimd", "sp": "sync"}
        self.final_dma = {e: list(self.dma_ops[e]) for e in ALLENG}

        with nc.Block() as block:
            for e in ALLENG:
                ops = self.eng_ops[e]
                if not ops:
                    continue

                def body(eng, ops=ops, e=e):
                    for op in ops:
                        for d in op.waits:
                            eng.wait_ge(sems[d.sig[:3]], d.sig[3])
                        if op.fn is None:
                            if op.sig is None:
                                continue
                            ins = eng.nop()
                        else:
                            ins = op.fn(eng)
                        if op.sig is not None:
                            ins.then_inc(sems[op.sig[:3]], 16 if op.dma else 1)
                    lst = self.dma_ops[e]
                    done = set()
                    for op in reversed(lst):
                        k = op.sig[:3]
                        if k in done:
                            continue
                        done.add(k)
                        eng.wait_ge(sems[k], op.sig[3])

                getattr(block, engobj[e])(body)
        for cm in reversed(stack):
            cm.__exit__(None, None, None)
        return nc

from concourse.bass_utils import run_bass_kernel_spmd

D = 1024
NMETA = 16
PAD = 112
OFF_A, OFF_B, OFF_C, OFF_D, OFF_G, IN_W = 0, 3328, 6416, 9488, 11536, 15632
FFH = 2816
C_NM, C_NF, C_MU, C_W0, C_A0, C_KK, C_KA, C_RK, C_V0 = 0, 8, 16, 42, 50, 58, 66, 74, 82
C_SCW, C_SCB, C_LCW, C_LCB, C_BG, C_LAM, C_LNW, C_LNB, NFM = 90, 154, 170, 202, 210, 226, 234, 242, 250
R_LNW, R_LNB, R_SSMN, R_DTB, R_ALOG, R_SD, NROW = 0, 1024, 2048, 3072, 3088, 3104, 3120


class K:
    pass


def build(nchunks, depth, nch_group, enable=("a", "b", "c", "d"), arena_kb=82):
    nc = bass.Bass("TRN2", target_bir_lowering=False)
    P = Prog(nc)
    TP = nchunks * 128
    L = depth
    GMAX = nch_group * 128
    k = K()
    xin = P.dram("xin", [TP, D], F32, kind="ExternalInput")
    out = P.dram("out", [TP - 128, D], F32, kind="ExternalOutput")
    w_in = P.dram("w_in", [L, D, IN_W], F32, kind="ExternalInput")
    w_qksw = P.dram("w_qksw", [L, D, 1024], F32, kind="ExternalInput")
    w_branch = P.dram("w_branch", [L, 4, D, D], F32, kind="ExternalInput")
    w_out = P.dram("w_out", [L, D, D], F32, kind="ExternalInput")
    w_ffn_in = P.dram("w_ffn_in", [L, D, 2 * FFH], F32, kind="ExternalInput")
    w_ffn_out = P.dram("w_ffn_out", [L, FFH, D], F32, kind="ExternalInput")
    fmpack = P.dram("fmpack", [128, L * NFM + 8], F32, kind="ExternalInput")
    rowpack = P.dram("rowpack", [L, NROW], F32, kind="ExternalInput")
    lru_wg = P.dram("lru_wg", [L, 2, 8, 128, 128], F32, kind="ExternalInput")
    consts = P.dram("consts", [128, 1280], F32, kind="ExternalInput")
    ropetab = P.dram("ropetab", [128, 2, TP], F32, kind="ExternalInput")
    rettab = P.dram("rettab", [128, 1548], F32, kind="ExternalInput")
    rw_w2 = P.dram("rw_w2", [L, 64, D], F32, kind="ExternalInput")
    rw_a2 = P.dram("rw_a2", [L, 64, D], F32, kind="ExternalInput")
    rw_g2 = P.dram("rw_g2", [L, 128, D], F32, kind="ExternalInput")
    rw_v1 = P.dram("rw_v1", [max(L - 1, 1), D, 32], F32, kind="ExternalInput")
    rw_v2 = P.dram("rw_v2", [max(L - 1, 1), 32, D], F32, kind="ExternalInput")

    def cast_weight(name, src_tile, idx, shape):
        nblk = max(1, shape[0] // 256)
        t = P.dram(name, list(shape), BF16, kind="Internal", parts=nblk)
        rb = shape[0] // nblk
        for i in range(nblk):
            src_ap = src_tile.h[idx + (slice(i * rb, (i + 1) * rb), slice(None))]
            P.dma(View(t, t.h[i * rb:(i + 1) * rb, :], i, i + 1), View(src_tile, src_ap, 0, 1), eng="pool")
        return t

    wb_in, wb_qksw, wb_branch, wb_out, wb_ffi, wb_ffo = [], [], [], [], [], []
    for l in range(L):
        wb_in.append(cast_weight(f"wb_in{l}", w_in, (l,), [D, IN_W]))
        wb_qksw.append(cast_weight(f"wb_qksw{l}", w_qksw, (l,), [D, 1024]))
        wb_branch.append([cast_weight(f"wb_br{l}_{n}", w_branch, (l, n), [D, D]) for n in range(4)])
        wb_out.append(cast_weight(f"wb_out{l}", w_out, (l,), [D, D]))
        wb_ffi.append(cast_weight(f"wb_ffi{l}", w_ffn_in, (l,), [D, 2 * FFH]))
        wb_ffo.append(cast_weight(f"wb_ffo{l}", w_ffn_out, (l,), [FFH, D]))

    cst = P.sb("cst", [128, 1280], F32)
    P.dma(cst[:], consts[:])
    ident = cst[:, 0:128]
    identb_t = P.sb("identb", [128, 128], BF16)
    P.copy(identb_t[:], ident)
    identb = identb_t[:]
    onesb_t = P.sb("onesb", [128, 128], BF16)
    P.memset(onesb_t[:], 1.0)
    onesb = onesb_t[:]
    fm = P.sb("fm", [128, L * NFM + 8], F32)
    P.dma(fm[:], fmpack[:])

    def fmc(l, off, n=1):
        return fm[:, l * NFM + off: l * NFM + off + n]

    psA = [P.ps(f"psA{i}", [128, 512]) for i in range(4)]
    psB = [P.ps(f"psB{i}", [128, 512]) for i in range(3)]
    psT = P.ps("psT", [128, 1024], BF16)
    cnt = {"A": 0, "B": 0, "ev": 0, "w": 0}

    def nextA():
        cnt["A"] += 1
        return psA[cnt["A"] % 4]

    def nextB():
        cnt["B"] += 1
        return psB[cnt["B"] % 3]

    def ev_eng():
        cnt["ev"] += 1
        return "act" if cnt["ev"] % 2 else "dve"

    s = P.sb("s", [128, 8, GMAX], F32, parts=8)
    hn = P.sb("hn", [128, 8, GMAX], BF16, parts=8)
    rstd = P.sb("rstd", [128, GMAX], F32)
    ys = [P.sb(f"ys{n}", [128, 8, GMAX], BF16, parts=8) for n in range(4)]
    mfm = P.sb("mfm", [128, 8, GMAX], BF16, parts=8)
    NW = 3
    wbuf = [P.sb(f"wbuf{i}", [128, 8, 512], BF16) for i in range(NW)]
    arena_t = P.sb("arena", [128, arena_kb * 256], F32)
    ar = {"off": 0}

    def ar_reset():
        ar["off"] = 0

    def ar_alloc(name, shape, dtype=F32, parts=1):
        n = 1
        for d_ in shape[1:]:
            n *= d_
        nf = n if dtype == F32 else (n + 1) // 2
        nf = (nf + 7) // 8 * 8
        off = ar["off"]
        assert off + nf <= arena_kb * 256, (name, off, nf)
        ar["off"] = off + nf
        ap = arena_t.h[:, off:off + nf]
        if dtype != F32:
            ap = ap.bitcast(dtype)[:, 0:n]
        else:
            ap = ap[:, 0:n]
        if len(shape) == 3:
            ap = ap.rearrange("p (a b) -> p a b", a=shape[1])
        return Tile(name, ap, parts)

    def rowload(dst, l, off, n):
        P.dma(dst, View(rowpack, rowpack.h[l:l + 1, off:off + n].partition_broadcast(128), 0, 1))
    tmpf = [P.sb(f"tmpf{i}", [128, GMAX], F32) for i in range(8)]
    tcnt = [0]

    def tmp():
        tcnt[0] += 1
        return tmpf[tcnt[0] % 8]

    def wload(dram_view):
        cnt["w"] += 1
        wt = wbuf[cnt["w"] % NW]
        kt, n = dram_view.ap.shape[1], dram_view.ap.shape[2]
        P.dma(wt[:, 0:kt, 0:n], dram_view, eng="sp")
        return wt

    def wview(t, l_idx, c0, ncols, r0=0, nrows=1024):
        ap = t.h[l_idx + (slice(r0, r0 + nrows), slice(c0, c0 + ncols))]
        ap = ap.rearrange("(kt p) n -> p kt n", p=128)
        return View(t, ap, 0, t.nparts)

    ar_reset()
    sq = ar_alloc("sq", [128, 8, GMAX], F32, parts=8)
    sqb = ar_alloc("sqb", [128, 8, GMAX], BF16, parts=8)
    xtm = [ar_alloc(f"xtm{i}", [128, D], F32) for i in range(2)]
    macc = [ar_alloc(f"macc{i}", [128, GMAX], F32) for i in range(4)]

    def rmsnorm(G, wcol, outt, eps=1e-6):
        for kt in range(8):
            P.act(sqb[:, kt, 0:G], s[:, kt, 0:G], AF.Square)
        pp = nextA()
        for kt in range(8):
            P.mm(pp[:, 0:G], onesb, sqb[:, kt, 0:G], start=(kt == 0), stop=(kt == 7))
        P.act(rstd[:, 0:G], pp[:, 0:G], AF.Ln, scale=1.0 / D, bias=eps_t[:, 0:1])
        P.act(rstd[:, 0:G], rstd[:, 0:G], AF.Exp, scale=-0.5)
        for kt in range(8):
            P.stt(outt[:, kt, 0:G], s[:, kt, 0:G], wcol[:, kt:kt + 1], rstd[:, 0:G], ALU.mult, ALU.mult)

    onesf_t = P.sb("onesf", [128, 128], F32)
    P.memset(onesf_t[:], 1.0)
    onesb_f = onesf_t
    eps_t = P.sb("eps", [128, 2], F32)
    P.memset(eps_t[:, 0:1], 1e-6)
    P.memset(eps_t[:, 1:2], 1.0)

    def proj_fm(src, KT, wt_fn, ncols, G, evac):
        nblk = (ncols + 511) // 512
        for b in range(nblk):
            nb = min(512, ncols - b * 512)
            wt = wt_fn(b * 512, nb)
            for jj in range(nb // 128):
                pp = nextA()
                for kt in range(KT):
                    P.mm(pp[:, 0:G], wt[:, kt, jj * 128:(jj + 1) * 128], src[:, kt, 0:G],
                         start=(kt == 0), stop=(kt == KT - 1))
                evac(b * 4 + jj, pp[:, 0:G])

    def proj_tm(src, KT, wt_fn, ncols, G, evac):
        nblk = (ncols + 511) // 512
        for b in range(nblk):
            nb = min(512, ncols - b * 512)
            wt = wt_fn(b * 512, nb)
            for c in range(G // 128):
                pp = nextA()
                for kt in range(KT):
                    P.mm(pp[:, 0:nb], src[:, kt, c * 128:(c + 1) * 128], wt[:, kt, 0:nb],
                         start=(kt == 0), stop=(kt == KT - 1))
                evac(c, b, pp[:, 0:nb], nb)

    k.__dict__.update(locals())

    lru_tail = [P.sb(f"lru_tail{l}", [128, 8, 3], F32) for l in range(L)]
    lru_h = [P.sb(f"lru_h{l}", [128, 8], F32) for l in range(L)]
    lru_nsp = [P.sb(f"lru_nsp{l}", [128, 8], F32) for l in range(L)]
    ar_reset()
    lru_wgs = ar_alloc("lru_wgs", [128, 16, 128], BF16)
    lru_xcb = ar_alloc("lru_xcb", [128, GMAX], BF16)
    pin = ar_alloc("lru_pin", [128, 16, GMAX + 4], F32, parts=16)
    for l in range(L):
        P.memset(lru_tail[l][:], 0.0)
        P.memset(lru_h[l][:], 0.0)
        P.act(lru_nsp[l][:], fmc(l, C_LAM, 8), AF.Exp, scale=-1.0)
        P.act(lru_nsp[l][:], lru_nsp[l][:], AF.Ln, bias=eps_t[:, 1:2])
        P.ts(lru_nsp[l][:], lru_nsp[l][:], -8.0, ALU.mult)

    def gelu_tanh(outv, xv, G):
        t1 = tmp()
        P.act(t1[:, 0:G], xv, AF.Square)
        P.ts(t1[:, 0:G], t1[:, 0:G], 0.044715, ALU.mult, 1.0, ALU.add)
        P.tt(t1[:, 0:G], t1[:, 0:G], xv, ALU.mult)
        P.act(t1[:, 0:G], t1[:, 0:G], AF.Sigmoid, scale=1.5957691216057308)
        P.tt(outv, t1[:, 0:G], xv, ALU.mult)

    def lru_mix(l, g, G):
        def wt_fn(c0, nb):
            return wload(wview(wb_in[l], (), OFF_D + c0, nb))

        def evac(j, pp):
            if j < 8:
                P.copy(pin[:, j, 0:G], pp, eng=ev_eng())
            else:
                P.copy(pin[:, j, 3:3 + G], pp, eng=ev_eng())
        proj_fm(hn, 8, wt_fn, 2048, G, evac)
        P.dma(lru_wgs[:], View(lru_wg, lru_wg.h[l].rearrange("k n c e -> c (k n) e"), 0, 1), eng="pool")
        for n in range(8):
            xe = pin[:, 8 + n, :]
            P.copy(xe[:, 0:3], lru_tail[l][:, n, :])
            xc = tmp()
            lw = C_LCW + n * 4
            P.ts(xc[:, 0:G], xe[:, 0:G], fmc(l, lw), ALU.mult, fmc(l, C_LCB + n), ALU.add)
            for j in range(1, 4):
                P.stt(xc[:, 0:G], xe[:, j:j + G], fmc(l, lw + j), xc[:, 0:G], ALU.mult, ALU.add)
            P.copy(lru_tail[l][:, n, :], xe[:, G:G + 3])
            xcb16 = lru_xcb
            P.copy(xcb16[:, 0:G], xc[:, 0:G])
            gates = []
            for kk_ in range(2):
                pp = nextB()
                P.mm(pp[:, 0:G], lru_wgs[:, kk_ * 8 + n, :], xcb16[:, 0:G])
                gt = tmp()
                P.act(gt[:, 0:G], pp[:, 0:G], AF.Sigmoid, bias=fmc(l, C_BG + kk_ * 8 + n))
                gates.append(gt)
            rg, ig = gates
            la = tmp()
            P.ts(la[:, 0:G], rg[:, 0:G], lru_nsp[l][:, n:n + 1], ALU.mult)
            av = tmp()
            P.act(av[:, 0:G], la[:, 0:G], AF.Exp)
            P.act(la[:, 0:G], la[:, 0:G], AF.Exp, scale=2.0)
            P.ts(la[:, 0:G], la[:, 0:G], -1.0, ALU.mult, 1.0, ALU.add)
            P.act(la[:, 0:G], la[:, 0:G], AF.Sqrt)
            P.tt(ig[:, 0:G], ig[:, 0:G], xc[:, 0:G], ALU.mult)
            P.tt(ig[:, 0:G], ig[:, 0:G], la[:, 0:G], ALU.mult)
            if g == 0:
                P.memset(ig[:, 0:PAD], 0.0)
            P.scan(rg[:, 0:G], av[:, 0:G], ig[:, 0:G], lru_h[l][:, n:n + 1])
            P.copy(lru_h[l][:, n:n + 1], rg[:, G - 1:G])
            gelu_tanh(la[:, 0:G], pin[:, n, 0:G], G)
            P.tt(ys[3][:, n, 0:G], rg[:, 0:G], la[:, 0:G], ALU.mult)


    def zero_y(n, G):
        for kt in range(8):
            P.memset(ys[n][:, kt, 0:G], 0.0, eng="pool")

    mixers = {"a": None, "b": None, "c": None, "d": lru_mix}
    k.__dict__.update(locals())
    for nm, modfn in MIXER_BUILDERS.items():
        ar_reset()
        mixers[nm] = modfn(k)

    def layer(l, g, G):
        rmsnorm(G, fmc(l, C_NM, 8), hn)
        for n, nm in enumerate("abcd"):
            if nm in enable and mixers[nm] is not None:
                P.fence()
                mixers[nm](l, g, G)
            else:
                zero_y(n, G)
        P.fence()
        for db in range(2):
            for n in range(4):
                wb = wload(wview(wb_branch[l][n], (), db * 512, 512))
                wg = wload(wview(wb_in[l], (), OFF_G + n * 1024 + db * 512, 512))
                for dj in range(4):
                    pz = nextA()
                    for kt in range(8):
                        P.mm(pz[:, 0:G], wb[:, kt, dj * 128:(dj + 1) * 128], ys[n][:, kt, 0:G],
                             start=(kt == 0), stop=(kt == 7))
                    pg = nextA()
                    for kt in range(8):
                        P.mm(pg[:, 0:G], wg[:, kt, dj * 128:(dj + 1) * 128], hn[:, kt, 0:G],
                             start=(kt == 0), stop=(kt == 7))
                    sg = tmp()
                    P.act(sg[:, 0:G], pg[:, 0:G], AF.Sigmoid)
                    if n == 0:
                        P.tt(macc[dj][:, 0:G], sg[:, 0:G], pz[:, 0:G], ALU.mult)
                    else:
                        P.tt(sg[:, 0:G], sg[:, 0:G], pz[:, 0:G], ALU.mult)
                        P.tt(macc[dj][:, 0:G], macc[dj][:, 0:G], sg[:, 0:G], ALU.add, eng="pool")
            for dj in range(4):
                P.copy(mfm[:, db * 4 + dj, 0:G], macc[dj][:, 0:G])

        def evac_out(j, pp):
            P.tt(s[:, j, 0:G], s[:, j, 0:G], pp, ALU.add)
        proj_fm(mfm, 8, lambda c0, nb: wload(wview(wb_out[l], (), c0, nb)), 1024, G, evac_out)
        rmsnorm(G, fmc(l, C_NF, 8), hn)
        for hb in range(6):
            nb = min(512, FFH - hb * 512)
            wg_ = wload(wview(wb_ffi[l], (), hb * 512, nb))
            wu_ = wload(wview(wb_ffi[l], (), FFH + hb * 512, nb))
            for jj in range(nb // 128):
                pg = nextA()
                pu = nextA()
                for kt in range(8):
                    P.mm(pg[:, 0:G], wg_[:, kt, jj * 128:(jj + 1) * 128], hn[:, kt, 0:G], start=(kt == 0), stop=(kt == 7))
                for kt in range(8):
                    P.mm(pu[:, 0:G], wu_[:, kt, jj * 128:(jj + 1) * 128], hn[:, kt, 0:G], start=(kt == 0), stop=(kt == 7))
                sg = tmp()
                P.act(sg[:, 0:G], pg[:, 0:G], AF.Silu)
                ht_ = hb * 4 + jj
                P.tt(ys[ht_ // 8][:, ht_ % 8, 0:G], sg[:, 0:G], pu[:, 0:G], ALU.mult)
        for db in range(2):
            accs = [psA[i] for i in range(4)]
            for hb3, (h0, nh) in enumerate(((0, 8), (8, 8), (16, 6))):
                wt = wload(wview(wb_ffo[l], (), db * 512, 512, r0=h0 * 128, nrows=nh * 128))
                for dj in range(4):
                    for hh in range(nh):
                        ht = h0 + hh
                        P.mm(accs[dj][:, 0:G], wt[:, hh, dj * 128:(dj + 1) * 128], ys[ht // 8][:, ht % 8, 0:G],
                             start=(ht == 0), stop=(ht == 21))
            for dj in range(4):
                P.tt(s[:, db * 4 + dj, 0:G], s[:, db * 4 + dj, 0:G], accs[dj][:, 0:G], ALU.add)
        if g == 0:
            for kt in range(8):
                P.memset(s[:, kt, 0:PAD], 0.0, eng="pool")

    ngroups = (nchunks + nch_group - 1) // nch_group
    otm = xtm
    fin = sq
    for g in range(ngroups):
        c0 = g * nch_group
        c1 = min(nchunks, c0 + nch_group)
        G = (c1 - c0) * 128
        P.fence()
        for c in range(c0, c1):
            xt = xtm[c % 2]
            P.dma(xt[:], xin[c * 128:(c + 1) * 128, :])
            for kt in range(8):
                pp = nextB()
                P.tr(pp[:, 0:128], xt[:, kt * 128:(kt + 1) * 128], ident)
                P.copy(s[:, kt, (c - c0) * 128:(c - c0 + 1) * 128], pp[:, 0:128], eng=ev_eng())
        for l in range(L):
            layer(l, g, G)
        P.fence()
        rmsnorm(G, fm[:, L * NFM:L * NFM + 8], fin)
        for c in range(c0, c1):
            if c == 0:
                continue
            ot = otm[c % 2]
            for kt in range(8):
                pp = nextB()
                P.tr(pp[:, 0:128], fin[:, kt, (c - c0) * 128:(c - c0 + 1) * 128], ident)
                P.copy(ot[:, kt * 128:(kt + 1) * 128], pp[:, 0:128], eng=ev_eng())
            P.dma(out[(c - 1) * 128:c * 128, :], ot[:])
    if BUILD_ONLY:
        return nc
    P.emit()
    return nc


BUILD_ONLY = False
MIXER_BUILDERS = {}


def _fmcol(v):
    v = np.asarray(v, np.float32).reshape(-1, 128)
    return np.ascontiguousarray(v.T)


def prep_shared(inp, L, TP):
    f32 = np.float32
    sh = {}
    sh["w_in"] = np.ascontiguousarray(inp["w_in"][:L], f32)
    qk = inp["w_in"][:L, :, OFF_C:OFF_C + 1024].reshape(L, D, 16, 2, 32)
    sh["w_qksw"] = np.ascontiguousarray(qk[:, :, :, ::-1, :].reshape(L, D, 1024), f32)
    sh["w_branch"] = np.ascontiguousarray(inp["w_branch"][:L], f32)
    sh["w_out"] = np.ascontiguousarray(inp["w_out"][:L], f32)
    sh["w_ffn_in"] = np.ascontiguousarray(inp["w_ffn_in"][:L], f32)
    sh["w_ffn_out"] = np.ascontiguousarray(inp["w_ffn_out"][:L], f32)
    cols = []
    for l in range(L):
        vec = inp["rwkv_vec"][l]
        v0 = inp["rwkv_v0"][l - 1] if l > 0 else np.zeros(D, f32)
        parts = [_fmcol(inp["norm_mix"][l]), _fmcol(inp["norm_ffn"][l]), _fmcol(inp["rwkv_mu"][l]),
                 _fmcol(vec[0]), _fmcol(vec[1]), _fmcol(vec[2]), _fmcol(vec[3]),
                 _fmcol(inp["rwkv_rk"][l].reshape(-1)), _fmcol(v0)]
        scw = inp["ssm_conv_w"][l]
        parts.append(np.ascontiguousarray(scw.reshape(4, 16, 128).transpose(2, 1, 0)).reshape(128, 64))
        parts.append(_fmcol(inp["ssm_conv_b"][l]))
        lcw = inp["lru_conv_w"][l]
        parts.append(np.ascontiguousarray(lcw.reshape(4, 8, 128).transpose(2, 1, 0)).reshape(128, 32))
        parts.append(_fmcol(inp["lru_conv_b"][l]))
        parts.append(_fmcol(inp["lru_b_gate"][l].reshape(-1)))
        parts.append(_fmcol(inp["lru_lambda"][l]))
        parts.append(_fmcol(vec[4]))
        parts.append(_fmcol(vec[5]))
        cols.append(np.concatenate(parts, axis=1))
    cols.append(_fmcol(inp["final_norm"]))
    sh["fmpack"] = np.ascontiguousarray(np.concatenate(cols, axis=1), f32)
    assert sh["fmpack"].shape[1] == L * NFM + 8
    rows = []
    for l in range(L):
        vec = inp["rwkv_vec"][l]
        rows.append(np.concatenate([vec[4], vec[5], inp["ssm_norm"][l], inp["ssm_dt_bias"][l],
                                    inp["ssm_a_log"][l], inp["ssm_d"][l]]))
    sh["rowpack"] = np.ascontiguousarray(np.stack(rows), f32)
    sh["lru_wg"] = np.ascontiguousarray(inp["lru_w_gate"][:L], f32)
    i = np.arange(128)
    ident = np.eye(128, dtype=f32)
    triu = (i[:, None] <= i[None, :]).astype(f32)
    mstr = (i[:, None] > i[None, :]).astype(f32)
    spare = np.zeros((128, 128), f32)
    spare[:, 0] = (i >= PAD)
    sut = (i[:, None] < i[None, :]).astype(f32)
    blk = ((i[:, None] // 64) == (i[None, :] // 64)).astype(f32)
    sh["consts"] = np.ascontiguousarray(np.concatenate([ident, triu, mstr, spare, sut, blk, sut, triu, sut, triu], axis=1))
    pos = (np.arange(TP) - PAD).astype(f32)
    half = 32
    freq = np.power(np.float32(10000.0), -np.arange(half, dtype=f32) / half).astype(f32)
    ang = pos[None, :] * freq[:, None]
    cosf = np.cos(ang).astype(f32)
    sinf = np.sin(ang).astype(f32)
    ch = np.arange(128) % 64
    cos_t = cosf[ch % 32]
    sin_t = np.where((ch < 32)[:, None], -sinf[ch % 32], sinf[ch % 32])
    sh["ropetab"] = np.ascontiguousarray(np.stack([cos_t, sin_t], axis=1), f32)
    H = 8
    log_g = np.log1p(-np.exp2(-5.0 - np.arange(H, dtype=f32))).astype(f32)
    idx = np.arange(128, dtype=f32)
    rel = idx[None, :] - idx[:, None]
    intraT = np.where(rel >= 0, np.exp(np.where(rel >= 0, rel, 0.0)[None] * log_g[:, None, None]), 0.0)
    qdec = np.exp((idx + 1.0)[None, :] * log_g[:, None])
    kdec = np.exp((127.0 - idx)[None, :] * log_g[:, None])
    cdec = np.exp(128.0 * log_g)
    rt = np.zeros((128, 1548), f32)
    rt[:, 0:1024] = intraT.transpose(1, 0, 2).reshape(128, 1024)
    pp_ = np.arange(128) // 64
    for j in range(4):
        rt[:, 1024 + j * 128:1024 + (j + 1) * 128] = qdec[2 * j + pp_, :]
        rt[:, 1544 + j] = cdec[2 * j + pp_]
    rt[:, 1536:1544] = kdec.T
    sh["rettab"] = rt
    sh["rw_w2"] = np.ascontiguousarray(inp["rwkv_w2"][:L], f32)
    sh["rw_a2"] = np.ascontiguousarray(inp["rwkv_a2"][:L], f32)
    sh["rw_g2"] = np.ascontiguousarray(inp["rwkv_g2"][:L], f32)
    n1 = max(L - 1, 1)
    sh["rw_v1"] = np.ascontiguousarray(inp["rwkv_v1"][:n1], f32)
    sh["rw_v2"] = np.ascontiguousarray(inp["rwkv_v2"][:n1], f32)
    return sh


def run_module(inp, depth, nch_group, enable=("a", "b", "c", "d"), ncores=None, arena_kb=82):
    x = np.asarray(inp["x"], np.float32)
    B, S, _ = x.shape
    T = S + NMETA
    assert (T + PAD) % 128 == 0
    TP = T + PAD
    nchunks = TP // 128
    nc = build(nchunks, depth, nch_group, enable, arena_kb)
    sh = prep_shared(inp, depth, TP)
    meta = np.asarray(inp["meta"], np.float32)
    ncores = ncores or B
    in_maps = []
    for c in range(ncores):
        b = c % B
        xin = np.concatenate([np.zeros((PAD, D), np.float32), meta, x[b]], axis=0)
        m = dict(sh)
        m["xin"] = np.ascontiguousarray(xin)
        in_maps.append(m)
    res = run_bass_kernel_spmd(nc, in_maps, core_ids=list(range(ncores)))
    outs = [res.results[b]["out"] for b in range(B)]
    return np.stack(outs, axis=0)


def kernel(**inputs):
    return run_module(inputs, 4, NCH_GROUP, ncores=4, arena_kb=82)


NCH_GROUP = 2

RT_INTRA, RT_Q, RT_KDEC, RT_C, RT_N = 0, 1024, 1536, 1544, 1548


def build_ret(k):
    P, L, GMAX, NCHG = k.P, k.L, k.GMAX, k.nch_group
    hn, ys, w_in, w_qksw = k.hn, k.ys, k.w_in, k.w_qksw
    A = k.ar_alloc
    rt = A("rt", [128, RT_N], F32)
    rope_c = P.sb("rope_c", [128, GMAX], F32)
    rope_s = P.sb("rope_s", [128, GMAX], F32)
    pin = A("ret_pin", [128, 16, GMAX], F32, parts=16)
    qrT = A("qrT", [128, 8, GMAX], BF16, parts=8)
    qdT = A("qdT", [128, 4, GMAX], BF16, parts=4)
    vtm = A("ret_vtm", [128, NCHG, 1024], BF16, parts=NCHG)
    gtm = A("ret_gtm", [128, NCHG, 1024], F32, parts=NCHG)
    ktm = A("ret_ktm", [128, NCHG, 512], BF16, parts=NCHG)
    scm = [A(f"ret_scm{i}", [128, 8, 128], BF16) for i in range(2)]
    ysb = [A(f"ret_ysb{i}", [128, 8, 128], F32) for i in range(2)]
    ysq = A("ret_ysq", [128, 8, 128], F32)
    ssq = P.sb("ret_ssq", [128, 8], F32)
    Rst = [P.sb(f"ret_R{l}", [128, 4, 128], F32) for l in range(L)]
    Rb = P.sb("ret_Rb", [128, 4, 128], BF16)
    for l in range(L):
        P.memset(Rst[l][:], 0.0)

    def mix(l, g, G):
        c0 = g * NCHG
        nchk = G // 128
        P.dma(rt[:], k.rettab[:])
        if l == 0:
            P.dma(rope_c[:, 0:G], k.ropetab[:, 0, c0 * 128:c0 * 128 + G])
            P.dma(rope_s[:, 0:G], k.ropetab[:, 1, c0 * 128:c0 * 128 + G])

        def ev1(j, pp):
            P.copy(pin[:, j, 0:G], pp, eng=k.ev_eng())

        def ev2(j, pp):
            P.copy(pin[:, 8 + j, 0:G], pp, eng=k.ev_eng())
        k.proj_fm(hn, 8, lambda c, nb: k.wload(k.wview(k.wb_in[l], (), OFF_C + c, nb)), 1024, G, ev1)
        k.proj_fm(hn, 8, lambda c, nb: k.wload(k.wview(k.wb_qksw[l], (), c, nb)), 1024, G, ev2)
        for j in range(8):
            cc, ss = rope_c, rope_s
            t1 = k.tmp()
            t2 = k.tmp()
            P.tt(t1[:, 0:G], pin[:, j, 0:G], cc[:, 0:G], ALU.mult)
            P.tt(t2[:, 0:G], pin[:, 8 + j, 0:G], ss[:, 0:G], ALU.mult, eng="pool")
            if j < 4:
                P.tt(t1[:, 0:G], t1[:, 0:G], t2[:, 0:G], ALU.add)
                P.copy(qrT[:, j, 0:G], t1[:, 0:G], eng="act")
                for c in range(nchk):
                    P.tt(qdT[:, j, c * 128:(c + 1) * 128], t1[:, c * 128:(c + 1) * 128],
                         rt[:, RT_Q + j * 128:RT_Q + (j + 1) * 128], ALU.mult)
            else:
                P.tt(t1[:, 0:G], t1[:, 0:G], t2[:, 0:G], ALU.add)
                P.act(qrT[:, j, 0:G], t1[:, 0:G], AF.Copy, scale=0.125)

        def ev3(c, b, pp, nb):
            if b < 2:
                P.copy(vtm[:, c, b * 512:(b + 1) * 512], pp, eng=k.ev_eng())
            else:
                P.act(gtm[:, c, (b - 2) * 512:(b - 1) * 512], pp, AF.Silu)
        k.proj_tm(hn, 8, lambda c, nb: k.wload(k.wview(k.wb_in[l], (), OFF_C + 1024 + c, nb)), 2048, G, ev3)

        for c in range(nchk):
            cs = slice(c * 128, (c + 1) * 128)
            for j in range(4):
                pt = k.psT
                P.tr(pt[:, j * 128:(j + 1) * 128], qrT[:, 4 + j, cs], k.identb)
            for h in range(8):
                P.ts(ktm[:, c, h * 64:(h + 1) * 64], k.psT[:, h * 64:(h + 1) * 64],
                     rt[:, RT_KDEC + h:RT_KDEC + h + 1], ALU.mult, eng="dve")
            sc = scm[c % 2]
            for hb in range(2):
                pp = k.nextB()
                for hh in range(4):
                    h = hb * 4 + hh
                    j, pb = h // 2, 64 * (h % 2)
                    P.mm(pp[:, hh * 128:(hh + 1) * 128], qrT[pb:pb + 64, 4 + j, cs], qrT[pb:pb + 64, j, cs])
                P.tt(sc[:, hb * 4:(hb + 1) * 4, :], pp[:, :].re(lambda a: a.rearrange("p (h l) -> p h l", h=4)),
                     rt[:, RT_INTRA + hb * 512:RT_INTRA + (hb + 1) * 512].re(
                         lambda a: a.rearrange("p (h l) -> p h l", h=4)), ALU.mult)
            P.copy(Rb[:], Rst[l][:], eng="act")
            yb = ysb[c % 2]
            for hb in range(2):
                pp = k.nextB()
                for hh in range(4):
                    h = hb * 4 + hh
                    j, pb = h // 2, 64 * (h % 2)
                    P.mm(pp[:, hh * 128:(hh + 1) * 128], sc[:, h, :], vtm[:, c, h * 128:(h + 1) * 128],
                         start=True, stop=False)
                    P.mm(pp[:, hh * 128:(hh + 1) * 128], qdT[pb:pb + 64, j, cs], Rb[pb:pb + 64, j, :],
                         start=False, stop=True)
                P.copy(yb[:, hb * 4:(hb + 1) * 4, :], pp[:, :].re(lambda a: a.rearrange("p (h l) -> p h l", h=4)),
                       eng="act")
            for j in range(4):
                pp = k.nextB()
                for hp in range(2):
                    h = 2 * j + hp
                    P.mm(pp[:, hp * 128:(hp + 1) * 128], ktm[:, c, j * 128:(j + 1) * 128],
                         vtm[:, c, h * 128:(h + 1) * 128])
                for hp in range(2):
                    pb = 64 * hp
                    P.stt(Rst[l][pb:pb + 64, j, :], Rst[l][pb:pb + 64, j, :], rt[pb:pb + 64, RT_C + j:RT_C + j + 1],
                          pp[pb:pb + 64, hp * 128:(hp + 1) * 128], ALU.mult, ALU.add)
            P.tt(ysq[:], yb[:], yb[:], ALU.mult, eng="pool")
            P.red(ssq[:], ysq[:])
            P.ts(ssq[:], ssq[:], 1.0 / 128, ALU.mult, 1e-6, ALU.add)
            P.act(ssq[:], ssq[:], AF.Ln)
            P.act(ssq[:], ssq[:], AF.Exp, scale=-0.5)
            P.tt(yb[:], yb[:], ssq[:].re(lambda a: a.unsqueeze(2).broadcast_to([128, 8, 128])), ALU.mult)
            P.tt(yb[:], yb[:], gtm[:, c, :].re(lambda a: a.rearrange("p (h e) -> p h e", h=8)), ALU.mult)
            for kt in range(8):
                pp = k.nextB()
                P.tr(pp[:, 0:128], yb[:, kt, :], k.ident)
                P.copy(ys[2][:, kt, cs], pp[:, 0:128], eng=k.ev_eng())
    return mix


MIXER_BUILDERS["c"] = build_ret

def build_ssd(k):
    P, L, GMAX, NCHG = k.P, k.L, k.GMAX, k.nch_group
    hn, ys, w_in = k.hn, k.ys, k.w_in
    A = k.ar_alloc
    triU = k.cst[:, 128:256]
    mstr = k.cst[:, 256:384]
    xe = A("ssd_xe", [128, 16, GMAX + 4], F32, parts=16)
    ztm = A("ssd_ztm", [128, NCHG, 1024], F32, parts=NCHG)
    dtm = A("ssd_dtm", [128, NCHG, 16], F32, parts=NCHG)
    rowb = A("ssd_rowb", [128, 1072], F32)
    xsf = A("ssd_xsf", [128, 8, GMAX], F32, parts=8)
    bfm = A("ssd_bfm", [128, 4, GMAX], BF16, parts=4)
    cfm = A("ssd_cfm", [128, 4, GMAX], BF16, parts=4)
    btm = A("ssd_btm", [128, 4, 128], BF16)
    xs_tm = A("ssd_xstm", [128, 16, 64], F32)
    xdt = A("ssd_xdt", [128, 16, 64], BF16)
    xdt2 = A("ssd_xdt2", [128, 16, 64], BF16)
    rhsL = A("ssd_rhsL", [128, 16, 128], F32)
    Lm = A("ssd_Lm", [128, 16, 128], F32)
    Wm = A("ssd_Wm", [128, 16, 128], BF16)
    cbm = A("ssd_cbm", [128, 4, 128], F32)
    yt = A("ssd_yt", [128, 16, 64], F32)
    yt2 = A("ssd_yt2", [128, 16, 64], F32)
    sm = P.sb("ssd_small", [128, 8, 16], F32)
    ssq = P.sb("ssd_ssq", [128, 4], F32)
    St = [P.sb(f"ssd_S{l}", [128, 16, 64], F32) for l in range(L)]
    Sb = P.sb("ssd_Sb", [128, 16, 64], BF16)
    tails = [P.sb(f"ssd_tail{l}", [128, 16, 3], F32) for l in range(L)]
    for l in range(L):
        P.memset(St[l][:], 0.0)
        P.memset(tails[l][:], 0.0)

    def bc(v, n):
        return v.re(lambda a: a.unsqueeze(2).broadcast_to([128, a.shape[1], n]))

    def mix(l, g, G):
        nchk = G // 128
        k.rowload(rowb[:], l, R_SSMN, 1072)
        nw_b = rowb[:, 0:1024]
        dtb_b = rowb[:, 1024:1040]
        alog_b = rowb[:, 1040:1056]
        dsk_b = rowb[:, 1056:1072]
        P.act(sm[:, 6, :], alog_b, AF.Exp)
        P.ts(sm[:, 6, :], sm[:, 6, :], -1.0, ALU.mult)

        def ev_z(c, b, pp, nb):
            P.act(ztm[:, c, b * 512:(b + 1) * 512], pp, AF.Silu)
        k.proj_tm(hn, 8, lambda c, nb: k.wload(k.wview(k.wb_in[l], (), OFF_B + c, nb)), 1024, G, ev_z)

        def ev_x(j, pp):
            P.copy(xe[:, j, 3:3 + G], pp, eng=k.ev_eng())
        k.proj_fm(hn, 8, lambda c, nb: k.wload(k.wview(k.wb_in[l], (), OFF_B + 1024 + c, nb)), 2048, G, ev_x)

        def ev_dt(c, b, pp, nb):
            P.tt(dtm[:, c, :], pp, dtb_b, ALU.add)
        k.proj_tm(hn, 8, lambda c, nb: k.wload(k.wview(k.wb_in[l], (), OFF_B + 3072 + c, nb)), 16, G, ev_dt)
        for j in range(16):
            P.copy(xe[:, j, 0:3], tails[l][:, j, :])
            xc = k.tmp()
            cw = C_SCW + j * 4
            P.ts(xc[:, 0:G], xe[:, j, 0:G], k.fmc(l, cw), ALU.mult, k.fmc(l, C_SCB + j), ALU.add)
            for t in range(1, 4):
                P.stt(xc[:, 0:G], xe[:, j, t:t + G], k.fmc(l, cw + t), xc[:, 0:G], ALU.mult, ALU.add)
            P.copy(tails[l][:, j, :], xe[:, j, G:G + 3])
            if j < 8:
                P.act(xsf[:, j, 0:G], xc[:, 0:G], AF.Silu)
            elif j < 12:
                P.act(bfm[:, j - 8, 0:G], xc[:, 0:G], AF.Silu)
            else:
                P.act(cfm[:, j - 12, 0:G], xc[:, 0:G], AF.Silu)

        for c in range(nchk):
            cs = slice(c * 128, (c + 1) * 128)
            dt = sm[:, 0, :]
            dA = sm[:, 1, :]
            ea = sm[:, 2, :]
            dec = sm[:, 3, :]
            etot = sm[:, 4, :]
            dtdec = sm[:, 5, :]
            P.act(dt, dtm[:, c, :], AF.Exp)
            P.act(dt, dt, AF.Ln, bias=k.eps_t[:, 1:2])
            if g == 0 and c == 0:
                P.ts(dt, dt, k.cst[:, 384:385], ALU.mult)
            P.tt(dA, dt, sm[:, 6, :], ALU.mult)
            pc = k.nextB()
            P.mm(pc[:, 0:16], triU, dA)
            P.mm(pc[:, 16:32], k.onesf_t[:], dA)
            P.act(ea, pc[:, 0:16], AF.Exp)
            P.copy(sm[:, 7, :], pc[:, 0:16])
            P.tt(dec, pc[:, 16:32], sm[:, 7, :], ALU.subtract)
            P.act(dec, dec, AF.Exp)
            P.act(etot, pc[:, 16:32], AF.Exp)
            P.tt(dtdec, dt, dec, ALU.mult)
            for hb in range(2):
                pp = k.nextA()
                for jj in range(4):
                    P.tr(pp[:, jj * 128:(jj + 1) * 128], xsf[:, hb * 4 + jj, cs], k.ident)
                P.copy(xs_tm[:, hb * 8:(hb + 1) * 8, :], pp[:, :].re(lambda a: a.rearrange("p (h e) -> p h e", h=8)),
                       eng="act")
            P.tt(xdt[:], xs_tm[:], bc(dt, 64), ALU.mult)
            P.tt(xdt2[:], xs_tm[:], bc(dtdec, 64), ALU.mult, eng="pool")
            for gi in range(4):
                P.tr(k.psT[:, gi * 128:(gi + 1) * 128], bfm[:, gi, cs], k.identb)
            P.copy(btm[:], k.psT[:, 0:512].re(lambda a: a.rearrange("p (g n) -> p g n", g=4)), eng="act")
            P.tt(rhsL[:], triU.re(lambda a: a.unsqueeze(1).broadcast_to([128, 16, 128])), bc(dA, 128), ALU.mult)
            pcb = k.nextB()
            for gi in range(4):
                P.mm(pcb[:, gi * 128:(gi + 1) * 128], bfm[:, gi, cs], cfm[:, gi, cs])
            P.tt(cbm[:], pcb[:, :].re(lambda a: a.rearrange("p (g l) -> p g l", g=4)),
                 triU.re(lambda a: a.unsqueeze(1).broadcast_to([128, 4, 128])), ALU.mult)
            for gi in range(4):
                pl = k.nextA()
                P.mm(pl[:, :], mstr, rhsL[:, gi * 4:(gi + 1) * 4, :].re(lambda a: a.rearrange("p h l -> p (h l)")))
                P.act(Lm[:, gi * 4:(gi + 1) * 4, :], pl[:, :].re(lambda a: a.rearrange("p (h l) -> p h l", h=4)), AF.Exp)
                P.tt(Wm[:, gi * 4:(gi + 1) * 4, :], Lm[:, gi * 4:(gi + 1) * 4, :],
                     cbm[:, gi, :].re(lambda a: a.unsqueeze(1).broadcast_to([128, 4, 128])), ALU.mult)
            P.copy(Sb[:], St[l][:], eng="act")
            for hb in range(2):
                py = k.nextA()
                for hh in range(8):
                    h = hb * 8 + hh
                    P.mm(py[:, hh * 64:(hh + 1) * 64], Wm[:, h, :], xdt[:, h, :])
                po = k.nextB()
                for gg in range(2):
                    gi = hb * 2 + gg
                    P.mm(po[:, gg * 256:(gg + 1) * 256], cfm[:, gi, cs],
                         Sb[:, gi * 4:(gi + 1) * 4, :].re(lambda a: a.rearrange("p h e -> p (h e)")))
                hs = slice(hb * 8, (hb + 1) * 8)
                P.tt(yt[:, hs, :], po[:, :].re(lambda a: a.rearrange("p (h e) -> p h e", h=8)),
                     bc(sm[:, 2, hb * 8:(hb + 1) * 8], 64), ALU.mult)
                P.tt(yt[:, hs, :], yt[:, hs, :], py[:, :].re(lambda a: a.rearrange("p (h e) -> p h e", h=8)), ALU.add)
            P.tt(yt2[:], xs_tm[:], bc(dsk_b, 64), ALU.mult, eng="pool")
            P.tt(yt[:], yt[:], yt2[:], ALU.add)
            P.tt(yt[:], yt[:], ztm[:, c, :].re(lambda a: a.rearrange("p (h e) -> p h e", h=16)), ALU.mult)
            for hb in range(2):
                pst = k.nextB()
                for gg in range(2):
                    gi = hb * 2 + gg
                    P.mm(pst[:, gg * 256:(gg + 1) * 256], btm[:, gi, :],
                         xdt2[:, gi * 4:(gi + 1) * 4, :].re(lambda a: a.rearrange("p h e -> p (h e)")))
                hs = slice(hb * 8, (hb + 1) * 8)
                P.tt(St[l][:, hs, :], St[l][:, hs, :], bc(sm[:, 4, hb * 8:(hb + 1) * 8], 64), ALU.mult)
                P.tt(St[l][:, hs, :], St[l][:, hs, :],
                     pst[:, :].re(lambda a: a.rearrange("p (h e) -> p h e", h=8)), ALU.add)
            P.tt(yt2[:], yt[:], yt[:], ALU.mult, eng="pool")
            P.red(ssq[:], yt2[:].re(lambda a: a.rearrange("p (g e) n -> p g (e n)", g=4)))
            P.ts(ssq[:], ssq[:], 1.0 / 256, ALU.mult, 1e-5, ALU.add)
            P.act(ssq[:], ssq[:], AF.Ln)
            P.act(ssq[:], ssq[:], AF.Exp, scale=-0.5)
            ytg = yt[:].re(lambda a: a.rearrange("p (g e) n -> p g (e n)", g=4))
            P.tt(ytg, ytg, bc(ssq[:], 256), ALU.mult)
            ytf = yt[:].re(lambda a: a.rearrange("p h e -> p (h e)"))
            P.tt(ytf, ytf, nw_b, ALU.mult)
            for kt in range(8):
                pp = k.nextA()
                P.tr(pp[:, 0:128], ytf[:, kt * 128:(kt + 1) * 128], k.ident)
                P.copy(ys[1][:, kt, cs], pp[:, 0:128], eng=k.ev_eng())
    return mix


MIXER_BUILDERS["b"] = build_ssd

def build_rwkv(k):
    P, L, GMAX, NCHG = k.P, k.L, k.GMAX, k.nch_group
    hn, ys, w_in = k.hn, k.ys, k.w_in
    A = k.ar_alloc
    cst = k.cst
    ident, identb = k.ident, k.identb
    mstr = cst[:, 256:384]
    blkb_t = P.sb("rw_blkb", [128, 128], BF16)
    P.copy(blkb_t[:], cst[:, 640:768])
    blk = blkb_t[:]
    sqb2 = [A(f"rw_sqb{i}", [128, 2, 128], BF16) for i in range(2)]
    mask4 = cst[:, 768:1280]
    pa = A("rw_pa", [128, 26, GMAX + 1], F32, parts=26)
    w2s = A("rw_w2s", [128, 1024], BF16)
    a2s = A("rw_a2s", [128, 1024], BF16)
    g2s = A("rw_g2s", [128, 1024], BF16)
    v1s = A("rw_v1s", [128, 8, 32], BF16)
    v2s = A("rw_v2s", [128, 1024], BF16)
    twb = A("rw_twb", [128, 128], BF16)
    adb = A("rw_adb", [128, 128], BF16)
    sgb = A("rw_sgb", [128, 128], BF16)
    bt3 = [A(f"rw_bt3{i}", [128, 3, 128], BF16) for i in range(2)]
    gfm = A("rw_gfm", [128, 8, 128], BF16, parts=8)
    bonus = A("rw_bonus", [128, 8, 128], BF16, parts=8)
    off_alias = k.ar["off"]
    vb16 = A("rw_vb16", [128, 8, GMAX], BF16, parts=8)
    vvb = A("rw_vvb", [128, GMAX], BF16)
    k.ar["off"] = off_alias
    ar = A("rw_ar", [128, 8, 256], BF16, parts=8)
    bet = A("rw_bet", [128, 8, 128], BF16, parts=8)
    ktl = A("rw_ktl", [128, 8, 128], BF16, parts=8)
    tm3 = A("rw_tm3", [128, 3, 1024], BF16)
    btm, ktm, vtm = tm3[:, 0, :], tm3[:, 1, :], tm3[:, 2, :]
    AT = [A(f"rw_AT{i}", [128, 4, 512], BF16, parts=4) for i in range(1)] * 2
    P0 = [A(f"rw_P0{i}", [128, 4, 128], BF16) for i in range(1)] * 2
    QP = [A(f"rw_QP{i}", [128, 4, 256], BF16) for i in range(2)]
    Xb = [A(f"rw_Xb{i}", [128, 4, 64], BF16) for i in range(2)]
    Ub = A("rw_Ub", [128, 16, 64], BF16)
    ysb = A("rw_ysb", [128, 16, 64], F32)
    ysq = A("rw_ysq", [128, 16, 64], F32)
    elast = P.sb("rw_elast", [128, 8], F32)
    st4 = P.sb("rw_st4", [128, 4, 16], F32)
    nw0 = P.sb("rw_nw0", [128, 8], F32)
    cm = P.sb("rw_cm", [128, 2], F32)
    P.memset(cm[:, 0:1], -0.5)
    P.memset(cm[:, 1:2], 1e-24)
    vfirst = P.sb("rw_vfirst", [128, 8, GMAX], F32, parts=8)
    Hst = [P.sb(f"rw_H{l}", [128, 8, 64], F32) for l in range(L)]
    H0b = P.sb("rw_H0b", [128, 8, 64], BF16)
    shtail = [P.sb(f"rw_sht{l}", [128, 26], F32) for l in range(L)]
    for l in range(L):
        P.memset(Hst[l][:], 0.0)
        P.memset(shtail[l][:], 0.0)

    def bc(v, n):
        return v.re(lambda a: a.unsqueeze(2).broadcast_to([128, a.shape[1], n]))

    rtp = [A(f"rw_tmp{i}", [128, 128], F32) for i in range(12)]
    rtc = [0]

    def ctmp():
        rtc[0] += 1
        return rtp[rtc[0] % 12]

    def mix(l, g, G):
        nchk = G // 128
        fmc = k.fmc

        def ev(j, pp):
            P.copy(pa[:, j, 1:1 + G], pp, eng=k.ev_eng())
        k.proj_fm(hn, 8, lambda c, nb: k.wload(k.wview(k.wb_in[l], (), OFF_A + c, nb)), 3328, G, ev)
        P.dma(w2s[0:64, :], View(k.rw_w2, k.rw_w2.h[l], 0, 1), eng="pool")
        P.dma(a2s[64:128, :], View(k.rw_a2, k.rw_a2.h[l], 0, 1), eng="pool")
        P.dma(g2s[:, :], View(k.rw_g2, k.rw_g2.h[l], 0, 1), eng="pool")
        if l > 0:
            P.dma(v1s[:], View(k.rw_v1, k.rw_v1.h[l - 1].rearrange("(kt p) r -> p kt r", p=128), 0, 1), eng="pool")
            P.dma(v2s[0:32, :], View(k.rw_v2, k.rw_v2.h[l - 1], 0, 1), eng="pool")
        P.ts(nw0[:], fmc(l, C_W0, 8), -1.0, ALU.mult)
        P.copy(pa[:, :, 0], shtail[l][:])
        P.copy(shtail[l][:], pa[:, :, G])
        for j in range(26):
            d = k.tmp()
            P.tt(d[:, 0:G], pa[:, j, 0:G], pa[:, j, 1:G + 1], ALU.subtract)
            P.stt(pa[:, j, 1:G + 1], d[:, 0:G], fmc(l, C_MU + j), pa[:, j, 1:G + 1], ALU.mult, ALU.add)
        if l == 0:
            for j in range(8):
                P.copy(vfirst[:, j, 0:G], pa[:, 16 + j, 1:G + 1], eng=k.ev_eng())
        else:
            for j in range(8):
                P.copy(vb16[:, j, 0:G], pa[:, 16 + j, 1:G + 1], eng=k.ev_eng())
            pv = k.nextA()
            for kt in range(8):
                P.mm(pv[0:32, 0:G], v1s[:, kt, :], vb16[:, kt, 0:G], start=(kt == 0), stop=(kt == 7))
            P.copy(vvb[0:32, 0:G], pv[0:32, 0:G])
            for j in range(8):
                pp = k.nextA()
                P.mm(pp[:, 0:G], v2s[0:32, j * 128:(j + 1) * 128], vvb[0:32, 0:G])
                sg = k.tmp()
                P.act(sg[:, 0:G], pp[:, 0:G], AF.Sigmoid, bias=fmc(l, C_V0 + j))
                d = k.tmp()
                vcur = pa[:, 16 + j, 1:G + 1]
                P.tt(d[:, 0:G], vfirst[:, j, 0:G], vcur, ALU.subtract)
                P.tt(d[:, 0:G], d[:, 0:G], sg[:, 0:G], ALU.mult)
                P.tt(vcur, vcur, d[:, 0:G], ALU.add)

        P.fence()
        for c in range(nchk):
            cs = slice(c * 128, (c + 1) * 128)
            c1 = slice(1 + c * 128, 1 + (c + 1) * 128)
            P.act(twb[0:64, :], pa[0:64, 24, c1], AF.Tanh)
            P.copy(adb[64:128, :], pa[64:128, 24, c1])
            P.act(sgb[:, :], pa[:, 25, c1], AF.Sigmoid)
            for j in range(8):
                js = slice(j * 128, (j + 1) * 128)
                ed_j, av_j, kk_j = ctmp()[:, 0:128], ctmp()[:, 0:128], ctmp()[:, 0:128]
                b3 = bt3[j % 2]
                r_, k_, v_ = pa[:, j, c1], pa[:, 8 + j, c1], pa[:, 16 + j, c1]
                pw = k.nextA()
                P.mm(pw[:, 0:128], w2s[0:64, js], twb[0:64, :])
                P.mm(pw[:, 128:256], a2s[64:128, js], adb[64:128, :])
                P.mm(pw[:, 256:384], g2s[:, js], sgb[:, :])
                t1 = ctmp()
                P.act(t1[:, 0:128], pw[:, 0:128], AF.Exp, scale=-1.0, bias=nw0[:, j:j + 1])
                P.act(t1[:, 0:128], t1[:, 0:128], AF.Ln, bias=k.eps_t[:, 1:2])
                P.act(ed_j, t1[:, 0:128], AF.Exp, scale=-1.0, bias=cm[:, 0:1])
                P.act(av_j, pw[:, 128:256], AF.Sigmoid, bias=fmc(l, C_A0 + j))
                P.copy(gfm[:, j, :], pw[:, 256:384])
                kq = ctmp()
                P.ts(kq[:, 0:128], k_, fmc(l, C_KK + j), ALU.mult)
                sq2 = sqb2[j % 2]
                P.tt(sq2[:, 0, :], kq[:, 0:128], kq[:, 0:128], ALU.mult, eng="pool")
                pn = k.nextB()
                P.mm(pn[:, 0:128], blk, sq2[:, 0, :])
                rn = ctmp()
                P.act(rn[:, 0:128], pn[:, 0:128], AF.Ln, bias=cm[:, 1:2])
                P.act(rn[:, 0:128], rn[:, 0:128], AF.Exp, scale=-0.5)
                P.tt(kk_j, kq[:, 0:128], rn[:, 0:128], ALU.mult)
                t2 = ctmp()
                P.ts(t2[:, 0:128], av_j, -1.0, ALU.add, fmc(l, C_KA + j), ALU.mult)
                P.stt(k_, t2[:, 0:128], 1.0, k_, ALU.add, ALU.mult)
                P.stt(sq2[:, 1, :], r_, fmc(l, C_RK + j), k_, ALU.mult, ALU.mult)
                P.mm(pn[:, 128:256], blk, sq2[:, 1, :])
                P.tt(bonus[:, j, :], pn[:, 128:256], v_, ALU.mult)
                P.copy(b3[:, 2, :], v_, eng="act")
                cw = ctmp()
                P.scan(cw[:, 0:128], k.onesf_t[:, 0:128], ed_j, 0.0)
                e_in = ctmp()
                P.act(e_in[:, 0:128], cw[:, 0:128], AF.Exp, scale=-1.0)
                P.copy(elast[:, j:j + 1], e_in[:, 127:128])
                e_inv = ctmp()
                P.act(e_inv[:, 0:128], cw[:, 0:128], AF.Exp)
                e_ex = ctmp()
                P.tt(e_ex[:, 0:128], cw[:, 0:128], ed_j, ALU.subtract)
                P.act(e_ex[:, 0:128], e_ex[:, 0:128], AF.Exp, scale=-1.0)
                P.stt(ar[:, j, 0:128], kk_j, -1.0, e_ex[:, 0:128], ALU.mult, ALU.mult)
                P.tt(ar[:, j, 128:256], r_, e_in[:, 0:128], ALU.mult)
                P.tt(t2[:, 0:128], kk_j, av_j, ALU.mult, eng="pool")
                P.tt(bet[:, j, :], t2[:, 0:128], e_inv[:, 0:128], ALU.mult)
                P.tt(ktl[:, j, :], k_, e_inv[:, 0:128], ALU.mult)
                P.ts(b3[:, 0, :], bet[:, j, :], elast[:, j:j + 1], ALU.mult)
                P.ts(b3[:, 1, :], ktl[:, j, :], elast[:, j:j + 1], ALU.mult)
                for q in range(3):
                    P.tr(k.psT[:, q * 128:(q + 1) * 128], b3[:, q, :], identb)
                P.copy(tm3[:, :, js], k.psT[:, 0:384].re(lambda a: a.rearrange("p (q c) -> p q c", q=3)),
                       eng=k.ev_eng())
            P.copy(H0b[:], Hst[l][:])
            for hb in range(4):
                at = AT[hb % 2]
                p0 = P0[hb % 2]
                pp0 = k.nextB()
                for hh in range(4):
                    h = hb * 4 + hh
                    j, pb = h // 2, 64 * (h % 2)
                    b1 = k.nextA()
                    P.mm(b1[:, 0:256], bet[pb:pb + 64, j, :], ar[pb:pb + 64, j, :])
                    P.mm(b1[:, 256:512], ktl[pb:pb + 64, j, :], ar[pb:pb + 64, j, :])
                    P.tt(at[:, hh, :], b1[:, :], mask4, ALU.mult)
                    P.mm(pp0[:, hh * 128:(hh + 1) * 128], ar[pb:pb + 64, j, 0:128], bet[pb:pb + 64, j, :])
                P.tt(p0[:], pp0[:, :].re(lambda a: a.rearrange("p (h s) -> p h s", h=4)),
                     mstr.re(lambda a: a.unsqueeze(1).broadcast_to([128, 4, 128])), ALU.mult)
                px = k.nextB()
                for hh in range(4):
                    h = hb * 4 + hh
                    j, pb = h // 2, 64 * (h % 2)
                    P.mm(px[:, hh * 64:(hh + 1) * 64], ar[pb:pb + 64, j, 0:128], H0b[pb:pb + 64, j, :],
                         start=True, stop=False)
                    P.mm(px[:, hh * 64:(hh + 1) * 64], at[:, hh, 256:384], vtm[:, h * 64:(h + 1) * 64],
                         start=False, stop=True)
                xcur = Xb[0]
                P.copy(xcur[:], px[:, 0:256].re(lambda a: a.rearrange("p (h v) -> p h v", h=4)), eng="act")
                for step in range(7):
                    def Q(hh):
                        return at[:, hh, 0:128] if step == 0 else QP[(step - 1) % 2][:, hh, 0:128]

                    def Pm(hh):
                        return p0[:, hh, :] if step == 0 else QP[(step - 1) % 2][:, hh, 128:256]
                    px = k.nextB()
                    for hh in range(4):
                        P.mm(px[:, hh * 64:(hh + 1) * 64], identb, xcur[:, hh, :], start=True, stop=False)
                        P.mm(px[:, hh * 64:(hh + 1) * 64], Q(hh), xcur[:, hh, :], start=False, stop=True)
                    if step < 6:
                        xn = Xb[(step + 1) % 2]
                        P.copy(xn[:], px[:, 0:256].re(lambda a: a.rearrange("p (h v) -> p h v", h=4)), eng="act")
                        xcur = xn
                        qn = QP[step % 2]
                        for half in range(2):
                            pq = k.nextA()
                            for h2 in range(2):
                                hh = half * 2 + h2
                                P.mm(pq[:, h2 * 256:h2 * 256 + 128], Pm(hh), Q(hh))
                                P.mm(pq[:, h2 * 256 + 128:h2 * 256 + 256], Q(hh), Pm(hh))
                            P.copy(qn[:, half * 2:half * 2 + 2, :],
                                   pq[:, :].re(lambda a: a.rearrange("p (h s) -> p h s", h=2)), eng="dve")
                    else:
                        P.copy(Ub[:, hb * 4:(hb + 1) * 4, :],
                               px[:, 0:256].re(lambda a: a.rearrange("p (h v) -> p h v", h=4)), eng="act")
                py = k.nextB()
                for hh in range(4):
                    h = hb * 4 + hh
                    j, pb = h // 2, 64 * (h % 2)
                    P.mm(py[:, hh * 64:(hh + 1) * 64], ar[pb:pb + 64, j, 128:256], H0b[pb:pb + 64, j, :],
                         start=True, stop=False)
                    P.mm(py[:, hh * 64:(hh + 1) * 64], at[:, hh, 128:256], Ub[:, h, :], start=False, stop=False)
                    P.mm(py[:, hh * 64:(hh + 1) * 64], at[:, hh, 384:512], vtm[:, h * 64:(h + 1) * 64],
                         start=False, stop=True)
                P.copy(ysb[:, hb * 4:(hb + 1) * 4, :], py[:, 0:256].re(lambda a: a.rearrange("p (h v) -> p h v", h=4)),
                       eng="dve")
            for j in range(8):
                ph = k.nextA()
                P.mm(ph[:, 0:128], btm[:, j * 128:(j + 1) * 128],
                     Ub[:, 2 * j:2 * j + 2, :].re(lambda a: a.rearrange("p h v -> p (h v)")), start=True, stop=False)
                P.mm(ph[:, 0:128], ktm[:, j * 128:(j + 1) * 128], vtm[:, j * 128:(j + 1) * 128], start=False, stop=True)
                for hp in range(2):
                    pb = 64 * hp
                    P.stt(Hst[l][pb:pb + 64, j, :], Hst[l][pb:pb + 64, j, :], elast[pb:pb + 64, j:j + 1],
                          ph[pb:pb + 64, hp * 64:(hp + 1) * 64], ALU.mult, ALU.add)
            s1, s2, mean, var = st4[:, 0, :], st4[:, 1, :], st4[:, 2, :], st4[:, 3, :]
            P.red(s1, ysb[:])
            P.tt(ysq[:], ysb[:], ysb[:], ALU.mult, eng="pool")
            P.red(s2, ysq[:])
            P.ts(mean, s1, 1.0 / 64, ALU.mult)
            P.tt(var, mean, mean, ALU.mult)
            P.stt(var, s2, 1.0 / 64, var, ALU.mult, ALU.subtract)
            P.ts(var, var, 64e-5, ALU.add)
            P.act(var, var, AF.Ln)
            P.act(var, var, AF.Exp, scale=-0.5)
            P.tt(ysb[:], ysb[:], bc(mean, 64), ALU.subtract)
            P.tt(ysb[:], ysb[:], bc(var, 64), ALU.mult)
            for kt in range(8):
                pp = k.nextA()
                P.tr(pp[:, 0:128], ysb[:, 2 * kt:2 * kt + 2, :].re(lambda a: a.rearrange("p h v -> p (h v)")), ident)
                t3 = ctmp()
                P.ts(t3[:, 0:128], pp[:, 0:128], fmc(l, C_LNW + kt), ALU.mult, fmc(l, C_LNB + kt), ALU.add)
                P.tt(t3[:, 0:128], t3[:, 0:128], bonus[:, kt, :], ALU.add)
                P.tt(ys[0][:, kt, cs], t3[:, 0:128], gfm[:, kt, :], ALU.mult)
    return mix


MIXER_BUILDERS["a"] = build_rwkv
```

```python
import numpy as np
import concourse.bass as bass
import concourse.mybir as mybir

F32 = mybir.dt.float32
BF16 = mybir.dt.bfloat16
AF = mybir.ActivationFunctionType
ALU = mybir.AluOpType
AX = mybir.AxisListType

COMPUTE = ("pe", "act", "dve", "pool")
ALLENG = ("pe", "act", "dve", "pool", "sp")
EPOCH = 30000
NDMASEM = 24
SAME_ENGINE_SYNC = True


class View:
    __slots__ = ("t", "ap", "p0", "p1")

    def __init__(self, t, ap, p0, p1):
        self.t, self.ap, self.p0, self.p1 = t, ap, p0, p1

    def re(self, fn):
        return View(self.t, fn(self.ap), self.p0, self.p1)

    def __getitem__(self, idx):
        return View(self.t, self.ap[idx], self.p0, self.p1)


class Tile:
    def __init__(self, name, h, nparts=1):
        self.name, self.h, self.nparts = name, h, nparts
        self.psum = False
        self.lastw = [None] * nparts
        self.readers = [[] for _ in range(nparts)]

    def __getitem__(self, idx):
        ap = self.h[idx]
        if self.nparts == 1:
            return View(self, ap, 0, 1)
        i = idx[1] if isinstance(idx, tuple) and len(idx) > 1 else slice(None)
        if isinstance(i, int):
            return View(self, ap, i, i + 1)
        p0, p1, _ = i.indices(self.nparts)
        return View(self, ap, p0, p1)

    def all(self):
        return self[:]


class Op:
    __slots__ = ("eng", "fn", "deps", "dma", "pos", "gid", "sig", "signaler", "waits", "dpos", "rg")


class Prog:
    def __init__(self, nc):
        self.nc = nc
        self.ops = []
        self.eng_ops = {e: [] for e in ALLENG}
        self.dma_ops = {e: [] for e in ALLENG}
        self.ntile = 0
        self.fence_dma = {}
        self.fence_skip = ()

    def sb(self, name, shape, dtype=F32, parts=1):
        self.ntile += 1
        h = self.nc.alloc_sbuf_tensor(f"{name}_{self.ntile}", list(shape), dtype)
        return Tile(name, h, parts)

    def ps(self, name, shape, dtype=F32, parts=1):
        self.ntile += 1
        h = self.nc.alloc_psum_tensor(f"{name}_{self.ntile}", list(shape), dtype)
        t = Tile(name, h, parts)
        t.psum = True
        return t

    def dram(self, name, shape, dtype=F32, kind="Internal", parts=1):
        h = self.nc.dram_tensor(name, list(shape), dtype, kind=kind).ap()
        return Tile(name, h, parts)

    def add(self, eng, fn, reads=(), writes=(), dma=False):
        op = Op()
        op.eng, op.fn, op.dma = eng, fn, dma
        op.gid = len(self.ops)
        op.rg = None
        op.sig = None
        op.signaler = False
        deps = set()
        for v in reads:
            if v is None:
                continue
            t = v.t
            for p in range(v.p0, v.p1):
                if t.lastw[p] is not None:
                    deps.add(t.lastw[p])
                if t.psum:
                    for r in t.readers[p]:
                        if r.eng != eng:
                            deps.add(r)
        for v in writes:
            t = v.t
            for p in range(v.p0, v.p1):
                if t.lastw[p] is not None:
                    deps.add(t.lastw[p])
                for r in t.readers[p]:
                    deps.add(r)
        for v in reads:
            if v is None:
                continue
            t = v.t
            for p in range(v.p0, v.p1):
                t.readers[p].append(op)
        for v in writes:
            t = v.t
            for p in range(v.p0, v.p1):
                t.lastw[p] = op
                t.readers[p] = []
        deps.discard(op)
        if dma:
            lst = self.dma_ops[eng]
            op.dpos = len(lst)
            if op.dpos >= NDMASEM:
                deps.add(lst[op.dpos - NDMASEM])
            lst.append(op)
        op.deps = deps
        op.pos = len(self.eng_ops[eng])
        self.eng_ops[eng].append(op)
        self.ops.append(op)
        return op

    def fence(self):
        lastc = [self.eng_ops[e][-1] for e in ALLENG if self.eng_ops[e] and not self.eng_ops[e][-1].dma]
        lastc = []
        for e in ALLENG:
            for op in reversed(self.eng_ops[e]):
                if not op.dma:
                    lastc.append(op)
                    break
        dmas = []
        for e in ALLENG:
            if e in self.fence_skip:
                continue
            dmas += self.dma_ops[e][self.fence_dma.get(e, 0):]
            self.fence_dma[e] = len(self.dma_ops[e])
        for e in ALLENG:
            if not self.eng_ops[e] or e in self.fence_skip:
                continue
            op = self.add(e, None, [], [])
            op.deps |= set(lastc) | set(dmas)
            op.deps.discard(op)

    def mm(self, out, lhsT, rhs, start=True, stop=True, **kw):
        op = self.add("pe", lambda e: e.matmul(out.ap, lhsT.ap, rhs.ap, start=start, stop=stop, **kw),
                      [lhsT, rhs] + ([] if start else [out]), [out])
        op.rg = (lhsT.ap.base_partition(), lhsT.ap.shape[0], out.ap.base_partition())
        return op

    def tr(self, out, in_, ident):
        op = self.add("pe", lambda e: e.transpose(out.ap, in_.ap, ident.ap), [in_, ident], [out])
        op.rg = (in_.ap.base_partition(), in_.ap.shape[0], out.ap.base_partition())
        return op

    def act(self, out, in_, func, bias=None, scale=None, accum=None, eng="act"):
        def fn(e):
            kw = {}
            if bias is not None:
                kw["bias"] = bias.ap if isinstance(bias, View) else bias
            if scale is not None:
                kw["scale"] = scale.ap if isinstance(scale, View) else scale
            if accum is not None:
                kw["accum_out"] = accum.ap
            return e.activation(out.ap, in_.ap, func, **kw)
        rd = [in_] + [x for x in (bias, scale) if isinstance(x, View)]
        wr = [out] + ([accum] if accum is not None else [])
        return self.add(eng, fn, rd, wr)

    def tt(self, out, a, b, op, eng="dve"):
        return self.add(eng, lambda e: e.tensor_tensor(out.ap, a.ap, b.ap, op), [a, b], [out])

    def ts(self, out, a, s1, op0, s2=None, op1=None, accum=None, eng="dve"):
        def fn(e):
            kw = {}
            if accum is not None:
                kw["accum_out"] = accum.ap
            return e.tensor_scalar(out.ap, a.ap, s1.ap if isinstance(s1, View) else s1,
                                   (s2.ap if isinstance(s2, View) else s2), op0,
                                   op1 if op1 is not None else ALU.bypass, **kw)
        rd = [a] + [x for x in (s1, s2) if isinstance(x, View)]
        wr = [out] + ([accum] if accum is not None else [])
        return self.add(eng, fn, rd, wr)

    def stt(self, out, a, s, b, op0, op1, eng="dve"):
        return self.add(eng, lambda e: e.scalar_tensor_tensor(out.ap, a.ap, s.ap if isinstance(s, View) else s,
                                                              b.ap, op0, op1),
                        [a, b] + ([s] if isinstance(s, View) else []), [out])

    def copy(self, out, in_, eng="dve"):
        if eng == "act":
            return self.add(eng, lambda e: e.copy(out.ap, in_.ap), [in_], [out])
        return self.add(eng, lambda e: e.tensor_copy(out.ap, in_.ap), [in_], [out])

    def memset(self, out, val, eng="dve"):
        return self.add(eng, lambda e: e.memset(out.ap, val), [], [out])

    def red(self, out, in_, op=ALU.add, axis=AX.X, eng="dve"):
        return self.add(eng, lambda e: e.tensor_reduce(out.ap, in_.ap, axis, op), [in_], [out])

    def scan(self, out, d0, d1, init, op0=ALU.mult, op1=ALU.add):
        return self.add("dve", lambda e: e.tensor_tensor_scan(out.ap, d0.ap, d1.ap,
                                                              init.ap if isinstance(init, View) else init, op0, op1),
                        [d0, d1] + ([init] if isinstance(init, View) else []), [out])

    def recip(self, out, in_):
        return self.add("dve", lambda e: e.reciprocal(out.ap, in_.ap), [in_], [out])

    def dma(self, out, in_, eng="act", **kw):
        return self.add(eng, lambda e: e.dma_start(out=out.ap, in_=in_.ap, **kw), [in_], [out], dma=True)

    def emit(self):
        nc = self.nc
        seen = {f: {e: -1 for e in COMPUTE + ("sp",)} for f in ALLENG}
        seen_dma = {f: set() for f in ALLENG}
        for op in self.ops:
            f = op.eng
            best = {}
            dwaits = []
            for d in op.deps:
                if d.dma:
                    if d.gid in seen_dma[f]:
                        continue
                    seen_dma[f].add(d.gid)
                    dwaits.append(d)
                    d.signaler = True
                else:
                    if d.eng == f and f != "pe" and not SAME_ENGINE_SYNC:
                        continue
                    if d.eng == "pe" and f == "pe" and d.rg is not None and d.rg == op.rg:
                        continue
                    if seen[f][d.eng] >= d.pos:
                        continue
                    if d.eng not in best or best[d.eng].pos < d.pos:
                        best[d.eng] = d
            for e, d in best.items():
                seen[f][e] = d.pos
                d.signaler = True
            op.waits = list(best.values()) + dwaits
        nsig = {e: 0 for e in ALLENG}
        for e in ALLENG:
            for op in self.eng_ops[e]:
                if op.dma:
                    op.sig = ("dma", e, op.dpos % NDMASEM, 16 * (op.dpos // NDMASEM + 1))
                elif op.signaler:
                    n = nsig[e]
                    op.sig = ("cmp", e, n // EPOCH, n % EPOCH + 1)
                    nsig[e] = n + 1
        sems = {}
        stack = []

        def getsem(key):
            if key not in sems:
                cm = nc.semaphore("s_%s_%s_%d" % key)
                sems[key] = cm.__enter__()
                stack.append(cm)
            return sems[key]

        for op in self.ops:
            if op.sig is not None:
                getsem(op.sig[:3])
        engobj = {"pe": "tensor", "act": "scalar", "dve": "vector", "pool": "gpsimd", "sp": "sync"}
        self.final_dma = {e: list(self.dma_ops[e]) for e in ALLENG}

        with nc.Block() as block:
            for e in ALLENG:
                ops = self.eng_ops[e]
                if not ops:
                    continue

                def body(eng, ops=ops, e=e):
                    for op in ops:
                        for d in op.waits:
                            eng.wait_ge(sems[d.sig[:3]], d.sig[3])
                        if op.fn is None:
                            if op.sig is None:
                                continue
                            ins = eng.nop()
                        else:
                            ins = op.fn(eng)
                        if op.sig is not None:
                            ins.then_inc(sems[op.sig[:3]], 16 if op.dma else 1)
                    lst = self.dma_ops[e]
                    done = set()
                    for op in reversed(lst):
                        k = op.sig[:3]
                        if k in done:
                            continue
                        done.add(k)
                        eng.wait_ge(sems[k], op.sig[3])

                getattr(block, engobj[e])(body)
        for cm in reversed(stack):
            cm.__exit__(None, None, None)
        return nc

from concourse.bass_utils import run_bass_kernel_spmd

D = 1024
NMETA = 16
PAD = 112
OFF_A, OFF_B, OFF_C, OFF_D, OFF_G, IN_W = 0, 3328, 6416, 9488, 11536, 15632
FFH = 2816
C_NM, C_NF, C_MU, C_W0, C_A0, C_KK, C_KA, C_RK, C_V0 = 0, 8, 16, 42, 50, 58, 66, 74, 82
C_SCW, C_SCB, C_LCW, C_LCB, C_BG, C_LAM, C_LNW, C_LNB, NFM = 90, 154, 170, 202, 210, 226, 234, 242, 250
R_LNW, R_LNB, R_SSMN, R_DTB, R_ALOG, R_SD, NROW = 0, 1024, 2048, 3072, 3088, 3104, 3120


class K:
    pass


def build(nchunks, depth, nch_group, enable=("a", "b", "c", "d"), arena_kb=82):
    nc = bass.Bass("TRN2", target_bir_lowering=False)
    P = Prog(nc)
    P.fence_skip = ("sp",)
    TP = nchunks * 128
    L = depth
    GMAX = nch_group * 128
    k = K()
    xin = P.dram("xin", [TP, D], F32, kind="ExternalInput")
    out = P.dram("out", [TP - 128, D], F32, kind="ExternalOutput")
    w_in = P.dram("w_in", [L, D, IN_W], F32, kind="ExternalInput")
    w_qksw = P.dram("w_qksw", [L, D, 1024], F32, kind="ExternalInput")
    w_branch = P.dram("w_branch", [L, 4, D, D], F32, kind="ExternalInput")
    w_out = P.dram("w_out", [L, D, D], F32, kind="ExternalInput")
    w_ffn_in = P.dram("w_ffn_in", [L, D, 2 * FFH], F32, kind="ExternalInput")
    w_ffn_out = P.dram("w_ffn_out", [L, FFH, D], F32, kind="ExternalInput")
    fmpack = P.dram("fmpack", [128, L * NFM + 8], F32, kind="ExternalInput")
    rowpack = P.dram("rowpack", [L, NROW], F32, kind="ExternalInput")
    lru_wg = P.dram("lru_wg", [L, 2, 8, 128, 128], F32, kind="ExternalInput")
    consts = P.dram("consts", [128, 1280], F32, kind="ExternalInput")
    ropetab = P.dram("ropetab", [128, 2, TP], F32, kind="ExternalInput")
    rettab = P.dram("rettab", [128, 1548], F32, kind="ExternalInput")
    rw_w2 = P.dram("rw_w2", [L, 64, D], F32, kind="ExternalInput")
    rw_a2 = P.dram("rw_a2", [L, 64, D], F32, kind="ExternalInput")
    rw_g2 = P.dram("rw_g2", [L, 128, D], F32, kind="ExternalInput")
    rw_v1 = P.dram("rw_v1", [max(L - 1, 1), D, 32], F32, kind="ExternalInput")
    rw_v2 = P.dram("rw_v2", [max(L - 1, 1), 32, D], F32, kind="ExternalInput")

    def cast_weight(name, src_tile, idx, shape):
        nblk = max(1, shape[0] // 256)
        t = P.dram(name, list(shape), BF16, kind="Internal", parts=nblk)
        rb = shape[0] // nblk
        for i in range(nblk):
            src_ap = src_tile.h[idx + (slice(i * rb, (i + 1) * rb), slice(None))]
            P.dma(View(t, t.h[i * rb:(i + 1) * rb, :], i, i + 1), View(src_tile, src_ap, 0, 1), eng="pool")
        return t

    wb_in, wb_qksw, wb_branch, wb_out, wb_ffi, wb_ffo = [], [], [], [], [], []
    for l in range(L):
        wb_in.append(cast_weight(f"wb_in{l}", w_in, (l,), [D, IN_W]))
        wb_qksw.append(cast_weight(f"wb_qksw{l}", w_qksw, (l,), [D, 1024]))
        wb_branch.append([cast_weight(f"wb_br{l}_{n}", w_branch, (l, n), [D, D]) for n in range(4)])
        wb_out.append(cast_weight(f"wb_out{l}", w_out, (l,), [D, D]))
        wb_ffi.append(cast_weight(f"wb_ffi{l}", w_ffn_in, (l,), [D, 2 * FFH]))
        wb_ffo.append(cast_weight(f"wb_ffo{l}", w_ffn_out, (l,), [FFH, D]))

    cst = P.sb("cst", [128, 1280], F32)
    P.dma(cst[:], consts[:])
    ident = cst[:, 0:128]
    identb_t = P.sb("identb", [128, 128], BF16)
    P.copy(identb_t[:], ident)
    identb = identb_t[:]
    onesb_t = P.sb("onesb", [128, 128], BF16)
    P.memset(onesb_t[:], 1.0)
    onesb = onesb_t[:]
    fm = P.sb("fm", [128, L * NFM + 8], F32)
    P.dma(fm[:], fmpack[:])

    def fmc(l, off, n=1):
        return fm[:, l * NFM + off: l * NFM + off + n]

    psA = [P.ps(f"psA{i}", [128, 512]) for i in range(4)]
    psB = [P.ps(f"psB{i}", [128, 512]) for i in range(3)]
    psT = P.ps("psT", [128, 1024], BF16)
    cnt = {"A": 0, "B": 0, "ev": 0, "w": 0}

    def nextA():
        cnt["A"] += 1
        return psA[cnt["A"] % 4]

    def nextB():
        cnt["B"] += 1
        return psB[cnt["B"] % 3]

    def ev_eng():
        cnt["ev"] += 1
        return "act" if cnt["ev"] % 2 else "dve"

    s = P.sb("s", [128, 8, GMAX], F32, parts=8)
    hn = P.sb("hn", [128, 8, GMAX], BF16, parts=8)
    rstd = P.sb("rstd", [128, GMAX], F32)
    ys = [P.sb(f"ys{n}", [128, 8, GMAX], BF16, parts=8) for n in range(4)]
    mfm = P.sb("mfm", [128, 8, GMAX], BF16, parts=8)
    NW = 3
    wbuf = [P.sb(f"wbuf{i}", [128, 8, 512], BF16) for i in range(NW)]
    arena_t = P.sb("arena", [128, arena_kb * 256], F32)
    ar = {"off": 0}

    def ar_reset():
        ar["off"] = 0

    def ar_alloc(name, shape, dtype=F32, parts=1):
        n = 1
        for d_ in shape[1:]:
            n *= d_
        nf = n if dtype == F32 else (n + 1) // 2
        nf = (nf + 7) // 8 * 8
        off = ar["off"]
        assert off + nf <= arena_kb * 256, (name, off, nf)
        ar["off"] = off + nf
        ap = arena_t.h[:, off:off + nf]
        if dtype != F32:
            ap = ap.bitcast(dtype)[:, 0:n]
        else:
            ap = ap[:, 0:n]
        if len(shape) == 3:
            ap = ap.rearrange("p (a b) -> p a b", a=shape[1])
        return Tile(name, ap, parts)

    def rowload(dst, l, off, n):
        P.dma(dst, View(rowpack, rowpack.h[l:l + 1, off:off + n].partition_broadcast(128), 0, 1))
    tmpf = [P.sb(f"tmpf{i}", [128, GMAX], F32) for i in range(8)]
    tcnt = [0]

    def tmp():
        tcnt[0] += 1
        return tmpf[tcnt[0] % 8]

    def wload(dram_view):
        cnt["w"] += 1
        wt = wbuf[cnt["w"] % NW]
        kt, n = dram_view.ap.shape[1], dram_view.ap.shape[2]
        P.dma(wt[:, 0:kt, 0:n], dram_view, eng="sp")
        return wt

    def wview(t, l_idx, c0, ncols, r0=0, nrows=1024):
        ap = t.h[l_idx + (slice(r0, r0 + nrows), slice(c0, c0 + ncols))]
        ap = ap.rearrange("(kt p) n -> p kt n", p=128)
        return View(t, ap, 0, t.nparts)

    ar_reset()
    sq = ar_alloc("sq", [128, 8, GMAX], F32, parts=8)
    sqb = ar_alloc("sqb", [128, 8, GMAX], BF16, parts=8)
    xtm = [ar_alloc(f"xtm{i}", [128, D], F32) for i in range(2)]
    macc = [ar_alloc(f"macc{i}", [128, GMAX], F32) for i in range(4)]

    def rmsnorm(G, wcol, outt, eps=1e-6):
        for kt in range(8):
            P.act(sqb[:, kt, 0:G], s[:, kt, 0:G], AF.Square)
        pp = nextA()
        for kt in range(8):
            P.mm(pp[:, 0:G], onesb, sqb[:, kt, 0:G], start=(kt == 0), stop=(kt == 7))
        P.act(rstd[:, 0:G], pp[:, 0:G], AF.Ln, scale=1.0 / D, bias=eps_t[:, 0:1])
        P.act(rstd[:, 0:G], rstd[:, 0:G], AF.Exp, scale=-0.5)
        for kt in range(8):
            P.stt(outt[:, kt, 0:G], s[:, kt, 0:G], wcol[:, kt:kt + 1], rstd[:, 0:G], ALU.mult, ALU.mult)

    onesf_t = P.sb("onesf", [128, 128], F32)
    P.memset(onesf_t[:], 1.0)
    onesb_f = onesf_t
    eps_t = P.sb("eps", [128, 2], F32)
    P.memset(eps_t[:, 0:1], 1e-6)
    P.memset(eps_t[:, 1:2], 1.0)

    def proj_fm(src, KT, wt_fn, ncols, G, evac):
        nblk = (ncols + 511) // 512
        for b in range(nblk):
            nb = min(512, ncols - b * 512)
            wt = wt_fn(b * 512, nb)
            for jj in range(nb // 128):
                pp = nextA()
                for kt in range(KT):
                    P.mm(pp[:, 0:G], wt[:, kt, jj * 128:(jj + 1) * 128], src[:, kt, 0:G],
                         start=(kt == 0), stop=(kt == KT - 1))
                evac(b * 4 + jj, pp[:, 0:G])

    def proj_tm(src, KT, wt_fn, ncols, G, evac):
        nblk = (ncols + 511) // 512
        for b in range(nblk):
            nb = min(512, ncols - b * 512)
            wt = wt_fn(b * 512, nb)
            for c in range(G // 128):
                pp = nextA()
                for kt in range(KT):
                    P.mm(pp[:, 0:nb], src[:, kt, c * 128:(c + 1) * 128], wt[:, kt, 0:nb],
                         start=(kt == 0), stop=(kt == KT - 1))
                evac(c, b, pp[:, 0:nb], nb)

    k.__dict__.update(locals())

    lru_tail = [P.sb(f"lru_tail{l}", [128, 8, 3], F32) for l in range(L)]
    lru_h = [P.sb(f"lru_h{l}", [128, 8], F32) for l in range(L)]
    lru_nsp = [P.sb(f"lru_nsp{l}", [128, 8], F32) for l in range(L)]
    ar_reset()
    lru_wgs = ar_alloc("lru_wgs", [128, 16, 128], BF16)
    lru_xcb = ar_alloc("lru_xcb", [128, GMAX], BF16)
    pin = ar_alloc("lru_pin", [128, 16, GMAX + 4], F32, parts=16)
    for l in range(L):
        P.memset(lru_tail[l][:], 0.0)
        P.memset(lru_h[l][:], 0.0)
        P.act(lru_nsp[l][:], fmc(l, C_LAM, 8), AF.Exp, scale=-1.0)
        P.act(lru_nsp[l][:], lru_nsp[l][:], AF.Ln, bias=eps_t[:, 1:2])
        P.ts(lru_nsp[l][:], lru_nsp[l][:], -8.0, ALU.mult)

    def gelu_tanh(outv, xv, G):
        t1 = tmp()
        P.act(t1[:, 0:G], xv, AF.Square)
        P.ts(t1[:, 0:G], t1[:, 0:G], 0.044715, ALU.mult, 1.0, ALU.add)
        P.tt(t1[:, 0:G], t1[:, 0:G], xv, ALU.mult)
        P.act(t1[:, 0:G], t1[:, 0:G], AF.Sigmoid, scale=1.5957691216057308)
        P.tt(outv, t1[:, 0:G], xv, ALU.mult)

    def lru_mix(l, g, G):
        def wt_fn(c0, nb):
            return wload(wview(wb_in[l], (), OFF_D + c0, nb))

        def evac(j, pp):
            if j < 8:
                P.copy(pin[:, j, 0:G], pp, eng=ev_eng())
            else:
                P.copy(pin[:, j, 3:3 + G], pp, eng=ev_eng())
        proj_fm(hn, 8, wt_fn, 2048, G, evac)
        P.dma(lru_wgs[:], View(lru_wg, lru_wg.h[l].rearrange("k n c e -> c (k n) e"), 0, 1), eng="pool")
        for n in range(8):
            xe = pin[:, 8 + n, :]
            P.copy(xe[:, 0:3], lru_tail[l][:, n, :])
            xc = tmp()
            lw = C_LCW + n * 4
            P.ts(xc[:, 0:G], xe[:, 0:G], fmc(l, lw), ALU.mult, fmc(l, C_LCB + n), ALU.add)
            for j in range(1, 4):
                P.stt(xc[:, 0:G], xe[:, j:j + G], fmc(l, lw + j), xc[:, 0:G], ALU.mult, ALU.add)
            P.copy(lru_tail[l][:, n, :], xe[:, G:G + 3])
            xcb16 = lru_xcb
            P.copy(xcb16[:, 0:G], xc[:, 0:G])
            gates = []
            for kk_ in range(2):
                pp = nextB()
                P.mm(pp[:, 0:G], lru_wgs[:, kk_ * 8 + n, :], xcb16[:, 0:G])
                gt = tmp()
                P.act(gt[:, 0:G], pp[:, 0:G], AF.Sigmoid, bias=fmc(l, C_BG + kk_ * 8 + n))
                gates.append(gt)
            rg, ig = gates
            la = tmp()
            P.ts(la[:, 0:G], rg[:, 0:G], lru_nsp[l][:, n:n + 1], ALU.mult)
            av = tmp()
            P.act(av[:, 0:G], la[:, 0:G], AF.Exp)
            P.act(la[:, 0:G], la[:, 0:G], AF.Exp, scale=2.0)
            P.ts(la[:, 0:G], la[:, 0:G], -1.0, ALU.mult, 1.0, ALU.add)
            P.act(la[:, 0:G], la[:, 0:G], AF.Sqrt)
            P.tt(ig[:, 0:G], ig[:, 0:G], xc[:, 0:G], ALU.mult)
            P.tt(ig[:, 0:G], ig[:, 0:G], la[:, 0:G], ALU.mult)
            if g == 0:
                P.memset(ig[:, 0:PAD], 0.0)
            P.scan(rg[:, 0:G], av[:, 0:G], ig[:, 0:G], lru_h[l][:, n:n + 1])
            P.copy(lru_h[l][:, n:n + 1], rg[:, G - 1:G])
            gelu_tanh(la[:, 0:G], pin[:, n, 0:G], G)
            P.tt(ys[3][:, n, 0:G], rg[:, 0:G], la[:, 0:G], ALU.mult)


    def zero_y(n, G):
        for kt in range(8):
            P.memset(ys[n][:, kt, 0:G], 0.0, eng="pool")

    mixers = {"a": None, "b": None, "c": None, "d": lru_mix}
    k.__dict__.update(locals())
    for nm, modfn in MIXER_BUILDERS.items():
        ar_reset()
        mixers[nm] = modfn(k)

    def layer(l, g, G):
        rmsnorm(G, fmc(l, C_NM, 8), hn)
        for n, nm in enumerate("abcd"):
            if nm in enable and mixers[nm] is not None:
                P.fence()
                mixers[nm](l, g, G)
            else:
                zero_y(n, G)
        P.fence()
        for db in range(2):
            for n in range(4):
                wb = wload(wview(wb_branch[l][n], (), db * 512, 512))
                wg = wload(wview(wb_in[l], (), OFF_G + n * 1024 + db * 512, 512))
                for dj in range(4):
                    pz = nextA()
                    for kt in range(8):
                        P.mm(pz[:, 0:G], wb[:, kt, dj * 128:(dj + 1) * 128], ys[n][:, kt, 0:G],
                             start=(kt == 0), stop=(kt == 7))
                    pg = nextA()
                    for kt in range(8):
                        P.mm(pg[:, 0:G], wg[:, kt, dj * 128:(dj + 1) * 128], hn[:, kt, 0:G],
                             start=(kt == 0), stop=(kt == 7))
                    sg = tmp()
                    P.act(sg[:, 0:G], pg[:, 0:G], AF.Sigmoid)
                    if n == 0:
                        P.tt(macc[dj][:, 0:G], sg[:, 0:G], pz[:, 0:G], ALU.mult)
                    else:
                        P.tt(sg[:, 0:G], sg[:, 0:G], pz[:, 0:G], ALU.mult)
                        P.tt(macc[dj][:, 0:G], macc[dj][:, 0:G], sg[:, 0:G], ALU.add, eng="pool")
            for dj in range(4):
                P.copy(mfm[:, db * 4 + dj, 0:G], macc[dj][:, 0:G])

        def evac_out(j, pp):
            P.tt(s[:, j, 0:G], s[:, j, 0:G], pp, ALU.add)
        proj_fm(mfm, 8, lambda c0, nb: wload(wview(wb_out[l], (), c0, nb)), 1024, G, evac_out)
        rmsnorm(G, fmc(l, C_NF, 8), hn)
        for hb in range(6):
            nb = min(512, FFH - hb * 512)
            wg_ = wload(wview(wb_ffi[l], (), hb * 512, nb))
            wu_ = wload(wview(wb_ffi[l], (), FFH + hb * 512, nb))
            for jj in range(nb // 128):
                pg = nextA()
                pu = nextA()
                for kt in range(8):
                    P.mm(pg[:, 0:G], wg_[:, kt, jj * 128:(jj + 1) * 128], hn[:, kt, 0:G], start=(kt == 0), stop=(kt == 7))
                for kt in range(8):
                    P.mm(pu[:, 0:G], wu_[:, kt, jj * 128:(jj + 1) * 128], hn[:, kt, 0:G], start=(kt == 0), stop=(kt == 7))
                sg = tmp()
                P.act(sg[:, 0:G], pg[:, 0:G], AF.Silu)
                ht_ = hb * 4 + jj
                P.tt(ys[ht_ // 8][:, ht_ % 8, 0:G], sg[:, 0:G], pu[:, 0:G], ALU.mult)
        for db in range(2):
            accs = [psA[i] for i in range(4)]
            for hb3, (h0, nh) in enumerate(((0, 8), (8, 8), (16, 6))):
                wt = wload(wview(wb_ffo[l], (), db * 512, 512, r0=h0 * 128, nrows=nh * 128))
                for dj in range(4):
                    for hh in range(nh):
                        ht = h0 + hh
                        P.mm(accs[dj][:, 0:G], wt[:, hh, dj * 128:(dj + 1) * 128], ys[ht // 8][:, ht % 8, 0:G],
                             start=(ht == 0), stop=(ht == 21))
            for dj in range(4):
                P.tt(s[:, db * 4 + dj, 0:G], s[:, db * 4 + dj, 0:G], accs[dj][:, 0:G], ALU.add)
        if g == 0:
            for kt in range(8):
                P.memset(s[:, kt, 0:PAD], 0.0, eng="pool")

    ngroups = (nchunks + nch_group - 1) // nch_group
    otm = xtm
    fin = sq
    for g in range(ngroups):
        c0 = g * nch_group
        c1 = min(nchunks, c0 + nch_group)
        G = (c1 - c0) * 128
        P.fence()
        for c in range(c0, c1):
            xt = xtm[c % 2]
            P.dma(xt[:], xin[c * 128:(c + 1) * 128, :])
            for kt in range(8):
                pp = nextB()
                P.tr(pp[:, 0:128], xt[:, kt * 128:(kt + 1) * 128], ident)
                P.copy(s[:, kt, (c - c0) * 128:(c - c0 + 1) * 128], pp[:, 0:128], eng=ev_eng())
        for l in range(L):
            layer(l, g, G)
        P.fence()
        rmsnorm(G, fm[:, L * NFM:L * NFM + 8], fin)
        for c in range(c0, c1):
            if c == 0:
                continue
            ot = otm[c % 2]
            for kt in range(8):
                pp = nextB()
                P.tr(pp[:, 0:128], fin[:, kt, (c - c0) * 128:(c - c0 + 1) * 128], ident)
                P.copy(ot[:, kt * 128:(kt + 1) * 128], pp[:, 0:128], eng=ev_eng())
            P.dma(out[(c - 1) * 128:c * 128, :], ot[:])
    if BUILD_ONLY:
        return nc
    P.emit()
    return nc


BUILD_ONLY = False
MIXER_BUILDERS = {}


def _fmcol(v):
    v = np.asarray(v, np.float32).reshape(-1, 128)
    return np.ascontiguousarray(v.T)


def prep_shared(inp, L, TP):
    f32 = np.float32
    sh = {}
    sh["w_in"] = np.ascontiguousarray(inp["w_in"][:L], f32)
    qk = inp["w_in"][:L, :, OFF_C:OFF_C + 1024].reshape(L, D, 16, 2, 32)
    sh["w_qksw"] = np.ascontiguousarray(qk[:, :, :, ::-1, :].reshape(L, D, 1024), f32)
    sh["w_branch"] = np.ascontiguousarray(inp["w_branch"][:L], f32)
    sh["w_out"] = np.ascontiguousarray(inp["w_out"][:L], f32)
    sh["w_ffn_in"] = np.ascontiguousarray(inp["w_ffn_in"][:L], f32)
    sh["w_ffn_out"] = np.ascontiguousarray(inp["w_ffn_out"][:L], f32)
    cols = []
    for l in range(L):
        vec = inp["rwkv_vec"][l]
        v0 = inp["rwkv_v0"][l - 1] if l > 0 else np.zeros(D, f32)
        parts = [_fmcol(inp["norm_mix"][l]), _fmcol(inp["norm_ffn"][l]), _fmcol(inp["rwkv_mu"][l]),
                 _fmcol(vec[0]), _fmcol(vec[1]), _fmcol(vec[2]), _fmcol(vec[3]),
                 _fmcol(inp["rwkv_rk"][l].reshape(-1)), _fmcol(v0)]
        scw = inp["ssm_conv_w"][l]
        parts.append(np.ascontiguousarray(scw.reshape(4, 16, 128).transpose(2, 1, 0)).reshape(128, 64))
        parts.append(_fmcol(inp["ssm_conv_b"][l]))
        lcw = inp["lru_conv_w"][l]
        parts.append(np.ascontiguousarray(lcw.reshape(4, 8, 128).transpose(2, 1, 0)).reshape(128, 32))
        parts.append(_fmcol(inp["lru_conv_b"][l]))
        parts.append(_fmcol(inp["lru_b_gate"][l].reshape(-1)))
        parts.append(_fmcol(inp["lru_lambda"][l]))
        parts.append(_fmcol(vec[4]))
        parts.append(_fmcol(vec[5]))
        cols.append(np.concatenate(parts, axis=1))
    cols.append(_fmcol(inp["final_norm"]))
    sh["fmpack"] = np.ascontiguousarray(np.concatenate(cols, axis=1), f32)
    assert sh["fmpack"].shape[1] == L * NFM + 8
    rows = []
    for l in range(L):
        vec = inp["rwkv_vec"][l]
        rows.append(np.concatenate([vec[4], vec[5], inp["ssm_norm"][l], inp["ssm_dt_bias"][l],
                                    inp["ssm_a_log"][l], inp["ssm_d"][l]]))
    sh["rowpack"] = np.ascontiguousarray(np.stack(rows), f32)
    sh["lru_wg"] = np.ascontiguousarray(inp["lru_w_gate"][:L], f32)
    i = np.arange(128)
    ident = np.eye(128, dtype=f32)
    triu = (i[:, None] <= i[None, :]).astype(f32)
    mstr = (i[:, None] > i[None, :]).astype(f32)
    spare = np.zeros((128, 128), f32)
    spare[:, 0] = (i >= PAD)
    sut = (i[:, None] < i[None, :]).astype(f32)
    blk = ((i[:, None] // 64) == (i[None, :] // 64)).astype(f32)
    sh["consts"] = np.ascontiguousarray(np.concatenate([ident, triu, mstr, spare, sut, blk, sut, triu, sut, triu], axis=1))
    pos = (np.arange(TP) - PAD).astype(f32)
    half = 32
    freq = np.power(np.float32(10000.0), -np.arange(half, dtype=f32) / half).astype(f32)
    ang = pos[None, :] * freq[:, None]
    cosf = np.cos(ang).astype(f32)
    sinf = np.sin(ang).astype(f32)
    ch = np.arange(128) % 64
    cos_t = cosf[ch % 32]
    sin_t = np.where((ch < 32)[:, None], -sinf[ch % 32], sinf[ch % 32])
    sh["ropetab"] = np.ascontiguousarray(np.stack([cos_t, sin_t], axis=1), f32)
    H = 8
    log_g = np.log1p(-np.exp2(-5.0 - np.arange(H, dtype=f32))).astype(f32)
    idx = np.arange(128, dtype=f32)
    rel = idx[None, :] - idx[:, None]
    intraT = np.where(rel >= 0, np.exp(np.where(rel >= 0, rel, 0.0)[None] * log_g[:, None, None]), 0.0)
    qdec = np.exp((idx + 1.0)[None, :] * log_g[:, None])
    kdec = np.exp((127.0 - idx)[None, :] * log_g[:, None])
    cdec = np.exp(128.0 * log_g)
    rt = np.zeros((128, 1548), f32)
    rt[:, 0:1024] = intraT.transpose(1, 0, 2).reshape(128, 1024)
    pp_ = np.arange(128) // 64
    for j in range(4):
        rt[:, 1024 + j * 128:1024 + (j + 1) * 128] = qdec[2 * j + pp_, :]
        rt[:, 1544 + j] = cdec[2 * j + pp_]
    rt[:, 1536:1544] = kdec.T
    sh["rettab"] = rt
    sh["rw_w2"] = np.ascontiguousarray(inp["rwkv_w2"][:L], f32)
    sh["rw_a2"] = np.ascontiguousarray(inp["rwkv_a2"][:L], f32)
    sh["rw_g2"] = np.ascontiguousarray(inp["rwkv_g2"][:L], f32)
    n1 = max(L - 1, 1)
    sh["rw_v1"] = np.ascontiguousarray(inp["rwkv_v1"][:n1], f32)
    sh["rw_v2"] = np.ascontiguousarray(inp["rwkv_v2"][:n1], f32)
    return sh


def run_module(inp, depth, nch_group, enable=("a", "b", "c", "d"), ncores=None, arena_kb=82):
    x = np.asarray(inp["x"], np.float32)
    B, S, _ = x.shape
    T = S + NMETA
    assert (T + PAD) % 128 == 0
    TP = T + PAD
    nchunks = TP // 128
    nc = build(nchunks, depth, nch_group, enable, arena_kb)
    sh = prep_shared(inp, depth, TP)
    meta = np.asarray(inp["meta"], np.float32)
    ncores = ncores or B
    in_maps = []
    for c in range(ncores):
        b = c % B
        xin = np.concatenate([np.zeros((PAD, D), np.float32), meta, x[b]], axis=0)
        m = dict(sh)
        m["xin"] = np.ascontiguousarray(xin)
        in_maps.append(m)
    res = run_bass_kernel_spmd(nc, in_maps, core_ids=list(range(ncores)))
    outs = [res.results[b]["out"] for b in range(B)]
    return np.stack(outs, axis=0)


def kernel(**inputs):
    return run_module(inputs, 4, NCH_GROUP, ncores=4, arena_kb=82)


NCH_GROUP = 2

RT_INTRA, RT_Q, RT_KDEC, RT_C, RT_N = 0, 1024, 1536, 1544, 1548


def build_ret(k):
    P, L, GMAX, NCHG = k.P, k.L, k.GMAX, k.nch_group
    hn, ys, w_in, w_qksw = k.hn, k.ys, k.w_in, k.w_qksw
    A = k.ar_alloc
    rt = A("rt", [128, RT_N], F32)
    rope_c = P.sb("rope_c", [128, GMAX], F32)
    rope_s = P.sb("rope_s", [128, GMAX], F32)
    pin = A("ret_pin", [128, 16, GMAX], F32, parts=16)
    qrT = A("qrT", [128, 8, GMAX], BF16, parts=8)
    qdT = A("qdT", [128, 4, GMAX], BF16, parts=4)
    vtm = A("ret_vtm", [128, NCHG, 1024], BF16, parts=NCHG)
    gtm = A("ret_gtm", [128, NCHG, 1024], F32, parts=NCHG)
    ktm = A("ret_ktm", [128, NCHG, 512], BF16, parts=NCHG)
    scm = [A(f"ret_scm{i}", [128, 8, 128], BF16) for i in range(2)]
    ysb = [A(f"ret_ysb{i}", [128, 8, 128], F32) for i in range(2)]
    ysq = A("ret_ysq", [128, 8, 128], F32)
    ssq = P.sb("ret_ssq", [128, 8], F32)
    Rst = [P.sb(f"ret_R{l}", [128, 4, 128], F32) for l in range(L)]
    Rb = P.sb("ret_Rb", [128, 4, 128], BF16)
    for l in range(L):
        P.memset(Rst[l][:], 0.0)

    def mix(l, g, G):
        c0 = g * NCHG
        nchk = G // 128
        P.dma(rt[:], k.rettab[:])
        if l == 0:
            P.dma(rope_c[:, 0:G], k.ropetab[:, 0, c0 * 128:c0 * 128 + G])
            P.dma(rope_s[:, 0:G], k.ropetab[:, 1, c0 * 128:c0 * 128 + G])

        def ev1(j, pp):
            P.copy(pin[:, j, 0:G], pp, eng=k.ev_eng())

        def ev2(j, pp):
            P.copy(pin[:, 8 + j, 0:G], pp, eng=k.ev_eng())
        k.proj_fm(hn, 8, lambda c, nb: k.wload(k.wview(k.wb_in[l], (), OFF_C + c, nb)), 1024, G, ev1)
        k.proj_fm(hn, 8, lambda c, nb: k.wload(k.wview(k.wb_qksw[l], (), c, nb)), 1024, G, ev2)
        for j in range(8):
            cc, ss = rope_c, rope_s
            t1 = k.tmp()
            t2 = k.tmp()
            P.tt(t1[:, 0:G], pin[:, j, 0:G], cc[:, 0:G], ALU.mult)
            P.tt(t2[:, 0:G], pin[:, 8 + j, 0:G], ss[:, 0:G], ALU.mult, eng="pool")
            if j < 4:
                P.tt(t1[:, 0:G], t1[:, 0:G], t2[:, 0:G], ALU.add)
                P.copy(qrT[:, j, 0:G], t1[:, 0:G], eng="act")
                for c in range(nchk):
                    P.tt(qdT[:, j, c * 128:(c + 1) * 128], t1[:, c * 128:(c + 1) * 128],
                         rt[:, RT_Q + j * 128:RT_Q + (j + 1) * 128], ALU.mult)
            else:
                P.tt(t1[:, 0:G], t1[:, 0:G], t2[:, 0:G], ALU.add)
                P.act(qrT[:, j, 0:G], t1[:, 0:G], AF.Copy, scale=0.125)

        def ev3(c, b, pp, nb):
            if b < 2:
                P.copy(vtm[:, c, b * 512:(b + 1) * 512], pp, eng=k.ev_eng())
            else:
                P.act(gtm[:, c, (b - 2) * 512:(b - 1) * 512], pp, AF.Silu)
        k.proj_tm(hn, 8, lambda c, nb: k.wload(k.wview(k.wb_in[l], (), OFF_C + 1024 + c, nb)), 2048, G, ev3)

        for c in range(nchk):
            cs = slice(c * 128, (c + 1) * 128)
            for j in range(4):
                pt = k.psT
                P.tr(pt[:, j * 128:(j + 1) * 128], qrT[:, 4 + j, cs], k.identb)
            for h in range(8):
                P.ts(ktm[:, c, h * 64:(h + 1) * 64], k.psT[:, h * 64:(h + 1) * 64],
                     rt[:, RT_KDEC + h:RT_KDEC + h + 1], ALU.mult, eng="dve")
            sc = scm[c % 2]
            for hb in range(2):
                pp = k.nextB()
                for hh in range(4):
                    h = hb * 4 + hh
                    j, pb = h // 2, 64 * (h % 2)
                    P.mm(pp[:, hh * 128:(hh + 1) * 128], qrT[pb:pb + 64, 4 + j, cs], qrT[pb:pb + 64, j, cs])
                P.tt(sc[:, hb * 4:(hb + 1) * 4, :], pp[:, :].re(lambda a: a.rearrange("p (h l) -> p h l", h=4)),
                     rt[:, RT_INTRA + hb * 512:RT_INTRA + (hb + 1) * 512].re(
                         lambda a: a.rearrange("p (h l) -> p h l", h=4)), ALU.mult)
            P.copy(Rb[:], Rst[l][:], eng="act")
            yb = ysb[c % 2]
            for hb in range(2):
                pp = k.nextB()
                for hh in range(4):
                    h = hb * 4 + hh
                    j, pb = h // 2, 64 * (h % 2)
                    P.mm(pp[:, hh * 128:(hh + 1) * 128], sc[:, h, :], vtm[:, c, h * 128:(h + 1) * 128],
                         start=True, stop=False)
                    P.mm(pp[:, hh * 128:(hh + 1) * 128], qdT[pb:pb + 64, j, cs], Rb[pb:pb + 64, j, :],
                         start=False, stop=True)
                P.copy(yb[:, hb * 4:(hb + 1) * 4, :], pp[:, :].re(lambda a: a.rearrange("p (h l) -> p h l", h=4)),
                       eng="act")
            for j in range(4):
                pp = k.nextB()
                for hp in range(2):
                    h = 2 * j + hp
                    P.mm(pp[:, hp * 128:(hp + 1) * 128], ktm[:, c, j * 128:(j + 1) * 128],
                         vtm[:, c, h * 128:(h + 1) * 128])
                for hp in range(2):
                    pb = 64 * hp
                    P.stt(Rst[l][pb:pb + 64, j, :], Rst[l][pb:pb + 64, j, :], rt[pb:pb + 64, RT_C + j:RT_C + j + 1],
                          pp[pb:pb + 64, hp * 128:(hp + 1) * 128], ALU.mult, ALU.add)
            P.tt(ysq[:], yb[:], yb[:], ALU.mult, eng="pool")
            P.red(ssq[:], ysq[:])
            P.ts(ssq[:], ssq[:], 1.0 / 128, ALU.mult, 1e-6, ALU.add)
            P.act(ssq[:], ssq[:], AF.Ln)
            P.act(ssq[:], ssq[:], AF.Exp, scale=-0.5)
            P.tt(yb[:], yb[:], ssq[:].re(lambda a: a.unsqueeze(2).broadcast_to([128, 8, 128])), ALU.mult)
            P.tt(yb[:], yb[:], gtm[:, c, :].re(lambda a: a.rearrange("p (h e) -> p h e", h=8)), ALU.mult)
            for kt in range(8):
                pp = k.nextB()
                P.tr(pp[:, 0:128], yb[:, kt, :], k.ident)
                P.copy(ys[2][:, kt, cs], pp[:, 0:128], eng=k.ev_eng())
    return mix


MIXER_BUILDERS["c"] = build_ret

def build_ssd(k):
    P, L, GMAX, NCHG = k.P, k.L, k.GMAX, k.nch_group
    hn, ys, w_in = k.hn, k.ys, k.w_in
    A = k.ar_alloc
    triU = k.cst[:, 128:256]
    mstr = k.cst[:, 256:384]
    xe = A("ssd_xe", [128, 16, GMAX + 4], F32, parts=16)
    ztm = A("ssd_ztm", [128, NCHG, 1024], F32, parts=NCHG)
    dtm = A("ssd_dtm", [128, NCHG, 16], F32, parts=NCHG)
    rowb = A("ssd_rowb", [128, 1072], F32)
    xsf = A("ssd_xsf", [128, 8, GMAX], F32, parts=8)
    bfm = A("ssd_bfm", [128, 4, GMAX], BF16, parts=4)
    cfm = A("ssd_cfm", [128, 4, GMAX], BF16, parts=4)
    btm = A("ssd_btm", [128, 4, 128], BF16)
    xs_tm = A("ssd_xstm", [128, 16, 64], F32)
    xdt = A("ssd_xdt", [128, 16, 64], BF16)
    xdt2 = A("ssd_xdt2", [128, 16, 64], BF16)
    rhsL = A("ssd_rhsL", [128, 16, 128], F32)
    Lm = A("ssd_Lm", [128, 16, 128], F32)
    Wm = A("ssd_Wm", [128, 16, 128], BF16)
    cbm = A("ssd_cbm", [128, 4, 128], F32)
    yt = A("ssd_yt", [128, 16, 64], F32)
    yt2 = A("ssd_yt2", [128, 16, 64], F32)
    sm = P.sb("ssd_small", [128, 8, 16], F32)
    ssq = P.sb("ssd_ssq", [128, 4], F32)
    St = [P.sb(f"ssd_S{l}", [128, 16, 64], F32) for l in range(L)]
    Sb = P.sb("ssd_Sb", [128, 16, 64], BF16)
    tails = [P.sb(f"ssd_tail{l}", [128, 16, 3], F32) for l in range(L)]
    for l in range(L):
        P.memset(St[l][:], 0.0)
        P.memset(tails[l][:], 0.0)

    def bc(v, n):
        return v.re(lambda a: a.unsqueeze(2).broadcast_to([128, a.shape[1], n]))

    def mix(l, g, G):
        nchk = G // 128
        k.rowload(rowb[:], l, R_SSMN, 1072)
        nw_b = rowb[:, 0:1024]
        dtb_b = rowb[:, 1024:1040]
        alog_b = rowb[:, 1040:1056]
        dsk_b = rowb[:, 1056:1072]
        P.act(sm[:, 6, :], alog_b, AF.Exp)
        P.ts(sm[:, 6, :], sm[:, 6, :], -1.0, ALU.mult)

        def ev_z(c, b, pp, nb):
            P.act(ztm[:, c, b * 512:(b + 1) * 512], pp, AF.Silu)
        k.proj_tm(hn, 8, lambda c, nb: k.wload(k.wview(k.wb_in[l], (), OFF_B + c, nb)), 1024, G, ev_z)

        def ev_x(j, pp):
            P.copy(xe[:, j, 3:3 + G], pp, eng=k.ev_eng())
        k.proj_fm(hn, 8, lambda c, nb: k.wload(k.wview(k.wb_in[l], (), OFF_B + 1024 + c, nb)), 2048, G, ev_x)

        def ev_dt(c, b, pp, nb):
            P.tt(dtm[:, c, :], pp, dtb_b, ALU.add)
        k.proj_tm(hn, 8, lambda c, nb: k.wload(k.wview(k.wb_in[l], (), OFF_B + 3072 + c, nb)), 16, G, ev_dt)
        for j in range(16):
            P.copy(xe[:, j, 0:3], tails[l][:, j, :])
            xc = k.tmp()
            cw = C_SCW + j * 4
            P.ts(xc[:, 0:G], xe[:, j, 0:G], k.fmc(l, cw), ALU.mult, k.fmc(l, C_SCB + j), ALU.add)
            for t in range(1, 4):
                P.stt(xc[:, 0:G], xe[:, j, t:t + G], k.fmc(l, cw + t), xc[:, 0:G], ALU.mult, ALU.add)
            P.copy(tails[l][:, j, :], xe[:, j, G:G + 3])
            if j < 8:
                P.act(xsf[:, j, 0:G], xc[:, 0:G], AF.Silu)
            elif j < 12:
                P.act(bfm[:, j - 8, 0:G], xc[:, 0:G], AF.Silu)
            else:
                P.act(cfm[:, j - 12, 0:G], xc[:, 0:G], AF.Silu)

        for c in range(nchk):
            cs = slice(c * 128, (c + 1) * 128)
            dt = sm[:, 0, :]
            dA = sm[:, 1, :]
            ea = sm[:, 2, :]
            dec = sm[:, 3, :]
            etot = sm[:, 4, :]
            dtdec = sm[:, 5, :]
            P.act(dt, dtm[:, c, :], AF.Exp)
            P.act(dt, dt, AF.Ln, bias=k.eps_t[:, 1:2])
            if g == 0 and c == 0:
                P.ts(dt, dt, k.cst[:, 384:385], ALU.mult)
            P.tt(dA, dt, sm[:, 6, :], ALU.mult)
            pc = k.nextB()
            P.mm(pc[:, 0:16], triU, dA)
            P.mm(pc[:, 16:32], k.onesf_t[:], dA)
            P.act(ea, pc[:, 0:16], AF.Exp)
            P.copy(sm[:, 7, :], pc[:, 0:16])
            P.tt(dec, pc[:, 16:32], sm[:, 7, :], ALU.subtract)
            P.act(dec, dec, AF.Exp)
            P.act(etot, pc[:, 16:32], AF.Exp)
            P.tt(dtdec, dt, dec, ALU.mult)
            for hb in range(2):
                pp = k.nextA()
                for jj in range(4):
                    P.tr(pp[:, jj * 128:(jj + 1) * 128], xsf[:, hb * 4 + jj, cs], k.ident)
                P.copy(xs_tm[:, hb * 8:(hb + 1) * 8, :], pp[:, :].re(lambda a: a.rearrange("p (h e) -> p h e", h=8)),
                       eng="act")
            P.tt(xdt[:], xs_tm[:], bc(dt, 64), ALU.mult)
            P.tt(xdt2[:], xs_tm[:], bc(dtdec, 64), ALU.mult, eng="pool")
            for gi in range(4):
                P.tr(k.psT[:, gi * 128:(gi + 1) * 128], bfm[:, gi, cs], k.identb)
            P.copy(btm[:], k.psT[:, 0:512].re(lambda a: a.rearrange("p (g n) -> p g n", g=4)), eng="act")
            P.tt(rhsL[:], triU.re(lambda a: a.unsqueeze(1).broadcast_to([128, 16, 128])), bc(dA, 128), ALU.mult)
            pcb = k.nextB()
            for gi in range(4):
                P.mm(pcb[:, gi * 128:(gi + 1) * 128], bfm[:, gi, cs], cfm[:, gi, cs])
            P.tt(cbm[:], pcb[:, :].re(lambda a: a.rearrange("p (g l) -> p g l", g=4)),
                 triU.re(lambda a: a.unsqueeze(1).broadcast_to([128, 4, 128])), ALU.mult)
            for gi in range(4):
                pl = k.nextA()
                P.mm(pl[:, :], mstr, rhsL[:, gi * 4:(gi + 1) * 4, :].re(lambda a: a.rearrange("p h l -> p (h l)")))
                P.act(Lm[:, gi * 4:(gi + 1) * 4, :], pl[:, :].re(lambda a: a.rearrange("p (h l) -> p h l", h=4)), AF.Exp)
                P.tt(Wm[:, gi * 4:(gi + 1) * 4, :], Lm[:, gi * 4:(gi + 1) * 4, :],
                     cbm[:, gi, :].re(lambda a: a.unsqueeze(1).broadcast_to([128, 4, 128])), ALU.mult)
            P.copy(Sb[:], St[l][:], eng="act")
            for hb in range(2):
                py = k.nextA()
                for hh in range(8):
                    h = hb * 8 + hh
                    P.mm(py[:, hh * 64:(hh + 1) * 64], Wm[:, h, :], xdt[:, h, :])
                po = k.nextB()
                for gg in range(2):
                    gi = hb * 2 + gg
                    P.mm(po[:, gg * 256:(gg + 1) * 256], cfm[:, gi, cs],
                         Sb[:, gi * 4:(gi + 1) * 4, :].re(lambda a: a.rearrange("p h e -> p (h e)")))
                hs = slice(hb * 8, (hb + 1) * 8)
                P.tt(yt[:, hs, :], po[:, :].re(lambda a: a.rearrange("p (h e) -> p h e", h=8)),
                     bc(sm[:, 2, hb * 8:(hb + 1) * 8], 64), ALU.mult)
                P.tt(yt[:, hs, :], yt[:, hs, :], py[:, :].re(lambda a: a.rearrange("p (h e) -> p h e", h=8)), ALU.add)
            P.tt(yt2[:], xs_tm[:], bc(dsk_b, 64), ALU.mult, eng="pool")
            P.tt(yt[:], yt[:], yt2[:], ALU.add)
            P.tt(yt[:], yt[:], ztm[:, c, :].re(lambda a: a.rearrange("p (h e) -> p h e", h=16)), ALU.mult)
            for hb in range(2):
                pst = k.nextB()
                for gg in range(2):
                    gi = hb * 2 + gg
                    P.mm(pst[:, gg * 256:(gg + 1) * 256], btm[:, gi, :],
                         xdt2[:, gi * 4:(gi + 1) * 4, :].re(lambda a: a.rearrange("p h e -> p (h e)")))
                hs = slice(hb * 8, (hb + 1) * 8)
                P.tt(St[l][:, hs, :], St[l][:, hs, :], bc(sm[:, 4, hb * 8:(hb + 1) * 8], 64), ALU.mult)
                P.tt(St[l][:, hs, :], St[l][:, hs, :],
                     pst[:, :].re(lambda a: a.rearrange("p (h e) -> p h e", h=8)), ALU.add)
            P.tt(yt2[:], yt[:], yt[:], ALU.mult, eng="pool")
            P.red(ssq[:], yt2[:].re(lambda a: a.rearrange("p (g e) n -> p g (e n)", g=4)))
            P.ts(ssq[:], ssq[:], 1.0 / 256, ALU.mult, 1e-5, ALU.add)
            P.act(ssq[:], ssq[:], AF.Ln)
            P.act(ssq[:], ssq[:], AF.Exp, scale=-0.5)
            ytg = yt[:].re(lambda a: a.rearrange("p (g e) n -> p g (e n)", g=4))
            P.tt(ytg, ytg, bc(ssq[:], 256), ALU.mult)
            ytf = yt[:].re(lambda a: a.rearrange("p h e -> p (h e)"))
            P.tt(ytf, ytf, nw_b, ALU.mult)
            for kt in range(8):
                pp = k.nextA()
                P.tr(pp[:, 0:128], ytf[:, kt * 128:(kt + 1) * 128], k.ident)
                P.copy(ys[1][:, kt, cs], pp[:, 0:128], eng=k.ev_eng())
    return mix


MIXER_BUILDERS["b"] = build_ssd

def build_rwkv(k):
    P, L, GMAX, NCHG = k.P, k.L, k.GMAX, k.nch_group
    hn, ys, w_in = k.hn, k.ys, k.w_in
    A = k.ar_alloc
    cst = k.cst
    ident, identb = k.ident, k.identb
    mstr = cst[:, 256:384]
    blkb_t = P.sb("rw_blkb", [128, 128], BF16)
    P.copy(blkb_t[:], cst[:, 640:768])
    blk = blkb_t[:]
    sqb2 = [A(f"rw_sqb{i}", [128, 2, 128], BF16) for i in range(2)]
    mask4 = cst[:, 768:1280]
    pa = A("rw_pa", [128, 26, GMAX + 1], F32, parts=26)
    w2s = A("rw_w2s", [128, 1024], BF16)
    a2s = A("rw_a2s", [128, 1024], BF16)
    g2s = A("rw_g2s", [128, 1024], BF16)
    v1s = A("rw_v1s", [128, 8, 32], BF16)
    v2s = A("rw_v2s", [128, 1024], BF16)
    twb = A("rw_twb", [128, 128], BF16)
    adb = A("rw_adb", [128, 128], BF16)
    sgb = A("rw_sgb", [128, 128], BF16)
    bt3 = [A(f"rw_bt3{i}", [128, 3, 128], BF16) for i in range(2)]
    gfm = A("rw_gfm", [128, 8, 128], BF16, parts=8)
    bonus = A("rw_bonus", [128, 8, 128], BF16, parts=8)
    off_alias = k.ar["off"]
    vb16 = A("rw_vb16", [128, 8, GMAX], BF16, parts=8)
    vvb = A("rw_vvb", [128, GMAX], BF16)
    k.ar["off"] = off_alias
    ar = A("rw_ar", [128, 8, 256], BF16, parts=8)
    bet = A("rw_bet", [128, 8, 128], BF16, parts=8)
    ktl = A("rw_ktl", [128, 8, 128], BF16, parts=8)
    tm3 = A("rw_tm3", [128, 3, 1024], BF16)
    btm, ktm, vtm = tm3[:, 0, :], tm3[:, 1, :], tm3[:, 2, :]
    AT = [A(f"rw_AT{i}", [128, 4, 512], BF16, parts=4) for i in range(1)] * 2
    P0 = [A(f"rw_P0{i}", [128, 4, 128], BF16) for i in range(1)] * 2
    QP = [A(f"rw_QP{i}", [128, 4, 256], BF16) for i in range(2)]
    Xb = [A(f"rw_Xb{i}", [128, 4, 64], BF16) for i in range(2)]
    Ub = A("rw_Ub", [128, 16, 64], BF16)
    ysb = A("rw_ysb", [128, 16, 64], F32)
    ysq = A("rw_ysq", [128, 16, 64], F32)
    elast = P.sb("rw_elast", [128, 8], F32)
    st4 = P.sb("rw_st4", [128, 4, 16], F32)
    nw0 = P.sb("rw_nw0", [128, 8], F32)
    cm = P.sb("rw_cm", [128, 2], F32)
    P.memset(cm[:, 0:1], -0.5)
    P.memset(cm[:, 1:2], 1e-24)
    vfirst = P.sb("rw_vfirst", [128, 8, GMAX], F32, parts=8)
    Hst = [P.sb(f"rw_H{l}", [128, 8, 64], F32) for l in range(L)]
    H0b = P.sb("rw_H0b", [128, 8, 64], BF16)
    shtail = [P.sb(f"rw_sht{l}", [128, 26], F32) for l in range(L)]
    for l in range(L):
        P.memset(Hst[l][:], 0.0)
        P.memset(shtail[l][:], 0.0)

    def bc(v, n):
        return v.re(lambda a: a.unsqueeze(2).broadcast_to([128, a.shape[1], n]))

    rtp = [A(f"rw_tmp{i}", [128, 128], F32) for i in range(12)]
    rtc = [0]

    def ctmp():
        rtc[0] += 1
        return rtp[rtc[0] % 12]

    def mix(l, g, G):
        nchk = G // 128
        fmc = k.fmc

        def ev(j, pp):
            P.copy(pa[:, j, 1:1 + G], pp, eng=k.ev_eng())
        k.proj_fm(hn, 8, lambda c, nb: k.wload(k.wview(k.wb_in[l], (), OFF_A + c, nb)), 3328, G, ev)
        P.dma(w2s[0:64, :], View(k.rw_w2, k.rw_w2.h[l], 0, 1), eng="pool")
        P.dma(a2s[64:128, :], View(k.rw_a2, k.rw_a2.h[l], 0, 1), eng="pool")
        P.dma(g2s[:, :], View(k.rw_g2, k.rw_g2.h[l], 0, 1), eng="pool")
        if l > 0:
            P.dma(v1s[:], View(k.rw_v1, k.rw_v1.h[l - 1].rearrange("(kt p) r -> p kt r", p=128), 0, 1), eng="pool")
            P.dma(v2s[0:32, :], View(k.rw_v2, k.rw_v2.h[l - 1], 0, 1), eng="pool")
        P.ts(nw0[:], fmc(l, C_W0, 8), -1.0, ALU.mult)
        P.copy(pa[:, :, 0], shtail[l][:])
        P.copy(shtail[l][:], pa[:, :, G])
        for j in range(26):
            d = k.tmp()
            P.tt(d[:, 0:G], pa[:, j, 0:G], pa[:, j, 1:G + 1], ALU.subtract)
            P.stt(pa[:, j, 1:G + 1], d[:, 0:G], fmc(l, C_MU + j), pa[:, j, 1:G + 1], ALU.mult, ALU.add)
        if l == 0:
            for j in range(8):
                P.copy(vfirst[:, j, 0:G], pa[:, 16 + j, 1:G + 1], eng=k.ev_eng())
        else:
            for j in range(8):
                P.copy(vb16[:, j, 0:G], pa[:, 16 + j, 1:G + 1], eng=k.ev_eng())
            pv = k.nextA()
            for kt in range(8):
                P.mm(pv[0:32, 0:G], v1s[:, kt, :], vb16[:, kt, 0:G], start=(kt == 0), stop=(kt == 7))
            P.copy(vvb[0:32, 0:G], pv[0:32, 0:G])
            for j in range(8):
                pp = k.nextA()
                P.mm(pp[:, 0:G], v2s[0:32, j * 128:(j + 1) * 128], vvb[0:32, 0:G])
                sg = k.tmp()
                P.act(sg[:, 0:G], pp[:, 0:G], AF.Sigmoid, bias=fmc(l, C_V0 + j))
                d = k.tmp()
                vcur = pa[:, 16 + j, 1:G + 1]
                P.tt(d[:, 0:G], vfirst[:, j, 0:G], vcur, ALU.subtract)
                P.tt(d[:, 0:G], d[:, 0:G], sg[:, 0:G], ALU.mult)
                P.tt(vcur, vcur, d[:, 0:G], ALU.add)

        P.fence()
        for c in range(nchk):
            cs = slice(c * 128, (c + 1) * 128)
            c1 = slice(1 + c * 128, 1 + (c + 1) * 128)
            P.act(twb[0:64, :], pa[0:64, 24, c1], AF.Tanh)
            P.copy(adb[64:128, :], pa[64:128, 24, c1])
            P.act(sgb[:, :], pa[:, 25, c1], AF.Sigmoid)
            for j in range(8):
                js = slice(j * 128, (j + 1) * 128)
                ed_j, av_j, kk_j = ctmp()[:, 0:128], ctmp()[:, 0:128], ctmp()[:, 0:128]
                b3 = bt3[j % 2]
                r_, k_, v_ = pa[:, j, c1], pa[:, 8 + j, c1], pa[:, 16 + j, c1]
                pw = k.nextA()
                P.mm(pw[:, 0:128], w2s[0:64, js], twb[0:64, :])
                P.mm(pw[:, 128:256], a2s[64:128, js], adb[64:128, :])
                P.mm(pw[:, 256:384], g2s[:, js], sgb[:, :])
                t1 = ctmp()
                P.act(t1[:, 0:128], pw[:, 0:128], AF.Exp, scale=-1.0, bias=nw0[:, j:j + 1])
                P.act(t1[:, 0:128], t1[:, 0:128], AF.Ln, bias=k.eps_t[:, 1:2])
                P.act(ed_j, t1[:, 0:128], AF.Exp, scale=-1.0, bias=cm[:, 0:1])
                P.act(av_j, pw[:, 128:256], AF.Sigmoid, bias=fmc(l, C_A0 + j))
                P.copy(gfm[:, j, :], pw[:, 256:384])
                kq = ctmp()
                P.ts(kq[:, 0:128], k_, fmc(l, C_KK + j), ALU.mult)
                sq2 = sqb2[j % 2]
                P.tt(sq2[:, 0, :], kq[:, 0:128], kq[:, 0:128], ALU.mult, eng="pool")
                pn = k.nextB()
                P.mm(pn[:, 0:128], blk, sq2[:, 0, :])
                rn = ctmp()
                P.act(rn[:, 0:128], pn[:, 0:128], AF.Ln, bias=cm[:, 1:2])
                P.act(rn[:, 0:128], rn[:, 0:128], AF.Exp, scale=-0.5)
                P.tt(kk_j, kq[:, 0:128], rn[:, 0:128], ALU.mult)
                t2 = ctmp()
                P.ts(t2[:, 0:128], av_j, -1.0, ALU.add, fmc(l, C_KA + j), ALU.mult)
                P.stt(k_, t2[:, 0:128], 1.0, k_, ALU.add, ALU.mult)
                P.stt(sq2[:, 1, :], r_, fmc(l, C_RK + j), k_, ALU.mult, ALU.mult)
                P.mm(pn[:, 128:256], blk, sq2[:, 1, :])
                P.tt(bonus[:, j, :], pn[:, 128:256], v_, ALU.mult)
                P.copy(b3[:, 2, :], v_, eng="act")
                cw = ctmp()
                P.scan(cw[:, 0:128], k.onesf_t[:, 0:128], ed_j, 0.0)
                e_in = ctmp()
                P.act(e_in[:, 0:128], cw[:, 0:128], AF.Exp, scale=-1.0)
                P.copy(elast[:, j:j + 1], e_in[:, 127:128])
                e_inv = ctmp()
                P.act(e_inv[:, 0:128], cw[:, 0:128], AF.Exp)
                e_ex = ctmp()
                P.tt(e_ex[:, 0:128], cw[:, 0:128], ed_j, ALU.subtract)
                P.act(e_ex[:, 0:128], e_ex[:, 0:128], AF.Exp, scale=-1.0)
                P.stt(ar[:, j, 0:128], kk_j, -1.0, e_ex[:, 0:128], ALU.mult, ALU.mult)
                P.tt(ar[:, j, 128:256], r_, e_in[:, 0:128], ALU.mult)
                P.tt(t2[:, 0:128], kk_j, av_j, ALU.mult, eng="pool")
                P.tt(bet[:, j, :], t2[:, 0:128], e_inv[:, 0:128], ALU.mult)
                P.tt(ktl[:, j, :], k_, e_inv[:, 0:128], ALU.mult)
                P.ts(b3[:, 0, :], bet[:, j, :], elast[:, j:j + 1], ALU.mult)
                P.ts(b3[:, 1, :], ktl[:, j, :], elast[:, j:j + 1], ALU.mult)
                for q in range(3):
                    P.tr(k.psT[:, q * 128:(q + 1) * 128], b3[:, q, :], identb)
                P.copy(tm3[:, :, js], k.psT[:, 0:384].re(lambda a: a.rearrange("p (q c) -> p q c", q=3)),
                       eng=k.ev_eng())
            P.copy(H0b[:], Hst[l][:])
            for hb in range(4):
                at = AT[hb % 2]
                p0 = P0[hb % 2]
                pp0 = k.nextB()
                for hh in range(4):
                    h = hb * 4 + hh
                    j, pb = h // 2, 64 * (h % 2)
                    b1 = k.nextA()
                    P.mm(b1[:, 0:256], bet[pb:pb + 64, j, :], ar[pb:pb + 64, j, :])
                    P.mm(b1[:, 256:512], ktl[pb:pb + 64, j, :], ar[pb:pb + 64, j, :])
                    P.tt(at[:, hh, :], b1[:, :], mask4, ALU.mult)
                    P.mm(pp0[:, hh * 128:(hh + 1) * 128], ar[pb:pb + 64, j, 0:128], bet[pb:pb + 64, j, :])
                P.tt(p0[:], pp0[:, :].re(lambda a: a.rearrange("p (h s) -> p h s", h=4)),
                     mstr.re(lambda a: a.unsqueeze(1).broadcast_to([128, 4, 128])), ALU.mult)
                px = k.nextB()
                for hh in range(4):
                    h = hb * 4 + hh
                    j, pb = h // 2, 64 * (h % 2)
                    P.mm(px[:, hh * 64:(hh + 1) * 64], ar[pb:pb + 64, j, 0:128], H0b[pb:pb + 64, j, :],
                         start=True, stop=False)
                    P.mm(px[:, hh * 64:(hh + 1) * 64], at[:, hh, 256:384], vtm[:, h * 64:(h + 1) * 64],
                         start=False, stop=True)
                xcur = Xb[0]
                P.copy(xcur[:], px[:, 0:256].re(lambda a: a.rearrange("p (h v) -> p h v", h=4)), eng="act")
                for step in range(7):
                    def Q(hh):
                        return at[:, hh, 0:128] if step == 0 else QP[(step - 1) % 2][:, hh, 0:128]

                    def Pm(hh):
                        return p0[:, hh, :] if step == 0 else QP[(step - 1) % 2][:, hh, 128:256]
                    px = k.nextB()
                    for hh in range(4):
                        P.mm(px[:, hh * 64:(hh + 1) * 64], Q(hh), xcur[:, hh, :])
                    if step < 6:
                        xn = Xb[(step + 1) % 2]
                        P.tt(xn[:], px[:, 0:256].re(lambda a: a.rearrange("p (h v) -> p h v", h=4)), xcur[:], ALU.add)
                        xcur = xn
                        qn = QP[step % 2]
                        for half in range(2):
                            pq = k.nextA()
                            for h2 in range(2):
                                hh = half * 2 + h2
                                P.mm(pq[:, h2 * 256:h2 * 256 + 128], Pm(hh), Q(hh))
                                P.mm(pq[:, h2 * 256 + 128:h2 * 256 + 256], Q(hh), Pm(hh))
                            P.copy(qn[:, half * 2:half * 2 + 2, :],
                                   pq[:, :].re(lambda a: a.rearrange("p (h s) -> p h s", h=2)), eng="act")
                    else:
                        P.tt(Ub[:, hb * 4:(hb + 1) * 4, :],
                             px[:, 0:256].re(lambda a: a.rearrange("p (h v) -> p h v", h=4)), xcur[:], ALU.add)
                py = k.nextB()
                for hh in range(4):
                    h = hb * 4 + hh
                    j, pb = h // 2, 64 * (h % 2)
                    P.mm(py[:, hh * 64:(hh + 1) * 64], ar[pb:pb + 64, j, 128:256], H0b[pb:pb + 64, j, :],
                         start=True, stop=False)
                    P.mm(py[:, hh * 64:(hh + 1) * 64], at[:, hh, 128:256], Ub[:, h, :], start=False, stop=False)
                    P.mm(py[:, hh * 64:(hh + 1) * 64], at[:, hh, 384:512], vtm[:, h * 64:(h + 1) * 64],
                         start=False, stop=True)
                P.copy(ysb[:, hb * 4:(hb + 1) * 4, :], py[:, 0:256].re(lambda a: a.rearrange("p (h v) -> p h v", h=4)),
                       eng="dve")
            for j in range(8):
                ph = k.nextA()
                P.mm(ph[:, 0:128], btm[:, j * 128:(j + 1) * 128],
                     Ub[:, 2 * j:2 * j + 2, :].re(lambda a: a.rearrange("p h v -> p (h v)")), start=True, stop=False)
                P.mm(ph[:, 0:128], ktm[:, j * 128:(j + 1) * 128], vtm[:, j * 128:(j + 1) * 128], start=False, stop=True)
                for hp in range(2):
                    pb = 64 * hp
                    P.stt(Hst[l][pb:pb + 64, j, :], Hst[l][pb:pb + 64, j, :], elast[pb:pb + 64, j:j + 1],
                          ph[pb:pb + 64, hp * 64:(hp + 1) * 64], ALU.mult, ALU.add)
            s1, s2, mean, var = st4[:, 0, :], st4[:, 1, :], st4[:, 2, :], st4[:, 3, :]
            P.red(s1, ysb[:])
            P.tt(ysq[:], ysb[:], ysb[:], ALU.mult, eng="pool")
            P.red(s2, ysq[:])
            P.ts(mean, s1, 1.0 / 64, ALU.mult)
            P.tt(var, mean, mean, ALU.mult)
            P.stt(var, s2, 1.0 / 64, var, ALU.mult, ALU.subtract)
            P.ts(var, var, 64e-5, ALU.add)
            P.act(var, var, AF.Ln)
            P.act(var, var, AF.Exp, scale=-0.5)
            P.tt(ysb[:], ysb[:], bc(mean, 64), ALU.subtract)
            P.tt(ysb[:], ysb[:], bc(var, 64), ALU.mult)
            for kt in range(8):
                pp = k.nextA()
                P.tr(pp[:, 0:128], ysb[:, 2 * kt:2 * kt + 2, :].re(lambda a: a.rearrange("p h v -> p (h v)")), ident)
                t3 = ctmp()
                P.ts(t3[:, 0:128], pp[:, 0:128], fmc(l, C_LNW + kt), ALU.mult, fmc(l, C_LNB + kt), ALU.add)
                P.tt(t3[:, 0:128], t3[:, 0:128], bonus[:, kt, :], ALU.add)
                P.tt(ys[0][:, kt, cs], t3[:, 0:128], gfm[:, kt, :], ALU.mult)
    return mix


MIXER_BUILDERS["a"] = build_rwkv
```

```python
import numpy as np
import concourse.bass as bass
import concourse.mybir as mybir

F32 = mybir.dt.float32
BF16 = mybir.dt.bfloat16
AF = mybir.ActivationFunctionType
ALU = mybir.AluOpType
AX = mybir.AxisListType

COMPUTE = ("pe", "act", "dve", "pool")
ALLENG = ("pe", "act", "dve", "pool", "sp")
EPOCH = 30000
NDMASEM = 24
SAME_ENGINE_SYNC = True


class View:
    __slots__ = ("t", "ap", "p0", "p1")

    def __init__(self, t, ap, p0, p1):
        self.t, self.ap, self.p0, self.p1 = t, ap, p0, p1

    def re(self, fn):
        return View(self.t, fn(self.ap), self.p0, self.p1)

    def __getitem__(self, idx):
        return View(self.t, self.ap[idx], self.p0, self.p1)


class Tile:
    def __init__(self, name, h, nparts=1):
        self.name, self.h, self.nparts = name, h, nparts
        self.psum = False
        self.lastw = [None] * nparts
        self.readers = [[] for _ in range(nparts)]

    def __getitem__(self, idx):
        ap = self.h[idx]
        if self.nparts == 1:
            return View(self, ap, 0, 1)
        i = idx[1] if isinstance(idx, tuple) and len(idx) > 1 else slice(None)
        if isinstance(i, int):
            return View(self, ap, i, i + 1)
        p0, p1, _ = i.indices(self.nparts)
        return View(self, ap, p0, p1)

    def all(self):
        return self[:]


class Op:
    __slots__ = ("eng", "fn", "deps", "dma", "pos", "gid", "sig", "signaler", "waits", "dpos", "rg")


class Prog:
    def __init__(self, nc):
        self.nc = nc
        self.ops = []
        self.eng_ops = {e: [] for e in ALLENG}
        self.dma_ops = {e: [] for e in ALLENG}
        self.ntile = 0
        self.fence_dma = {}
        self.fence_skip = ()

    def sb(self, name, shape, dtype=F32, parts=1):
        self.ntile += 1
        h = self.nc.alloc_sbuf_tensor(f"{name}_{self.ntile}", list(shape), dtype)
        return Tile(name, h, parts)

    def ps(self, name, shape, dtype=F32, parts=1):
        self.ntile += 1
        h = self.nc.alloc_psum_tensor(f"{name}_{self.ntile}", list(shape), dtype)
        t = Tile(name, h, parts)
        t.psum = True
        return t

    def dram(self, name, shape, dtype=F32, kind="Internal", parts=1):
        h = self.nc.dram_tensor(name, list(shape), dtype, kind=kind).ap()
        return Tile(name, h, parts)

    def add(self, eng, fn, reads=(), writes=(), dma=False):
        op = Op()
        op.eng, op.fn, op.dma = eng, fn, dma
        op.gid = len(self.ops)
        op.rg = None
        op.sig = None
        op.signaler = False
        deps = set()
        for v in reads:
            if v is None:
                continue
            t = v.t
            for p in range(v.p0, v.p1):
                if t.lastw[p] is not None:
                    deps.add(t.lastw[p])
                if t.psum:
                    for r in t.readers[p]:
                        if r.eng != eng:
                            deps.add(r)
        for v in writes:
            t = v.t
            for p in range(v.p0, v.p1):
                if t.lastw[p] is not None:
                    deps.add(t.lastw[p])
                for r in t.readers[p]:
                    deps.add(r)
        for v in reads:
            if v is None:
                continue
            t = v.t
            for p in range(v.p0, v.p1):
                t.readers[p].append(op)
        for v in writes:
            t = v.t
            for p in range(v.p0, v.p1):
                t.lastw[p] = op
                t.readers[p] = []
        deps.discard(op)
        if dma:
            lst = self.dma_ops[eng]
            op.dpos = len(lst)
            if op.dpos >= NDMASEM:
                deps.add(lst[op.dpos - NDMASEM])
            lst.append(op)
        op.deps = deps
        op.pos = len(self.eng_ops[eng])
        self.eng_ops[eng].append(op)
        self.ops.append(op)
        return op

    def fence(self):
        lastc = [self.eng_ops[e][-1] for e in ALLENG if self.eng_ops[e] and not self.eng_ops[e][-1].dma]
        lastc = []
        for e in ALLENG:
            for op in reversed(self.eng_ops[e]):
                if not op.dma:
                    lastc.append(op)
                    break
        dmas = []
        for e in ALLENG:
            if e in self.fence_skip:
                continue
            dmas += self.dma_ops[e][self.fence_dma.get(e, 0):]
            self.fence_dma[e] = len(self.dma_ops[e])
        for e in ALLENG:
            if not self.eng_ops[e] or e in self.fence_skip:
                continue
            op = self.add(e, None, [], [])
            op.deps |= set(lastc) | set(dmas)
            op.deps.discard(op)

    def mm(self, out, lhsT, rhs, start=True, stop=True, **kw):
        op = self.add("pe", lambda e: e.matmul(out.ap, lhsT.ap, rhs.ap, start=start, stop=stop, **kw),
                      [lhsT, rhs] + ([] if start else [out]), [out])
        op.rg = (lhsT.ap.base_partition(), lhsT.ap.shape[0], out.ap.base_partition())
        return op

    def tr(self, out, in_, ident):
        op = self.add("pe", lambda e: e.transpose(out.ap, in_.ap, ident.ap), [in_, ident], [out])
        op.rg = (in_.ap.base_partition(), in_.ap.shape[0], out.ap.base_partition())
        return op

    def act(self, out, in_, func, bias=None, scale=None, accum=None, eng="act"):
        def fn(e):
            kw = {}
            if bias is not None:
                kw["bias"] = bias.ap if isinstance(bias, View) else bias
            if scale is not None:
                kw["scale"] = scale.ap if isinstance(scale, View) else scale
            if accum is not None:
                kw["accum_out"] = accum.ap
            return e.activation(out.ap, in_.ap, func, **kw)
        rd = [in_] + [x for x in (bias, scale) if isinstance(x, View)]
        wr = [out] + ([accum] if accum is not None else [])
        return self.add(eng, fn, rd, wr)

    def tt(self, out, a, b, op, eng="dve"):
        return self.add(eng, lambda e: e.tensor_tensor(out.ap, a.ap, b.ap, op), [a, b], [out])

    def ts(self, out, a, s1, op0, s2=None, op1=None, accum=None, eng="dve"):
        def fn(e):
            kw = {}
            if accum is not None:
                kw["accum_out"] = accum.ap
            return e.tensor_scalar(out.ap, a.ap, s1.ap if isinstance(s1, View) else s1,
                                   (s2.ap if isinstance(s2, View) else s2), op0,
                                   op1 if op1 is not None else ALU.bypass, **kw)
        rd = [a] + [x for x in (s1, s2) if isinstance(x, View)]
        wr = [out] + ([accum] if accum is not None else [])
        return self.add(eng, fn, rd, wr)

    def stt(self, out, a, s, b, op0, op1, eng="dve"):
        return self.add(eng, lambda e: e.scalar_tensor_tensor(out.ap, a.ap, s.ap if isinstance(s, View) else s,
                                                              b.ap, op0, op1),
                        [a, b] + ([s] if isinstance(s, View) else []), [out])

    def copy(self, out, in_, eng="dve"):
        if eng == "act":
            return self.add(eng, lambda e: e.copy(out.ap, in_.ap), [in_], [out])
        return self.add(eng, lambda e: e.tensor_copy(out.ap, in_.ap), [in_], [out])

    def memset(self, out, val, eng="dve"):
        return self.add(eng, lambda e: e.memset(out.ap, val), [], [out])

    def red(self, out, in_, op=ALU.add, axis=AX.X, eng="dve"):
        return self.add(eng, lambda e: e.tensor_reduce(out.ap, in_.ap, axis, op), [in_], [out])

    def scan(self, out, d0, d1, init, op0=ALU.mult, op1=ALU.add):
        return self.add("dve", lambda e: e.tensor_tensor_scan(out.ap, d0.ap, d1.ap,
                                                              init.ap if isinstance(init, View) else init, op0, op1),
                        [d0, d1] + ([init] if isinstance(init, View) else []), [out])

    def recip(self, out, in_):
        return self.add("dve", lambda e: e.reciprocal(out.ap, in_.ap), [in_], [out])

    def dma(self, out, in_, eng="act", **kw):
        return self.add(eng, lambda e: e.dma_start(out=out.ap, in_=in_.ap, **kw), [in_], [out], dma=True)

    def emit(self):
        nc = self.nc
        seen = {f: {e: -1 for e in COMPUTE + ("sp",)} for f in ALLENG}
        seen_dma = {f: set() for f in ALLENG}
        for op in self.ops:
            f = op.eng
            best = {}
            dwaits = []
            for d in op.deps:
                if d.dma:
                    if d.gid in seen_dma[f]:
                        continue
                    seen_dma[f].add(d.gid)
                    dwaits.append(d)
                    d.signaler = True
                else:
                    if d.eng == f and f != "pe" and not SAME_ENGINE_SYNC:
                        continue
                    if d.eng == "pe" and f == "pe" and d.rg is not None and d.rg == op.rg:
                        continue
                    if seen[f][d.eng] >= d.pos:
                        continue
                    if d.eng not in best or best[d.eng].pos < d.pos:
                        best[d.eng] = d
            for e, d in best.items():
                seen[f][e] = d.pos
                d.signaler = True
            op.waits = list(best.values()) + dwaits
        nsig = {e: 0 for e in ALLENG}
        for e in ALLENG:
            for op in self.eng_ops[e]:
                if op.dma:
                    op.sig = ("dma", e, op.dpos % NDMASEM, 16 * (op.dpos // NDMASEM + 1))
                elif op.signaler:
                    n = nsig[e]
                    op.sig = ("cmp", e, n // EPOCH, n % EPOCH + 1)
                    nsig[e] = n + 1
        sems = {}
        stack = []

        def getsem(key):
            if key not in sems:
                cm = nc.semaphore("s_%s_%s_%d" % key)
                sems[key] = cm.__enter__()
                stack.append(cm)
            return sems[key]

        for op in self.ops:
            if op.sig is not None:
                getsem(op.sig[:3])
        engobj = {"pe": "tensor", "act": "scalar", "dve": "vector", "pool": "gpsimd", "sp": "sync"}
        self.final_dma = {e: list(self.dma_ops[e]) for e in ALLENG}

        with nc.Block() as block:
            for e in ALLENG:
                ops = self.eng_ops[e]
                if not ops:
                    continue

                def body(eng, ops=ops, e=e):
                    for op in ops:
                        for d in op.waits:
                            eng.wait_ge(sems[d.sig[:3]], d.sig[3])
                        if op.fn is None:
                            if op.sig is None:
                                continue
                            ins = eng.nop()
                        else:
                            ins = op.fn(eng)
                        if op.sig is not None:
                            ins.then_inc(sems[op.sig[:3]], 16 if op.dma else 1)
                    lst = self.dma_ops[e]
                    done = set()
                    for op in reversed(lst):
                        k = op.sig[:3]
                        if k in done:
                            continue
                        done.add(k)
                        eng.wait_ge(sems[k], op.sig[3])

                getattr(block, engobj[e])(body)
        for cm in reversed(stack):
            cm.__exit__(None, None, None)
        return nc

from concourse.bass_utils import run_bass_kernel_spmd

D = 1024
NMETA = 16
PAD = 112
OFF_A, OFF_B, OFF_C, OFF_D, OFF_G, IN_W = 0, 3328, 6416, 9488, 11536, 15632
FFH = 2816
C_NM, C_NF, C_MU, C_W0, C_A0, C_KK, C_KA, C_RK, C_V0 = 0, 8, 16, 42, 50, 58, 66, 74, 82
C_SCW, C_SCB, C_LCW, C_LCB, C_BG, C_LAM, C_LNW, C_LNB, NFM = 90, 154, 170, 202, 210, 226, 234, 242, 250
R_LNW, R_LNB, R_SSMN, R_DTB, R_ALOG, R_SD, NROW = 0, 1024, 2048, 3072, 3088, 3104, 3120


class K:
    pass


def build(nchunks, depth, nch_group, enable=("a", "b", "c", "d"), arena_kb=82):
    nc = bass.Bass("TRN2", target_bir_lowering=False)
    P = Prog(nc)
    P.fence_skip = ("sp",)
    TP = nchunks * 128
    L = depth
    GMAX = nch_group * 128
    k = K()
    xin = P.dram("xin", [TP, D], F32, kind="ExternalInput")
    out = P.dram("out", [TP - 128, D], F32, kind="ExternalOutput")
    w_in = P.dram("w_in", [L, D, IN_W], F32, kind="ExternalInput")
    w_qksw = P.dram("w_qksw", [L, D, 1024], F32, kind="ExternalInput")
    w_branch = P.dram("w_branch", [L, 4, D, D], F32, kind="ExternalInput")
    w_out = P.dram("w_out", [L, D, D], F32, kind="ExternalInput")
    w_ffn_in = P.dram("w_ffn_in", [L, D, 2 * FFH], F32, kind="ExternalInput")
    w_ffn_out = P.dram("w_ffn_out", [L, FFH, D], F32, kind="ExternalInput")
    fmpack = P.dram("fmpack", [128, L * NFM + 8], F32, kind="ExternalInput")
    rowpack = P.dram("rowpack", [L, NROW], F32, kind="ExternalInput")
    lru_wg = P.dram("lru_wg", [L, 2, 8, 128, 128], F32, kind="ExternalInput")
    consts = P.dram("consts", [128, 1280], F32, kind="ExternalInput")
    ropetab = P.dram("ropetab", [128, 2, TP], F32, kind="ExternalInput")
    rettab = P.dram("rettab", [128, 1548], F32, kind="ExternalInput")
    rw_w2 = P.dram("rw_w2", [L, 64, D], F32, kind="ExternalInput")
    rw_a2 = P.dram("rw_a2", [L, 64, D], F32, kind="ExternalInput")
    rw_g2 = P.dram("rw_g2", [L, 128, D], F32, kind="ExternalInput")
    rw_v1 = P.dram("rw_v1", [max(L - 1, 1), D, 32], F32, kind="ExternalInput")
    rw_v2 = P.dram("rw_v2", [max(L - 1, 1), 32, D], F32, kind="ExternalInput")

    def cast_weight(name, src_tile, idx, shape):
        nblk = max(1, shape[0] // 256)
        t = P.dram(name, list(shape), BF16, kind="Internal", parts=nblk)
        rb = shape[0] // nblk
        for i in range(nblk):
            src_ap = src_tile.h[idx + (slice(i * rb, (i + 1) * rb), slice(None))]
            P.dma(View(t, t.h[i * rb:(i + 1) * rb, :], i, i + 1), View(src_tile, src_ap, 0, 1), eng="pool")
        return t

    wb_in, wb_qksw, wb_branch, wb_out, wb_ffi, wb_ffo = {}, {}, {}, {}, {}, {}

    def cast_layer(l):
        wb_in[l] = cast_weight(f"wb_in{l}", w_in, (l,), [D, IN_W])
        wb_qksw[l] = cast_weight(f"wb_qksw{l}", w_qksw, (l,), [D, 1024])
        wb_branch[l] = [cast_weight(f"wb_br{l}_{n}", w_branch, (l, n), [D, D]) for n in range(4)]
        wb_out[l] = cast_weight(f"wb_out{l}", w_out, (l,), [D, D])
        wb_ffi[l] = cast_weight(f"wb_ffi{l}", w_ffn_in, (l,), [D, 2 * FFH])
        wb_ffo[l] = cast_weight(f"wb_ffo{l}", w_ffn_out, (l,), [FFH, D])

    for l_ in range(L):
        cast_layer(l_)

    cst = P.sb("cst", [128, 1280], F32)
    P.dma(cst[:], consts[:])
    ident = cst[:, 0:128]
    identb_t = P.sb("identb", [128, 128], BF16)
    P.copy(identb_t[:], ident)
    identb = identb_t[:]
    onesb_t = P.sb("onesb", [128, 128], BF16)
    P.memset(onesb_t[:], 1.0)
    onesb = onesb_t[:]
    fm = P.sb("fm", [128, L * NFM + 8], F32)
    P.dma(fm[:], fmpack[:])

    def fmc(l, off, n=1):
        return fm[:, l * NFM + off: l * NFM + off + n]

    psA = [P.ps(f"psA{i}", [128, 512]) for i in range(4)]
    psB = [P.ps(f"psB{i}", [128, 512]) for i in range(3)]
    psT = P.ps("psT", [128, 1024], BF16)
    cnt = {"A": 0, "B": 0, "ev": 0, "w": 0}

    def nextA():
        cnt["A"] += 1
        return psA[cnt["A"] % 4]

    def nextB():
        cnt["B"] += 1
        return psB[cnt["B"] % 3]

    def ev_eng():
        cnt["ev"] += 1
        return "act" if cnt["ev"] % 2 else "dve"

    s = P.sb("s", [128, 8, GMAX], F32, parts=8)
    hn = P.sb("hn", [128, 8, GMAX], BF16, parts=8)
    rstd = P.sb("rstd", [128, GMAX], F32)
    ys = [P.sb(f"ys{n}", [128, 8, GMAX], BF16, parts=8) for n in range(4)]
    mfm = P.sb("mfm", [128, 8, GMAX], BF16, parts=8)
    NW = 3
    wbuf = [P.sb(f"wbuf{i}", [128, 8, 512], BF16) for i in range(NW)]
    arena_t = P.sb("arena", [128, arena_kb * 256], F32)
    ar = {"off": 0}

    def ar_reset():
        ar["off"] = 0

    def ar_alloc(name, shape, dtype=F32, parts=1):
        n = 1
        for d_ in shape[1:]:
            n *= d_
        nf = n if dtype == F32 else (n + 1) // 2
        nf = (nf + 7) // 8 * 8
        off = ar["off"]
        assert off + nf <= arena_kb * 256, (name, off, nf)
        ar["off"] = off + nf
        ap = arena_t.h[:, off:off + nf]
        if dtype != F32:
            ap = ap.bitcast(dtype)[:, 0:n]
        else:
            ap = ap[:, 0:n]
        if len(shape) == 3:
            ap = ap.rearrange("p (a b) -> p a b", a=shape[1])
        return Tile(name, ap, parts)

    def rowload(dst, l, off, n):
        P.dma(dst, View(rowpack, rowpack.h[l:l + 1, off:off + n].partition_broadcast(128), 0, 1))
    tmpf = [P.sb(f"tmpf{i}", [128, GMAX], F32) for i in range(8)]
    tcnt = [0]

    def tmp():
        tcnt[0] += 1
        return tmpf[tcnt[0] % 8]

    def wload(dram_view):
        cnt["w"] += 1
        wt = wbuf[cnt["w"] % NW]
        kt, n = dram_view.ap.shape[1], dram_view.ap.shape[2]
        P.dma(wt[:, 0:kt, 0:n], dram_view, eng="sp")
        return wt

    def wview(t, l_idx, c0, ncols, r0=0, nrows=1024):
        ap = t.h[l_idx + (slice(r0, r0 + nrows), slice(c0, c0 + ncols))]
        ap = ap.rearrange("(kt p) n -> p kt n", p=128)
        return View(t, ap, 0, t.nparts)

    ar_reset()
    sq = ar_alloc("sq", [128, 8, GMAX], F32, parts=8)
    sqb = ar_alloc("sqb", [128, 8, GMAX], BF16, parts=8)
    xtm = [ar_alloc(f"xtm{i}", [128, D], F32) for i in range(2)]
    macc = [ar_alloc(f"macc{i}", [128, GMAX], F32) for i in range(4)]

    def rmsnorm(G, wcol, outt, eps=1e-6):
        for kt in range(8):
            P.act(sqb[:, kt, 0:G], s[:, kt, 0:G], AF.Square)
        pp = nextA()
        for kt in range(8):
            P.mm(pp[:, 0:G], onesb, sqb[:, kt, 0:G], start=(kt == 0), stop=(kt == 7))
        P.act(rstd[:, 0:G], pp[:, 0:G], AF.Ln, scale=1.0 / D, bias=eps_t[:, 0:1])
        P.act(rstd[:, 0:G], rstd[:, 0:G], AF.Exp, scale=-0.5)
        for kt in range(8):
            P.stt(outt[:, kt, 0:G], s[:, kt, 0:G], wcol[:, kt:kt + 1], rstd[:, 0:G], ALU.mult, ALU.mult)

    onesf_t = P.sb("onesf", [128, 128], F32)
    P.memset(onesf_t[:], 1.0)
    onesb_f = onesf_t
    eps_t = P.sb("eps", [128, 2], F32)
    P.memset(eps_t[:, 0:1], 1e-6)
    P.memset(eps_t[:, 1:2], 1.0)

    def proj_fm(src, KT, wt_fn, ncols, G, evac):
        nblk = (ncols + 511) // 512
        for b in range(nblk):
            nb = min(512, ncols - b * 512)
            wt = wt_fn(b * 512, nb)
            for jj in range(nb // 128):
                pp = nextA()
                for kt in range(KT):
                    P.mm(pp[:, 0:G], wt[:, kt, jj * 128:(jj + 1) * 128], src[:, kt, 0:G],
                         start=(kt == 0), stop=(kt == KT - 1))
                evac(b * 4 + jj, pp[:, 0:G])

    def proj_tm(src, KT, wt_fn, ncols, G, evac):
        nblk = (ncols + 511) // 512
        for b in range(nblk):
            nb = min(512, ncols - b * 512)
            wt = wt_fn(b * 512, nb)
            for c in range(G // 128):
                pp = nextA()
                for kt in range(KT):
                    P.mm(pp[:, 0:nb], src[:, kt, c * 128:(c + 1) * 128], wt[:, kt, 0:nb],
                         start=(kt == 0), stop=(kt == KT - 1))
                evac(c, b, pp[:, 0:nb], nb)

    k.__dict__.update(locals())

    lru_tail = [P.sb(f"lru_tail{l}", [128, 8, 3], F32) for l in range(L)]
    lru_h = [P.sb(f"lru_h{l}", [128, 8], F32) for l in range(L)]
    lru_nsp = [P.sb(f"lru_nsp{l}", [128, 8], F32) for l in range(L)]
    ar_reset()
    lru_wgs = ar_alloc("lru_wgs", [128, 16, 128], BF16)
    lru_xcb = ar_alloc("lru_xcb", [128, GMAX], BF16)
    pin = ar_alloc("lru_pin", [128, 16, GMAX + 4], F32, parts=16)
    for l in range(L):
        P.memset(lru_tail[l][:], 0.0)
        P.memset(lru_h[l][:], 0.0)
        P.act(lru_nsp[l][:], fmc(l, C_LAM, 8), AF.Exp, scale=-1.0)
        P.act(lru_nsp[l][:], lru_nsp[l][:], AF.Ln, bias=eps_t[:, 1:2])
        P.ts(lru_nsp[l][:], lru_nsp[l][:], -8.0, ALU.mult)

    def gelu_tanh(outv, xv, G):
        t1 = tmp()
        P.act(t1[:, 0:G], xv, AF.Square)
        P.ts(t1[:, 0:G], t1[:, 0:G], 0.044715, ALU.mult, 1.0, ALU.add)
        P.tt(t1[:, 0:G], t1[:, 0:G], xv, ALU.mult)
        P.act(t1[:, 0:G], t1[:, 0:G], AF.Sigmoid, scale=1.5957691216057308)
        P.tt(outv, t1[:, 0:G], xv, ALU.mult)

    def lru_mix(l, g, G):
        def wt_fn(c0, nb):
            return wload(wview(wb_in[l], (), OFF_D + c0, nb))

        def evac(j, pp):
            if j < 8:
                P.copy(pin[:, j, 0:G], pp, eng=ev_eng())
            else:
                P.copy(pin[:, j, 3:3 + G], pp, eng=ev_eng())
        proj_fm(hn, 8, wt_fn, 2048, G, evac)
        P.dma(lru_wgs[:], View(lru_wg, lru_wg.h[l].rearrange("k n c e -> c (k n) e"), 0, 1), eng="pool")
        for n in range(8):
            xe = pin[:, 8 + n, :]
            P.copy(xe[:, 0:3], lru_tail[l][:, n, :])
            xc = tmp()
            lw = C_LCW + n * 4
            P.ts(xc[:, 0:G], xe[:, 0:G], fmc(l, lw), ALU.mult, fmc(l, C_LCB + n), ALU.add)
            for j in range(1, 4):
                P.stt(xc[:, 0:G], xe[:, j:j + G], fmc(l, lw + j), xc[:, 0:G], ALU.mult, ALU.add)
            P.copy(lru_tail[l][:, n, :], xe[:, G:G + 3])
            xcb16 = lru_xcb
            P.copy(xcb16[:, 0:G], xc[:, 0:G])
            gates = []
            for kk_ in range(2):
                pp = nextB()
                P.mm(pp[:, 0:G], lru_wgs[:, kk_ * 8 + n, :], xcb16[:, 0:G])
                gt = tmp()
                P.act(gt[:, 0:G], pp[:, 0:G], AF.Sigmoid, bias=fmc(l, C_BG + kk_ * 8 + n))
                gates.append(gt)
            rg, ig = gates
            la = tmp()
            P.ts(la[:, 0:G], rg[:, 0:G], lru_nsp[l][:, n:n + 1], ALU.mult)
            av = tmp()
            P.act(av[:, 0:G], la[:, 0:G], AF.Exp)
            P.act(la[:, 0:G], la[:, 0:G], AF.Exp, scale=2.0)
            P.ts(la[:, 0:G], la[:, 0:G], -1.0, ALU.mult, 1.0, ALU.add)
            P.act(la[:, 0:G], la[:, 0:G], AF.Sqrt)
            P.tt(ig[:, 0:G], ig[:, 0:G], xc[:, 0:G], ALU.mult)
            P.tt(ig[:, 0:G], ig[:, 0:G], la[:, 0:G], ALU.mult)
            if g == 0:
                P.memset(ig[:, 0:PAD], 0.0)
            P.scan(rg[:, 0:G], av[:, 0:G], ig[:, 0:G], lru_h[l][:, n:n + 1])
            P.copy(lru_h[l][:, n:n + 1], rg[:, G - 1:G])
            gelu_tanh(la[:, 0:G], pin[:, n, 0:G], G)
            P.tt(ys[3][:, n, 0:G], rg[:, 0:G], la[:, 0:G], ALU.mult)


    def zero_y(n, G):
        for kt in range(8):
            P.memset(ys[n][:, kt, 0:G], 0.0, eng="pool")

    mixers = {"a": None, "b": None, "c": None, "d": lru_mix}
    k.__dict__.update(locals())
    for nm, modfn in MIXER_BUILDERS.items():
        ar_reset()
        mixers[nm] = modfn(k)

    def layer(l, g, G):
        rmsnorm(G, fmc(l, C_NM, 8), hn)
        for n, nm in enumerate("abcd"):
            if nm in enable and mixers[nm] is not None:
                P.fence()
                mixers[nm](l, g, G)
            else:
                zero_y(n, G)
        P.fence()
        for db in range(2):
            for n in range(4):
                wb = wload(wview(wb_branch[l][n], (), db * 512, 512))
                wg = wload(wview(wb_in[l], (), OFF_G + n * 1024 + db * 512, 512))
                for dj in range(4):
                    pz = nextA()
                    for kt in range(8):
                        P.mm(pz[:, 0:G], wb[:, kt, dj * 128:(dj + 1) * 128], ys[n][:, kt, 0:G],
                             start=(kt == 0), stop=(kt == 7))
                    pg = nextA()
                    for kt in range(8):
                        P.mm(pg[:, 0:G], wg[:, kt, dj * 128:(dj + 1) * 128], hn[:, kt, 0:G],
                             start=(kt == 0), stop=(kt == 7))
                    sg = tmp()
                    P.act(sg[:, 0:G], pg[:, 0:G], AF.Sigmoid)
                    if n == 0:
                        P.tt(macc[dj][:, 0:G], sg[:, 0:G], pz[:, 0:G], ALU.mult)
                    else:
                        P.tt(sg[:, 0:G], sg[:, 0:G], pz[:, 0:G], ALU.mult)
                        P.tt(macc[dj][:, 0:G], macc[dj][:, 0:G], sg[:, 0:G], ALU.add, eng="pool")
            for dj in range(4):
                P.copy(mfm[:, db * 4 + dj, 0:G], macc[dj][:, 0:G])

        def evac_out(j, pp):
            P.tt(s[:, j, 0:G], s[:, j, 0:G], pp, ALU.add)
        proj_fm(mfm, 8, lambda c0, nb: wload(wview(wb_out[l], (), c0, nb)), 1024, G, evac_out)
        rmsnorm(G, fmc(l, C_NF, 8), hn)
        for hb in range(6):
            nb = min(512, FFH - hb * 512)
            wg_ = wload(wview(wb_ffi[l], (), hb * 512, nb))
            wu_ = wload(wview(wb_ffi[l], (), FFH + hb * 512, nb))
            for jj in range(nb // 128):
                pg = nextA()
                pu = nextA()
                for kt in range(8):
                    P.mm(pg[:, 0:G], wg_[:, kt, jj * 128:(jj + 1) * 128], hn[:, kt, 0:G], start=(kt == 0), stop=(kt == 7))
                for kt in range(8):
                    P.mm(pu[:, 0:G], wu_[:, kt, jj * 128:(jj + 1) * 128], hn[:, kt, 0:G], start=(kt == 0), stop=(kt == 7))
                sg = tmp()
                P.act(sg[:, 0:G], pg[:, 0:G], AF.Silu)
                ht_ = hb * 4 + jj
                P.tt(ys[ht_ // 8][:, ht_ % 8, 0:G], sg[:, 0:G], pu[:, 0:G], ALU.mult)
        for db in range(2):
            accs = [psA[i] for i in range(4)]
            for hb3, (h0, nh) in enumerate(((0, 8), (8, 8), (16, 6))):
                wt = wload(wview(wb_ffo[l], (), db * 512, 512, r0=h0 * 128, nrows=nh * 128))
                for dj in range(4):
                    for hh in range(nh):
                        ht = h0 + hh
                        P.mm(accs[dj][:, 0:G], wt[:, hh, dj * 128:(dj + 1) * 128], ys[ht // 8][:, ht % 8, 0:G],
                             start=(ht == 0), stop=(ht == 21))
            for dj in range(4):
                P.tt(s[:, db * 4 + dj, 0:G], s[:, db * 4 + dj, 0:G], accs[dj][:, 0:G], ALU.add)
        if g == 0:
            for kt in range(8):
                P.memset(s[:, kt, 0:PAD], 0.0, eng="pool")

    ngroups = (nchunks + nch_group - 1) // nch_group
    otm = xtm
    fin = sq
    for g in range(ngroups):
        c0 = g * nch_group
        c1 = min(nchunks, c0 + nch_group)
        G = (c1 - c0) * 128
        P.fence()
        for c in range(c0, c1):
            xt = xtm[c % 2]
            P.dma(xt[:], xin[c * 128:(c + 1) * 128, :])
            for kt in range(8):
                pp = nextB()
                P.tr(pp[:, 0:128], xt[:, kt * 128:(kt + 1) * 128], ident)
                P.copy(s[:, kt, (c - c0) * 128:(c - c0 + 1) * 128], pp[:, 0:128], eng=ev_eng())
        for l in range(L):
            layer(l, g, G)
        P.fence()
        rmsnorm(G, fm[:, L * NFM:L * NFM + 8], fin)
        for c in range(c0, c1):
            if c == 0:
                continue
            ot = otm[c % 2]
            for kt in range(8):
                pp = nextB()
                P.tr(pp[:, 0:128], fin[:, kt, (c - c0) * 128:(c - c0 + 1) * 128], ident)
                P.copy(ot[:, kt * 128:(kt + 1) * 128], pp[:, 0:128], eng=ev_eng())
            P.dma(out[(c - 1) * 128:c * 128, :], ot[:])
    if BUILD_ONLY:
        return nc
    P.emit()
    return nc


BUILD_ONLY = False
MIXER_BUILDERS = {}


def _fmcol(v):
    v = np.asarray(v, np.float32).reshape(-1, 128)
    return np.ascontiguousarray(v.T)


def prep_shared(inp, L, TP):
    f32 = np.float32
    sh = {}
    sh["w_in"] = np.ascontiguousarray(inp["w_in"][:L], f32)
    qk = inp["w_in"][:L, :, OFF_C:OFF_C + 1024].reshape(L, D, 16, 2, 32)
    sh["w_qksw"] = np.ascontiguousarray(qk[:, :, :, ::-1, :].reshape(L, D, 1024), f32)
    sh["w_branch"] = np.ascontiguousarray(inp["w_branch"][:L], f32)
    sh["w_out"] = np.ascontiguousarray(inp["w_out"][:L], f32)
    sh["w_ffn_in"] = np.ascontiguousarray(inp["w_ffn_in"][:L], f32)
    sh["w_ffn_out"] = np.ascontiguousarray(inp["w_ffn_out"][:L], f32)
    cols = []
    for l in range(L):
        vec = inp["rwkv_vec"][l]
        v0 = inp["rwkv_v0"][l - 1] if l > 0 else np.zeros(D, f32)
        parts = [_fmcol(inp["norm_mix"][l]), _fmcol(inp["norm_ffn"][l]), _fmcol(inp["rwkv_mu"][l]),
                 _fmcol(vec[0]), _fmcol(vec[1]), _fmcol(vec[2]), _fmcol(vec[3]),
                 _fmcol(inp["rwkv_rk"][l].reshape(-1)), _fmcol(v0)]
        scw = inp["ssm_conv_w"][l]
        parts.append(np.ascontiguousarray(scw.reshape(4, 16, 128).transpose(2, 1, 0)).reshape(128, 64))
        parts.append(_fmcol(inp["ssm_conv_b"][l]))
        lcw = inp["lru_conv_w"][l]
        parts.append(np.ascontiguousarray(lcw.reshape(4, 8, 128).transpose(2, 1, 0)).reshape(128, 32))
        parts.append(_fmcol(inp["lru_conv_b"][l]))
        parts.append(_fmcol(inp["lru_b_gate"][l].reshape(-1)))
        parts.append(_fmcol(inp["lru_lambda"][l]))
        parts.append(_fmcol(vec[4]))
        parts.append(_fmcol(vec[5]))
        cols.append(np.concatenate(parts, axis=1))
    cols.append(_fmcol(inp["final_norm"]))
    sh["fmpack"] = np.ascontiguousarray(np.concatenate(cols, axis=1), f32)
    assert sh["fmpack"].shape[1] == L * NFM + 8
    rows = []
    for l in range(L):
        vec = inp["rwkv_vec"][l]
        rows.append(np.concatenate([vec[4], vec[5], inp["ssm_norm"][l], inp["ssm_dt_bias"][l],
                                    inp["ssm_a_log"][l], inp["ssm_d"][l]]))
    sh["rowpack"] = np.ascontiguousarray(np.stack(rows), f32)
    sh["lru_wg"] = np.ascontiguousarray(inp["lru_w_gate"][:L], f32)
    i = np.arange(128)
    ident = np.eye(128, dtype=f32)
    triu = (i[:, None] <= i[None, :]).astype(f32)
    mstr = (i[:, None] > i[None, :]).astype(f32)
    spare = np.zeros((128, 128), f32)
    spare[:, 0] = (i >= PAD)
    sut = (i[:, None] < i[None, :]).astype(f32)
    blk = ((i[:, None] // 64) == (i[None, :] // 64)).astype(f32)
    sh["consts"] = np.ascontiguousarray(np.concatenate([ident, triu, mstr, spare, sut, blk, sut, triu, sut, triu], axis=1))
    pos = (np.arange(TP) - PAD).astype(f32)
    half = 32
    freq = np.power(np.float32(10000.0), -np.arange(half, dtype=f32) / half).astype(f32)
    ang = pos[None, :] * freq[:, None]
    cosf = np.cos(ang).astype(f32)
    sinf = np.sin(ang).astype(f32)
    ch = np.arange(128) % 64
    cos_t = cosf[ch % 32]
    sin_t = np.where((ch < 32)[:, None], -sinf[ch % 32], sinf[ch % 32])
    sh["ropetab"] = np.ascontiguousarray(np.stack([cos_t, sin_t], axis=1), f32)
    H = 8
    log_g = np.log1p(-np.exp2(-5.0 - np.arange(H, dtype=f32))).astype(f32)
    idx = np.arange(128, dtype=f32)
    rel = idx[None, :] - idx[:, None]
    intraT = np.where(rel >= 0, np.exp(np.where(rel >= 0, rel, 0.0)[None] * log_g[:, None, None]), 0.0)
    qdec = np.exp((idx + 1.0)[None, :] * log_g[:, None])
    kdec = np.exp((127.0 - idx)[None, :] * log_g[:, None])
    cdec = np.exp(128.0 * log_g)
    rt = np.zeros((128, 1548), f32)
    rt[:, 0:1024] = intraT.transpose(1, 0, 2).reshape(128, 1024)
    pp_ = np.arange(128) // 64
    for j in range(4):
        rt[:, 1024 + j * 128:1024 + (j + 1) * 128] = qdec[2 * j + pp_, :]
        rt[:, 1544 + j] = cdec[2 * j + pp_]
    rt[:, 1536:1544] = kdec.T
    sh["rettab"] = rt
    sh["rw_w2"] = np.ascontiguousarray(inp["rwkv_w2"][:L], f32)
    sh["rw_a2"] = np.ascontiguousarray(inp["rwkv_a2"][:L], f32)
    sh["rw_g2"] = np.ascontiguousarray(inp["rwkv_g2"][:L], f32)
    n1 = max(L - 1, 1)
    sh["rw_v1"] = np.ascontiguousarray(inp["rwkv_v1"][:n1], f32)
    sh["rw_v2"] = np.ascontiguousarray(inp["rwkv_v2"][:n1], f32)
    return sh


def run_module(inp, depth, nch_group, enable=("a", "b", "c", "d"), ncores=None, arena_kb=82):
    x = np.asarray(inp["x"], np.float32)
    B, S, _ = x.shape
    T = S + NMETA
    assert (T + PAD) % 128 == 0
    TP = T + PAD
    nchunks = TP // 128
    nc = build(nchunks, depth, nch_group, enable, arena_kb)
    sh = prep_shared(inp, depth, TP)
    meta = np.asarray(inp["meta"], np.float32)
    ncores = ncores or B
    in_maps = []
    for c in range(ncores):
        b = c % B
        xin = np.concatenate([np.zeros((PAD, D), np.float32), meta, x[b]], axis=0)
        m = dict(sh)
        m["xin"] = np.ascontiguousarray(xin)
        in_maps.append(m)
    res = run_bass_kernel_spmd(nc, in_maps, core_ids=list(range(ncores)))
    outs = [res.results[b]["out"] for b in range(B)]
    return np.stack(outs, axis=0)


def kernel(**inputs):
    return run_module(inputs, 4, NCH_GROUP, ncores=4, arena_kb=82)


NCH_GROUP = 2

RT_INTRA, RT_Q, RT_KDEC, RT_C, RT_N = 0, 1024, 1536, 1544, 1548


def build_ret(k):
    P, L, GMAX, NCHG = k.P, k.L, k.GMAX, k.nch_group
    hn, ys, w_in, w_qksw = k.hn, k.ys, k.w_in, k.w_qksw
    A = k.ar_alloc
    rt = A("rt", [128, RT_N], F32)
    rope_c = P.sb("rope_c", [128, GMAX], F32)
    rope_s = P.sb("rope_s", [128, GMAX], F32)
    pin = A("ret_pin", [128, 16, GMAX], F32, parts=16)
    qrT = A("qrT", [128, 8, GMAX], BF16, parts=8)
    qdT = A("qdT", [128, 4, GMAX], BF16, parts=4)
    vtm = A("ret_vtm", [128, NCHG, 1024], BF16, parts=NCHG)
    gtm = A("ret_gtm", [128, NCHG, 1024], F32, parts=NCHG)
    ktm = A("ret_ktm", [128, NCHG, 512], BF16, parts=NCHG)
    scm = [A(f"ret_scm{i}", [128, 8, 128], BF16) for i in range(2)]
    ysb = [A(f"ret_ysb{i}", [128, 8, 128], F32) for i in range(2)]
    ysq = A("ret_ysq", [128, 8, 128], F32)
    ssq = P.sb("ret_ssq", [128, 8], F32)
    Rst = [P.sb(f"ret_R{l}", [128, 4, 128], F32) for l in range(L)]
    Rb = P.sb("ret_Rb", [128, 4, 128], BF16)
    for l in range(L):
        P.memset(Rst[l][:], 0.0)

    def mix(l, g, G):
        c0 = g * NCHG
        nchk = G // 128
        P.dma(rt[:], k.rettab[:])
        if l == 0:
            P.dma(rope_c[:, 0:G], k.ropetab[:, 0, c0 * 128:c0 * 128 + G])
            P.dma(rope_s[:, 0:G], k.ropetab[:, 1, c0 * 128:c0 * 128 + G])

        def ev1(j, pp):
            P.copy(pin[:, j, 0:G], pp, eng=k.ev_eng())

        def ev2(j, pp):
            P.copy(pin[:, 8 + j, 0:G], pp, eng=k.ev_eng())
        k.proj_fm(hn, 8, lambda c, nb: k.wload(k.wview(k.wb_in[l], (), OFF_C + c, nb)), 1024, G, ev1)
        k.proj_fm(hn, 8, lambda c, nb: k.wload(k.wview(k.wb_qksw[l], (), c, nb)), 1024, G, ev2)
        for j in range(8):
            cc, ss = rope_c, rope_s
            t1 = k.tmp()
            t2 = k.tmp()
            P.tt(t1[:, 0:G], pin[:, j, 0:G], cc[:, 0:G], ALU.mult)
            P.tt(t2[:, 0:G], pin[:, 8 + j, 0:G], ss[:, 0:G], ALU.mult, eng="pool")
            if j < 4:
                P.tt(t1[:, 0:G], t1[:, 0:G], t2[:, 0:G], ALU.add)
                P.copy(qrT[:, j, 0:G], t1[:, 0:G], eng="act")
                for c in range(nchk):
                    P.tt(qdT[:, j, c * 128:(c + 1) * 128], t1[:, c * 128:(c + 1) * 128],
                         rt[:, RT_Q + j * 128:RT_Q + (j + 1) * 128], ALU.mult)
            else:
                P.tt(t1[:, 0:G], t1[:, 0:G], t2[:, 0:G], ALU.add)
                P.act(qrT[:, j, 0:G], t1[:, 0:G], AF.Copy, scale=0.125)

        def ev3(c, b, pp, nb):
            if b < 2:
                P.copy(vtm[:, c, b * 512:(b + 1) * 512], pp, eng=k.ev_eng())
            else:
                P.act(gtm[:, c, (b - 2) * 512:(b - 1) * 512], pp, AF.Silu)
        k.proj_tm(hn, 8, lambda c, nb: k.wload(k.wview(k.wb_in[l], (), OFF_C + 1024 + c, nb)), 2048, G, ev3)

        for c in range(nchk):
            cs = slice(c * 128, (c + 1) * 128)
            for j in range(4):
                pt = k.psT
                P.tr(pt[:, j * 128:(j + 1) * 128], qrT[:, 4 + j, cs], k.identb)
            for h in range(8):
                P.ts(ktm[:, c, h * 64:(h + 1) * 64], k.psT[:, h * 64:(h + 1) * 64],
                     rt[:, RT_KDEC + h:RT_KDEC + h + 1], ALU.mult, eng="dve")
            sc = scm[c % 2]
            for hb in range(2):
                pp = k.nextB()
                for hh in range(4):
                    h = hb * 4 + hh
                    j, pb = h // 2, 64 * (h % 2)
                    P.mm(pp[:, hh * 128:(hh + 1) * 128], qrT[pb:pb + 64, 4 + j, cs], qrT[pb:pb + 64, j, cs])
                P.tt(sc[:, hb * 4:(hb + 1) * 4, :], pp[:, :].re(lambda a: a.rearrange("p (h l) -> p h l", h=4)),
                     rt[:, RT_INTRA + hb * 512:RT_INTRA + (hb + 1) * 512].re(
                         lambda a: a.rearrange("p (h l) -> p h l", h=4)), ALU.mult)
            P.copy(Rb[:], Rst[l][:], eng="act")
            yb = ysb[c % 2]
            for hb in range(2):
                pp = k.nextB()
                for hh in range(4):
                    h = hb * 4 + hh
                    j, pb = h // 2, 64 * (h % 2)
                    P.mm(pp[:, hh * 128:(hh + 1) * 128], sc[:, h, :], vtm[:, c, h * 128:(h + 1) * 128],
                         start=True, stop=False)
                    P.mm(pp[:, hh * 128:(hh + 1) * 128], qdT[pb:pb + 64, j, cs], Rb[pb:pb + 64, j, :],
                         start=False, stop=True)
                P.copy(yb[:, hb * 4:(hb + 1) * 4, :], pp[:, :].re(lambda a: a.rearrange("p (h l) -> p h l", h=4)),
                       eng="act")
            for j in range(4):
                pp = k.nextB()
                for hp in range(2):
                    h = 2 * j + hp
                    P.mm(pp[:, hp * 128:(hp + 1) * 128], ktm[:, c, j * 128:(j + 1) * 128],
                         vtm[:, c, h * 128:(h + 1) * 128])
                for hp in range(2):
                    pb = 64 * hp
                    P.stt(Rst[l][pb:pb + 64, j, :], Rst[l][pb:pb + 64, j, :], rt[pb:pb + 64, RT_C + j:RT_C + j + 1],
                          pp[pb:pb + 64, hp * 128:(hp + 1) * 128], ALU.mult, ALU.add)
            P.tt(ysq[:], yb[:], yb[:], ALU.mult, eng="pool")
            P.red(ssq[:], ysq[:])
            P.ts(ssq[:], ssq[:], 1.0 / 128, ALU.mult, 1e-6, ALU.add)
            P.act(ssq[:], ssq[:], AF.Ln)
            P.act(ssq[:], ssq[:], AF.Exp, scale=-0.5)
            P.tt(yb[:], yb[:], ssq[:].re(lambda a: a.unsqueeze(2).broadcast_to([128, 8, 128])), ALU.mult)
            P.tt(yb[:], yb[:], gtm[:, c, :].re(lambda a: a.rearrange("p (h e) -> p h e", h=8)), ALU.mult)
            for kt in range(8):
                pp = k.nextB()
                P.tr(pp[:, 0:128], yb[:, kt, :], k.ident)
                P.copy(ys[2][:, kt, cs], pp[:, 0:128], eng=k.ev_eng())
    return mix


MIXER_BUILDERS["c"] = build_ret

def build_ssd(k):
    P, L, GMAX, NCHG = k.P, k.L, k.GMAX, k.nch_group
    hn, ys, w_in = k.hn, k.ys, k.w_in
    A = k.ar_alloc
    triU = k.cst[:, 128:256]
    mstr = k.cst[:, 256:384]
    xe = A("ssd_xe", [128, 16, GMAX + 4], F32, parts=16)
    ztm = A("ssd_ztm", [128, NCHG, 1024], F32, parts=NCHG)
    dtm = A("ssd_dtm", [128, NCHG, 16], F32, parts=NCHG)
    rowb = A("ssd_rowb", [128, 1072], F32)
    xsf = A("ssd_xsf", [128, 8, GMAX], F32, parts=8)
    bfm = A("ssd_bfm", [128, 4, GMAX], BF16, parts=4)
    cfm = A("ssd_cfm", [128, 4, GMAX], BF16, parts=4)
    btm = A("ssd_btm", [128, 4, 128], BF16)
    xs_tm = A("ssd_xstm", [128, 16, 64], F32)
    xdt = A("ssd_xdt", [128, 16, 64], BF16)
    xdt2 = A("ssd_xdt2", [128, 16, 64], BF16)
    rhsL = A("ssd_rhsL", [128, 16, 128], F32)
    Lm = A("ssd_Lm", [128, 16, 128], F32)
    Wm = A("ssd_Wm", [128, 16, 128], BF16)
    cbm = A("ssd_cbm", [128, 4, 128], F32)
    yt = A("ssd_yt", [128, 16, 64], F32)
    yt2 = A("ssd_yt2", [128, 16, 64], F32)
    sm = P.sb("ssd_small", [128, 8, 16], F32)
    ssq = P.sb("ssd_ssq", [128, 4], F32)
    St = [P.sb(f"ssd_S{l}", [128, 16, 64], F32) for l in range(L)]
    Sb = P.sb("ssd_Sb", [128, 16, 64], BF16)
    tails = [P.sb(f"ssd_tail{l}", [128, 16, 3], F32) for l in range(L)]
    for l in range(L):
        P.memset(St[l][:], 0.0)
        P.memset(tails[l][:], 0.0)

    def bc(v, n):
        return v.re(lambda a: a.unsqueeze(2).broadcast_to([128, a.shape[1], n]))

    def mix(l, g, G):
        nchk = G // 128
        k.rowload(rowb[:], l, R_SSMN, 1072)
        nw_b = rowb[:, 0:1024]
        dtb_b = rowb[:, 1024:1040]
        alog_b = rowb[:, 1040:1056]
        dsk_b = rowb[:, 1056:1072]
        P.act(sm[:, 6, :], alog_b, AF.Exp)
        P.ts(sm[:, 6, :], sm[:, 6, :], -1.0, ALU.mult)

        def ev_z(c, b, pp, nb):
            P.act(ztm[:, c, b * 512:(b + 1) * 512], pp, AF.Silu)
        k.proj_tm(hn, 8, lambda c, nb: k.wload(k.wview(k.wb_in[l], (), OFF_B + c, nb)), 1024, G, ev_z)

        def ev_x(j, pp):
            P.copy(xe[:, j, 3:3 + G], pp, eng=k.ev_eng())
        k.proj_fm(hn, 8, lambda c, nb: k.wload(k.wview(k.wb_in[l], (), OFF_B + 1024 + c, nb)), 2048, G, ev_x)

        def ev_dt(c, b, pp, nb):
            P.tt(dtm[:, c, :], pp, dtb_b, ALU.add)
        k.proj_tm(hn, 8, lambda c, nb: k.wload(k.wview(k.wb_in[l], (), OFF_B + 3072 + c, nb)), 16, G, ev_dt)
        for j in range(16):
            P.copy(xe[:, j, 0:3], tails[l][:, j, :])
            xc = k.tmp()
            cw = C_SCW + j * 4
            P.ts(xc[:, 0:G], xe[:, j, 0:G], k.fmc(l, cw), ALU.mult, k.fmc(l, C_SCB + j), ALU.add)
            for t in range(1, 4):
                P.stt(xc[:, 0:G], xe[:, j, t:t + G], k.fmc(l, cw + t), xc[:, 0:G], ALU.mult, ALU.add)
            P.copy(tails[l][:, j, :], xe[:, j, G:G + 3])
            if j < 8:
                P.act(xsf[:, j, 0:G], xc[:, 0:G], AF.Silu)
            elif j < 12:
                P.act(bfm[:, j - 8, 0:G], xc[:, 0:G], AF.Silu)
            else:
                P.act(cfm[:, j - 12, 0:G], xc[:, 0:G], AF.Silu)

        for c in range(nchk):
            cs = slice(c * 128, (c + 1) * 128)
            dt = sm[:, 0, :]
            dA = sm[:, 1, :]
            ea = sm[:, 2, :]
            dec = sm[:, 3, :]
            etot = sm[:, 4, :]
            dtdec = sm[:, 5, :]
            P.act(dt, dtm[:, c, :], AF.Exp)
            P.act(dt, dt, AF.Ln, bias=k.eps_t[:, 1:2])
            if g == 0 and c == 0:
                P.ts(dt, dt, k.cst[:, 384:385], ALU.mult)
            P.tt(dA, dt, sm[:, 6, :], ALU.mult)
            pc = k.nextB()
            P.mm(pc[:, 0:16], triU, dA)
            P.mm(pc[:, 16:32], k.onesf_t[:], dA)
            P.act(ea, pc[:, 0:16], AF.Exp)
            P.copy(sm[:, 7, :], pc[:, 0:16])
            P.tt(dec, pc[:, 16:32], sm[:, 7, :], ALU.subtract)
            P.act(dec, dec, AF.Exp)
            P.act(etot, pc[:, 16:32], AF.Exp)
            P.tt(dtdec, dt, dec, ALU.mult)
            for hb in range(2):
                pp = k.nextA()
                for jj in range(4):
                    P.tr(pp[:, jj * 128:(jj + 1) * 128], xsf[:, hb * 4 + jj, cs], k.ident)
                P.copy(xs_tm[:, hb * 8:(hb + 1) * 8, :], pp[:, :].re(lambda a: a.rearrange("p (h e) -> p h e", h=8)),
                       eng="act")
            P.tt(xdt[:], xs_tm[:], bc(dt, 64), ALU.mult)
            P.tt(xdt2[:], xs_tm[:], bc(dtdec, 64), ALU.mult, eng="pool")
            for gi in range(4):
                P.tr(k.psT[:, gi * 128:(gi + 1) * 128], bfm[:, gi, cs], k.identb)
            P.copy(btm[:], k.psT[:, 0:512].re(lambda a: a.rearrange("p (g n) -> p g n", g=4)), eng="act")
            P.tt(rhsL[:], triU.re(lambda a: a.unsqueeze(1).broadcast_to([128, 16, 128])), bc(dA, 128), ALU.mult)
            pcb = k.nextB()
            for gi in range(4):
                P.mm(pcb[:, gi * 128:(gi + 1) * 128], bfm[:, gi, cs], cfm[:, gi, cs])
            P.tt(cbm[:], pcb[:, :].re(lambda a: a.rearrange("p (g l) -> p g l", g=4)),
                 triU.re(lambda a: a.unsqueeze(1).broadcast_to([128, 4, 128])), ALU.mult)
            for gi in range(4):
                pl = k.nextA()
                P.mm(pl[:, :], mstr, rhsL[:, gi * 4:(gi + 1) * 4, :].re(lambda a: a.rearrange("p h l -> p (h l)")))
                P.act(Lm[:, gi * 4:(gi + 1) * 4, :], pl[:, :].re(lambda a: a.rearrange("p (h l) -> p h l", h=4)), AF.Exp)
                P.tt(Wm[:, gi * 4:(gi + 1) * 4, :], Lm[:, gi * 4:(gi + 1) * 4, :],
                     cbm[:, gi, :].re(lambda a: a.unsqueeze(1).broadcast_to([128, 4, 128])), ALU.mult)
            P.copy(Sb[:], St[l][:], eng="act")
            for hb in range(2):
                py = k.nextA()
                for hh in range(8):
                    h = hb * 8 + hh
                    P.mm(py[:, hh * 64:(hh + 1) * 64], Wm[:, h, :], xdt[:, h, :])
                po = k.nextB()
                for gg in range(2):
                    gi = hb * 2 + gg
                    P.mm(po[:, gg * 256:(gg + 1) * 256], cfm[:, gi, cs],
                         Sb[:, gi * 4:(gi + 1) * 4, :].re(lambda a: a.rearrange("p h e -> p (h e)")))
                hs = slice(hb * 8, (hb + 1) * 8)
                P.tt(yt[:, hs, :], po[:, :].re(lambda a: a.rearrange("p (h e) -> p h e", h=8)),
                     bc(sm[:, 2, hb * 8:(hb + 1) * 8], 64), ALU.mult)
                P.tt(yt[:, hs, :], yt[:, hs, :], py[:, :].re(lambda a: a.rearrange("p (h e) -> p h e", h=8)), ALU.add)
            P.tt(yt2[:], xs_tm[:], bc(dsk_b, 64), ALU.mult, eng="pool")
            P.tt(yt[:], yt[:], yt2[:], ALU.add)
            P.tt(yt[:], yt[:], ztm[:, c, :].re(lambda a: a.rearrange("p (h e) -> p h e", h=16)), ALU.mult)
            for hb in range(2):
                pst = k.nextB()
                for gg in range(2):
                    gi = hb * 2 + gg
                    P.mm(pst[:, gg * 256:(gg + 1) * 256], btm[:, gi, :],
                         xdt2[:, gi * 4:(gi + 1) * 4, :].re(lambda a: a.rearrange("p h e -> p (h e)")))
                hs = slice(hb * 8, (hb + 1) * 8)
                P.tt(St[l][:, hs, :], St[l][:, hs, :], bc(sm[:, 4, hb * 8:(hb + 1) * 8], 64), ALU.mult)
                P.tt(St[l][:, hs, :], St[l][:, hs, :],
                     pst[:, :].re(lambda a: a.rearrange("p (h e) -> p h e", h=8)), ALU.add)
            P.tt(yt2[:], yt[:], yt[:], ALU.mult, eng="pool")
            P.red(ssq[:], yt2[:].re(lambda a: a.rearrange("p (g e) n -> p g (e n)", g=4)))
            P.ts(ssq[:], ssq[:], 1.0 / 256, ALU.mult, 1e-5, ALU.add)
            P.act(ssq[:], ssq[:], AF.Ln)
            P.act(ssq[:], ssq[:], AF.Exp, scale=-0.5)
            ytg = yt[:].re(lambda a: a.rearrange("p (g e) n -> p g (e n)", g=4))
            P.tt(ytg, ytg, bc(ssq[:], 256), ALU.mult)
            ytf = yt[:].re(lambda a: a.rearrange("p h e -> p (h e)"))
            P.tt(ytf, ytf, nw_b, ALU.mult)
            for kt in range(8):
                pp = k.nextA()
                P.tr(pp[:, 0:128], ytf[:, kt * 128:(kt + 1) * 128], k.ident)
                P.copy(ys[1][:, kt, cs], pp[:, 0:128], eng=k.ev_eng())
    return mix


MIXER_BUILDERS["b"] = build_ssd

def build_rwkv(k):
    P, L, GMAX, NCHG = k.P, k.L, k.GMAX, k.nch_group
    hn, ys, w_in = k.hn, k.ys, k.w_in
    A = k.ar_alloc
    cst = k.cst
    ident, identb = k.ident, k.identb
    mstr = cst[:, 256:384]
    blkb_t = P.sb("rw_blkb", [128, 128], BF16)
    P.copy(blkb_t[:], cst[:, 640:768])
    blk = blkb_t[:]
    sqb2 = [A(f"rw_sqb{i}", [128, 2, 128], BF16) for i in range(2)]
    mask4 = cst[:, 768:1280]
    pa = A("rw_pa", [128, 26, GMAX + 1], F32, parts=26)
    w2s = A("rw_w2s", [128, 1024], BF16)
    a2s = A("rw_a2s", [128, 1024], BF16)
    g2s = A("rw_g2s", [128, 1024], BF16)
    v1s = A("rw_v1s", [128, 8, 32], BF16)
    v2s = A("rw_v2s", [128, 1024], BF16)
    twb = A("rw_twb", [128, 128], BF16)
    adb = A("rw_adb", [128, 128], BF16)
    sgb = A("rw_sgb", [128, 128], BF16)
    bt3 = [A(f"rw_bt3{i}", [128, 3, 128], BF16) for i in range(2)]
    gfm = A("rw_gfm", [128, 8, 128], BF16, parts=8)
    bonus = A("rw_bonus", [128, 8, 128], BF16, parts=8)
    off_alias = k.ar["off"]
    vb16 = A("rw_vb16", [128, 8, GMAX], BF16, parts=8)
    vvb = A("rw_vvb", [128, GMAX], BF16)
    k.ar["off"] = off_alias
    ar = A("rw_ar", [128, 8, 256], BF16, parts=8)
    bet = A("rw_bet", [128, 8, 128], BF16, parts=8)
    ktl = A("rw_ktl", [128, 8, 128], BF16, parts=8)
    tm3 = A("rw_tm3", [128, 3, 1024], BF16)
    btm, ktm, vtm = tm3[:, 0, :], tm3[:, 1, :], tm3[:, 2, :]
    AT = [A(f"rw_AT{i}", [128, 4, 512], BF16, parts=4) for i in range(1)] * 2
    P0 = [A(f"rw_P0{i}", [128, 4, 128], BF16) for i in range(1)] * 2
    QP = [A(f"rw_QP{i}", [128, 4, 256], BF16) for i in range(2)]
    Xb = [A(f"rw_Xb{i}", [128, 4, 64], BF16) for i in range(2)]
    Ub = A("rw_Ub", [128, 16, 64], BF16)
    ysb = A("rw_ysb", [128, 16, 64], F32)
    ysq = A("rw_ysq", [128, 16, 64], F32)
    elast = P.sb("rw_elast", [128, 8], F32)
    st4 = P.sb("rw_st4", [128, 4, 16], F32)
    nw0 = P.sb("rw_nw0", [128, 8], F32)
    na0 = P.sb("rw_na0", [128, 8], F32)
    nv0 = P.sb("rw_nv0", [128, 8], F32)
    cm = P.sb("rw_cm", [128, 2], F32)
    P.memset(cm[:, 0:1], -0.5)
    P.memset(cm[:, 1:2], 1e-24)
    vfirst = P.sb("rw_vfirst", [128, 8, GMAX], F32, parts=8)
    Hst = [P.sb(f"rw_H{l}", [128, 8, 64], F32) for l in range(L)]
    H0b = P.sb("rw_H0b", [128, 8, 64], BF16)
    shtail = [P.sb(f"rw_sht{l}", [128, 26], F32) for l in range(L)]
    for l in range(L):
        P.memset(Hst[l][:], 0.0)
        P.memset(shtail[l][:], 0.0)

    def bc(v, n):
        return v.re(lambda a: a.unsqueeze(2).broadcast_to([128, a.shape[1], n]))

    rtp = [A(f"rw_tmp{i}", [128, 128], F32) for i in range(12)]
    rtc = [0]

    def ctmp():
        rtc[0] += 1
        return rtp[rtc[0] % 12]

    def mix(l, g, G):
        nchk = G // 128
        fmc = k.fmc

        def ev(j, pp):
            P.copy(pa[:, j, 1:1 + G], pp, eng=k.ev_eng())
        k.proj_fm(hn, 8, lambda c, nb: k.wload(k.wview(k.wb_in[l], (), OFF_A + c, nb)), 3328, G, ev)
        P.dma(w2s[0:64, :], View(k.rw_w2, k.rw_w2.h[l], 0, 1), eng="pool")
        P.dma(a2s[64:128, :], View(k.rw_a2, k.rw_a2.h[l], 0, 1), eng="pool")
        P.dma(g2s[:, :], View(k.rw_g2, k.rw_g2.h[l], 0, 1), eng="pool")
        if l > 0:
            P.dma(v1s[:], View(k.rw_v1, k.rw_v1.h[l - 1].rearrange("(kt p) r -> p kt r", p=128), 0, 1), eng="pool")
            P.dma(v2s[0:32, :], View(k.rw_v2, k.rw_v2.h[l - 1], 0, 1), eng="pool")
        P.ts(nw0[:], fmc(l, C_W0, 8), -1.0, ALU.mult)
        P.ts(na0[:], fmc(l, C_A0, 8), -1.0, ALU.mult)
        P.ts(nv0[:], fmc(l, C_V0, 8), -1.0, ALU.mult)
        P.copy(pa[:, :, 0], shtail[l][:])
        P.copy(shtail[l][:], pa[:, :, G])
        for j in range(26):
            d = k.tmp()
            P.tt(d[:, 0:G], pa[:, j, 0:G], pa[:, j, 1:G + 1], ALU.subtract)
            P.stt(pa[:, j, 1:G + 1], d[:, 0:G], fmc(l, C_MU + j), pa[:, j, 1:G + 1], ALU.mult, ALU.add)
        if l == 0:
            for j in range(8):
                P.copy(vfirst[:, j, 0:G], pa[:, 16 + j, 1:G + 1], eng=k.ev_eng())
        else:
            for j in range(8):
                P.copy(vb16[:, j, 0:G], pa[:, 16 + j, 1:G + 1], eng=k.ev_eng())
            pv = k.nextA()
            for kt in range(8):
                P.mm(pv[0:32, 0:G], v1s[:, kt, :], vb16[:, kt, 0:G], start=(kt == 0), stop=(kt == 7))
            P.copy(vvb[0:32, 0:G], pv[0:32, 0:G])
            for j in range(8):
                pp = k.nextA()
                P.mm(pp[:, 0:G], v2s[0:32, j * 128:(j + 1) * 128], vvb[0:32, 0:G])
                sg = k.tmp()
                P.act(sg[:, 0:G], pp[:, 0:G], AF.Exp, scale=-1.0, bias=nv0[:, j:j + 1])
                P.ts(sg[:, 0:G], sg[:, 0:G], 1.0, ALU.add)
                P.recip(sg[:, 0:G], sg[:, 0:G])
                d = k.tmp()
                vcur = pa[:, 16 + j, 1:G + 1]
                P.tt(d[:, 0:G], vfirst[:, j, 0:G], vcur, ALU.subtract)
                P.tt(d[:, 0:G], d[:, 0:G], sg[:, 0:G], ALU.mult)
                P.tt(vcur, vcur, d[:, 0:G], ALU.add)

        P.fence()
        for c in range(nchk):
            cs = slice(c * 128, (c + 1) * 128)
            c1 = slice(1 + c * 128, 1 + (c + 1) * 128)
            tq = ctmp()
            P.act(tq[0:64, :], pa[0:64, 24, c1], AF.Exp, scale=2.0)
            P.ts(tq[0:64, :], tq[0:64, :], 1.0, ALU.add)
            P.recip(tq[0:64, :], tq[0:64, :])
            P.ts(twb[0:64, :], tq[0:64, :], -2.0, ALU.mult, 1.0, ALU.add)
            P.copy(adb[64:128, :], pa[64:128, 24, c1])
            tq2 = ctmp()
            P.act(tq2[:, :], pa[:, 25, c1], AF.Exp, scale=-1.0)
            P.ts(tq2[:, :], tq2[:, :], 1.0, ALU.add)
            P.recip(tq2[:, :], tq2[:, :])
            P.copy(sgb[:, :], tq2[:, :], eng="pool")
            for j in range(8):
                js = slice(j * 128, (j + 1) * 128)
                ed_j, av_j, kk_j = ctmp()[:, 0:128], ctmp()[:, 0:128], ctmp()[:, 0:128]
                b3 = bt3[j % 2]
                r_, k_, v_ = pa[:, j, c1], pa[:, 8 + j, c1], pa[:, 16 + j, c1]
                pw = k.nextA()
                P.mm(pw[:, 0:128], w2s[0:64, js], twb[0:64, :])
                P.mm(pw[:, 128:256], a2s[64:128, js], adb[64:128, :])
                P.mm(pw[:, 256:384], g2s[:, js], sgb[:, :])
                t1 = ctmp()
                P.act(t1[:, 0:128], pw[:, 0:128], AF.Exp, scale=-1.0, bias=nw0[:, j:j + 1])
                P.act(t1[:, 0:128], t1[:, 0:128], AF.Ln, bias=k.eps_t[:, 1:2])
                P.act(ed_j, t1[:, 0:128], AF.Exp, scale=-1.0, bias=cm[:, 0:1])
                P.act(av_j, pw[:, 128:256], AF.Exp, scale=-1.0, bias=na0[:, j:j + 1])
                P.ts(av_j, av_j, 1.0, ALU.add)
                P.recip(av_j, av_j)
                P.copy(gfm[:, j, :], pw[:, 256:384])
                kq = ctmp()
                P.ts(kq[:, 0:128], k_, fmc(l, C_KK + j), ALU.mult)
                sq2 = sqb2[j % 2]
                P.tt(sq2[:, 0, :], kq[:, 0:128], kq[:, 0:128], ALU.mult, eng="pool")
                pn = k.nextB()
                P.mm(pn[:, 0:128], blk, sq2[:, 0, :])
                rn = ctmp()
                P.act(rn[:, 0:128], pn[:, 0:128], AF.Ln, bias=cm[:, 1:2])
                P.act(rn[:, 0:128], rn[:, 0:128], AF.Exp, scale=-0.5)
                P.tt(kk_j, kq[:, 0:128], rn[:, 0:128], ALU.mult)
                t2 = ctmp()
                P.ts(t2[:, 0:128], av_j, -1.0, ALU.add, fmc(l, C_KA + j), ALU.mult)
                P.stt(k_, t2[:, 0:128], 1.0, k_, ALU.add, ALU.mult)
                P.stt(sq2[:, 1, :], r_, fmc(l, C_RK + j), k_, ALU.mult, ALU.mult)
                P.mm(pn[:, 128:256], blk, sq2[:, 1, :])
                P.tt(bonus[:, j, :], pn[:, 128:256], v_, ALU.mult)
                P.copy(b3[:, 2, :], v_, eng="act")
                cw = ctmp()
                P.scan(cw[:, 0:128], k.onesf_t[:, 0:128], ed_j, 0.0)
                e_in = ctmp()
                P.act(e_in[:, 0:128], cw[:, 0:128], AF.Exp, scale=-1.0)
                P.copy(elast[:, j:j + 1], e_in[:, 127:128])
                e_inv = ctmp()
                P.act(e_inv[:, 0:128], cw[:, 0:128], AF.Exp)
                e_ex = ctmp()
                P.tt(e_ex[:, 0:128], cw[:, 0:128], ed_j, ALU.subtract)
                P.act(e_ex[:, 0:128], e_ex[:, 0:128], AF.Exp, scale=-1.0)
                P.stt(ar[:, j, 0:128], kk_j, -1.0, e_ex[:, 0:128], ALU.mult, ALU.mult)
                P.tt(ar[:, j, 128:256], r_, e_in[:, 0:128], ALU.mult)
                P.tt(t2[:, 0:128], kk_j, av_j, ALU.mult, eng="pool")
                P.tt(bet[:, j, :], t2[:, 0:128], e_inv[:, 0:128], ALU.mult)
                P.tt(ktl[:, j, :], k_, e_inv[:, 0:128], ALU.mult)
                P.ts(b3[:, 0, :], bet[:, j, :], elast[:, j:j + 1], ALU.mult)
                P.ts(b3[:, 1, :], ktl[:, j, :], elast[:, j:j + 1], ALU.mult)
                for q in range(3):
                    P.tr(k.psT[:, q * 128:(q + 1) * 128], b3[:, q, :], identb)
                P.copy(tm3[:, :, js], k.psT[:, 0:384].re(lambda a: a.rearrange("p (q c) -> p q c", q=3)),
                       eng=k.ev_eng())
            P.copy(H0b[:], Hst[l][:])
            for hb in range(4):
                at = AT[hb % 2]
                p0 = P0[hb % 2]
                pp0 = k.nextB()
                for hh in range(4):
                    h = hb * 4 + hh
                    j, pb = h // 2, 64 * (h % 2)
                    b1 = k.nextA()
                    P.mm(b1[:, 0:256], bet[pb:pb + 64, j, :], ar[pb:pb + 64, j, :])
                    P.mm(b1[:, 256:512], ktl[pb:pb + 64, j, :], ar[pb:pb + 64, j, :])
                    P.tt(at[:, hh, :], b1[:, :], mask4, ALU.mult)
                    P.mm(pp0[:, hh * 128:(hh + 1) * 128], ar[pb:pb + 64, j, 0:128], bet[pb:pb + 64, j, :])
                P.tt(p0[:], pp0[:, :].re(lambda a: a.rearrange("p (h s) -> p h s", h=4)),
                     mstr.re(lambda a: a.unsqueeze(1).broadcast_to([128, 4, 128])), ALU.mult)
                px = k.nextB()
                for hh in range(4):
                    h = hb * 4 + hh
                    j, pb = h // 2, 64 * (h % 2)
                    P.mm(px[:, hh * 64:(hh + 1) * 64], ar[pb:pb + 64, j, 0:128], H0b[pb:pb + 64, j, :],
                         start=True, stop=False)
                    P.mm(px[:, hh * 64:(hh + 1) * 64], at[:, hh, 256:384], vtm[:, h * 64:(h + 1) * 64],
                         start=False, stop=True)
                xcur = Xb[0]
                P.copy(xcur[:], px[:, 0:256].re(lambda a: a.rearrange("p (h v) -> p h v", h=4)), eng="act")
                for step in range(7):
                    def Q(hh):
                        return at[:, hh, 0:128] if step == 0 else QP[(step - 1) % 2][:, hh, 0:128]

                    def Pm(hh):
                        return p0[:, hh, :] if step == 0 else QP[(step - 1) % 2][:, hh, 128:256]
                    px = k.nextB()
                    for hh in range(4):
                        P.mm(px[:, hh * 64:(hh + 1) * 64], Q(hh), xcur[:, hh, :])
                    if step < 6:
                        xn = Xb[(step + 1) % 2]
                        P.tt(xn[:], px[:, 0:256].re(lambda a: a.rearrange("p (h v) -> p h v", h=4)), xcur[:], ALU.add)
                        xcur = xn
                        qn = QP[step % 2]
                        for half in range(2):
                            pq = k.nextA()
                            for h2 in range(2):
                                hh = half * 2 + h2
                                P.mm(pq[:, h2 * 256:h2 * 256 + 128], Pm(hh), Q(hh))
                                P.mm(pq[:, h2 * 256 + 128:h2 * 256 + 256], Q(hh), Pm(hh))
                            P.copy(qn[:, half * 2:half * 2 + 2, :],
                                   pq[:, :].re(lambda a: a.rearrange("p (h s) -> p h s", h=2)), eng="act")
                    else:
                        P.tt(Ub[:, hb * 4:(hb + 1) * 4, :],
                             px[:, 0:256].re(lambda a: a.rearrange("p (h v) -> p h v", h=4)), xcur[:], ALU.add)
                py = k.nextB()
                for hh in range(4):
                    h = hb * 4 + hh
                    j, pb = h // 2, 64 * (h % 2)
                    P.mm(py[:, hh * 64:(hh + 1) * 64], ar[pb:pb + 64, j, 128:256], H0b[pb:pb + 64, j, :],
                         start=True, stop=False)
                    P.mm(py[:, hh * 64:(hh + 1) * 64], at[:, hh, 128:256], Ub[:, h, :], start=False, stop=False)
                    P.mm(py[:, hh * 64:(hh + 1) * 64], at[:, hh, 384:512], vtm[:, h * 64:(h + 1) * 64],
                         start=False, stop=True)
                P.copy(ysb[:, hb * 4:(hb + 1) * 4, :], py[:, 0:256].re(lambda a: a.rearrange("p (h v) -> p h v", h=4)),
                       eng="dve")
            for j in range(8):
                ph = k.nextA()
                P.mm(ph[:, 0:128], btm[:, j * 128:(j + 1) * 128],
                     Ub[:, 2 * j:2 * j + 2, :].re(lambda a: a.rearrange("p h v -> p (h v)")), start=True, stop=False)
                P.mm(ph[:, 0:128], ktm[:, j * 128:(j + 1) * 128], vtm[:, j * 128:(j + 1) * 128], start=False, stop=True)
                for hp in range(2):
                    pb = 64 * hp
                    P.stt(Hst[l][pb:pb + 64, j, :], Hst[l][pb:pb + 64, j, :], elast[pb:pb + 64, j:j + 1],
                          ph[pb:pb + 64, hp * 64:(hp + 1) * 64], ALU.mult, ALU.add)
            s1, s2, mean, var = st4[:, 0, :], st4[:, 1, :], st4[:, 2, :], st4[:, 3, :]
            P.red(s1, ysb[:])
            P.tt(ysq[:], ysb[:], ysb[:], ALU.mult, eng="pool")
            P.red(s2, ysq[:])
            P.ts(mean, s1, 1.0 / 64, ALU.mult)
            P.tt(var, mean, mean, ALU.mult)
            P.stt(var, s2, 1.0 / 64, var, ALU.mult, ALU.subtract)
            P.ts(var, var, 64e-5, ALU.add)
            P.act(var, var, AF.Ln)
            P.act(var, var, AF.Exp, scale=-0.5)
            P.tt(ysb[:], ysb[:], bc(mean, 64), ALU.subtract)
            P.tt(ysb[:], ysb[:], bc(var, 64), ALU.mult)
            for kt in range(8):
                pp = k.nextA()
                P.tr(pp[:, 0:128], ysb[:, 2 * kt:2 * kt + 2, :].re(lambda a: a.rearrange("p h v -> p (h v)")), ident)
                t3 = ctmp()
                P.ts(t3[:, 0:128], pp[:, 0:128], fmc(l, C_LNW + kt), ALU.mult, fmc(l, C_LNB + kt), ALU.add)
                P.tt(t3[:, 0:128], t3[:, 0:128], bonus[:, kt, :], ALU.add)
                P.tt(ys[0][:, kt, cs], t3[:, 0:128], gfm[:, kt, :], ALU.mult)
    return mix


MIXER_BUILDERS["a"] = build_rwkv
```

```python
import numpy as np
import concourse.bass as bass
import concourse.mybir as mybir

F32 = mybir.dt.float32
BF16 = mybir.dt.bfloat16
AF = mybir.ActivationFunctionType
ALU = mybir.AluOpType
AX = mybir.AxisListType

COMPUTE = ("pe", "act", "dve", "pool")
ALLENG = ("pe", "act", "dve", "pool", "sp")
EPOCH = 30000
NDMASEM = 24
SAME_ENGINE_SYNC = True


class View:
    __slots__ = ("t", "ap", "p0", "p1")

    def __init__(self, t, ap, p0, p1):
        self.t, self.ap, self.p0, self.p1 = t, ap, p0, p1

    def re(self, fn):
        return View(self.t, fn(self.ap), self.p0, self.p1)

    def __getitem__(self, idx):
        return View(self.t, self.ap[idx], self.p0, self.p1)


class Tile:
    def __init__(self, name, h, nparts=1):
        self.name, self.h, self.nparts = name, h, nparts
        self.psum = False
        self.lastw = [None] * nparts
        self.readers = [[] for _ in range(nparts)]

    def __getitem__(self, idx):
        ap = self.h[idx]
        if self.nparts == 1:
            return View(self, ap, 0, 1)
        i = idx[1] if isinstance(idx, tuple) and len(idx) > 1 else slice(None)
        if isinstance(i, int):
            return View(self, ap, i, i + 1)
        p0, p1, _ = i.indices(self.nparts)
        return View(self, ap, p0, p1)

    def all(self):
        return self[:]


class Op:
    __slots__ = ("eng", "fn", "deps", "dma", "pos", "gid", "sig", "signaler", "waits", "dpos", "rg")


class Prog:
    def __init__(self, nc):
        self.nc = nc
        self.ops = []
        self.eng_ops = {e: [] for e in ALLENG}
        self.dma_ops = {e: [] for e in ALLENG}
        self.ntile = 0
        self.fence_dma = {}
        self.fence_skip = ()

    def sb(self, name, shape, dtype=F32, parts=1):
        self.ntile += 1
        h = self.nc.alloc_sbuf_tensor(f"{name}_{self.ntile}", list(shape), dtype)
        return Tile(name, h, parts)

    def ps(self, name, shape, dtype=F32, parts=1):
        self.ntile += 1
        h = self.nc.alloc_psum_tensor(f"{name}_{self.ntile}", list(shape), dtype)
        t = Tile(name, h, parts)
        t.psum = True
        return t

    def dram(self, name, shape, dtype=F32, kind="Internal", parts=1):
        h = self.nc.dram_tensor(name, list(shape), dtype, kind=kind).ap()
        return Tile(name, h, parts)

    def add(self, eng, fn, reads=(), writes=(), dma=False):
        op = Op()
        op.eng, op.fn, op.dma = eng, fn, dma
        op.gid = len(self.ops)
        op.rg = None
        op.sig = None
        op.signaler = False
        deps = set()
        for v in reads:
            if v is None:
                continue
            t = v.t
            for p in range(v.p0, v.p1):
                if t.lastw[p] is not None:
                    deps.add(t.lastw[p])
                if t.psum:
                    for r in t.readers[p]:
                        if r.eng != eng:
                            deps.add(r)
        for v in writes:
            t = v.t
            for p in range(v.p0, v.p1):
                if t.lastw[p] is not None:
                    deps.add(t.lastw[p])
                for r in t.readers[p]:
                    deps.add(r)
        for v in reads:
            if v is None:
                continue
            t = v.t
            for p in range(v.p0, v.p1):
                t.readers[p].append(op)
        for v in writes:
            t = v.t
            for p in range(v.p0, v.p1):
                t.lastw[p] = op
                t.readers[p] = []
        deps.discard(op)
        if dma:
            lst = self.dma_ops[eng]
            op.dpos = len(lst)
            if op.dpos >= NDMASEM:
                deps.add(lst[op.dpos - NDMASEM])
            lst.append(op)
        op.deps = deps
        op.pos = len(self.eng_ops[eng])
        self.eng_ops[eng].append(op)
        self.ops.append(op)
        return op

    def fence(self):
        lastc = [self.eng_ops[e][-1] for e in ALLENG if self.eng_ops[e] and not self.eng_ops[e][-1].dma]
        lastc = []
        for e in ALLENG:
            for op in reversed(self.eng_ops[e]):
                if not op.dma:
                    lastc.append(op)
                    break
        dmas = []
        for e in ALLENG:
            if e in self.fence_skip:
                continue
            dmas += self.dma_ops[e][self.fence_dma.get(e, 0):]
            self.fence_dma[e] = len(self.dma_ops[e])
        for e in ALLENG:
            if not self.eng_ops[e] or e in self.fence_skip:
                continue
            op = self.add(e, None, [], [])
            op.deps |= set(lastc) | set(dmas)
            op.deps.discard(op)

    def mm(self, out, lhsT, rhs, start=True, stop=True, **kw):
        op = self.add("pe", lambda e: e.matmul(out.ap, lhsT.ap, rhs.ap, start=start, stop=stop, **kw),
                      [lhsT, rhs] + ([] if start else [out]), [out])
        op.rg = (lhsT.ap.base_partition(), lhsT.ap.shape[0], out.ap.base_partition())
        return op

    def tr(self, out, in_, ident):
        op = self.add("pe", lambda e: e.transpose(out.ap, in_.ap, ident.ap), [in_, ident], [out])
        op.rg = (in_.ap.base_partition(), in_.ap.shape[0], out.ap.base_partition())
        return op

    def act(self, out, in_, func, bias=None, scale=None, accum=None, eng="act"):
        def fn(e):
            kw = {}
            if bias is not None:
                kw["bias"] = bias.ap if isinstance(bias, View) else bias
            if scale is not None:
                kw["scale"] = scale.ap if isinstance(scale, View) else scale
            if accum is not None:
                kw["accum_out"] = accum.ap
            return e.activation(out.ap, in_.ap, func, **kw)
        rd = [in_] + [x for x in (bias, scale) if isinstance(x, View)]
        wr = [out] + ([accum] if accum is not None else [])
        return self.add(eng, fn, rd, wr)

    def tt(self, out, a, b, op, eng="dve"):
        return self.add(eng, lambda e: e.tensor_tensor(out.ap, a.ap, b.ap, op), [a, b], [out])

    def ts(self, out, a, s1, op0, s2=None, op1=None, accum=None, eng="dve"):
        def fn(e):
            kw = {}
            if accum is not None:
                kw["accum_out"] = accum.ap
            return e.tensor_scalar(out.ap, a.ap, s1.ap if isinstance(s1, View) else s1,
                                   (s2.ap if isinstance(s2, View) else s2), op0,
                                   op1 if op1 is not None else ALU.bypass, **kw)
        rd = [a] + [x for x in (s1, s2) if isinstance(x, View)]
        wr = [out] + ([accum] if accum is not None else [])
        return self.add(eng, fn, rd, wr)

    def stt(self, out, a, s, b, op0, op1, eng="dve"):
        return self.add(eng, lambda e: e.scalar_tensor_tensor(out.ap, a.ap, s.ap if isinstance(s, View) else s,
                                                              b.ap, op0, op1),
                        [a, b] + ([s] if isinstance(s, View) else []), [out])

    def copy(self, out, in_, eng="dve"):
        if eng == "act":
            return self.add(eng, lambda e: e.copy(out.ap, in_.ap), [in_], [out])
        return self.add(eng, lambda e: e.tensor_copy(out.ap, in_.ap), [in_], [out])

    def memset(self, out, val, eng="dve"):
        return self.add(eng, lambda e: e.memset(out.ap, val), [], [out])

    def red(self, out, in_, op=ALU.add, axis=AX.X, eng="dve"):
        return self.add(eng, lambda e: e.tensor_reduce(out.ap, in_.ap, axis, op), [in_], [out])

    def scan(self, out, d0, d1, init, op0=ALU.mult, op1=ALU.add):
        return self.add("dve", lambda e: e.tensor_tensor_scan(out.ap, d0.ap, d1.ap,
                                                              init.ap if isinstance(init, View) else init, op0, op1),
                        [d0, d1] + ([init] if isinstance(init, View) else []), [out])

    def recip(self, out, in_):
        return self.add("dve", lambda e: e.reciprocal(out.ap, in_.ap), [in_], [out])

    def dma(self, out, in_, eng="act", **kw):
        return self.add(eng, lambda e: e.dma_start(out=out.ap, in_=in_.ap, **kw), [in_], [out], dma=True)

    def emit(self):
        nc = self.nc
        seen = {f: {e: -1 for e in COMPUTE + ("sp",)} for f in ALLENG}
        seen_dma = {f: set() for f in ALLENG}
        for op in self.ops:
            f = op.eng
            best = {}
            dwaits = []
            for d in op.deps:
                if d.dma:
                    if d.gid in seen_dma[f]:
                        continue
                    seen_dma[f].add(d.gid)
                    dwaits.append(d)
                    d.signaler = True
                else:
                    if d.eng == f and f != "pe" and not SAME_ENGINE_SYNC:
                        continue
                    if d.eng == "pe" and f == "pe" and d.rg is not None and d.rg == op.rg:
                        continue
                    if seen[f][d.eng] >= d.pos:
                        continue
                    if d.eng not in best or best[d.eng].pos < d.pos:
                        best[d.eng] = d
            for e, d in best.items():
                seen[f][e] = d.pos
                d.signaler = True
            op.waits = list(best.values()) + dwaits
        nsig = {e: 0 for e in ALLENG}
        for e in ALLENG:
            for op in self.eng_ops[e]:
                if op.dma:
                    op.sig = ("dma", e, op.dpos % NDMASEM, 16 * (op.dpos // NDMASEM + 1))
                elif op.signaler:
                    n = nsig[e]
                    op.sig = ("cmp", e, n // EPOCH, n % EPOCH + 1)
                    nsig[e] = n + 1
        sems = {}
        stack = []

        def getsem(key):
            if key not in sems:
                cm = nc.semaphore("s_%s_%s_%d" % key)
                sems[key] = cm.__enter__()
                stack.append(cm)
            return sems[key]

        for op in self.ops:
            if op.sig is not None:
                getsem(op.sig[:3])
        engobj = {"pe": "tensor", "act": "scalar", "dve": "vector", "pool": "gpsimd", "sp": "sync"}
        self.final_dma = {e: list(self.dma_ops[e]) for e in ALLENG}

        with nc.Block() as block:
            for e in ALLENG:
                ops = self.eng_ops[e]
                if not ops:
                    continue

                def body(eng, ops=ops, e=e):
                    for op in ops:
                        for d in op.waits:
                            eng.wait_ge(sems[d.sig[:3]], d.sig[3])
                        if op.fn is None:
                            if op.sig is None:
                                continue
                            ins = eng.nop()
                        else:
                            ins = op.fn(eng)
                        if op.sig is not None:
                            ins.then_inc(sems[op.sig[:3]], 16 if op.dma else 1)
                    lst = self.dma_ops[e]
                    done = set()
                    for op in reversed(lst):
                        k = op.sig[:3]
                        if k in done:
                            continue
                        done.add(k)
                        eng.wait_ge(sems[k], op.sig[3])

                getattr(block, engobj[e])(body)
        for cm in reversed(stack):
            cm.__exit__(None, None, None)
        return nc

from concourse.bass_utils import run_bass_kernel_spmd

D = 1024
NMETA = 16
PAD = 112
OFF_A, OFF_B, OFF_C, OFF_D, OFF_G, IN_W = 0, 3328, 6416, 9488, 11536, 15632
FFH = 2816
C_NM, C_NF, C_MU, C_W0, C_A0, C_KK, C_KA, C_RK, C_V0 = 0, 8, 16, 42, 50, 58, 66, 74, 82
C_SCW, C_SCB, C_LCW, C_LCB, C_BG, C_LAM, C_LNW, C_LNB, NFM = 90, 154, 170, 202, 210, 226, 234, 242, 250
R_LNW, R_LNB, R_SSMN, R_DTB, R_ALOG, R_SD, NROW = 0, 1024, 2048, 3072, 3088, 3104, 3120


class K:
    pass


def build(nchunks, depth, nch_group, enable=("a", "b", "c", "d"), arena_kb=82):
    nc = bass.Bass("TRN2", target_bir_lowering=False)
    P = Prog(nc)
    P.fence_skip = ("sp",)
    TP = nchunks * 128
    L = depth
    GMAX = nch_group * 128
    k = K()
    xin = P.dram("xin", [TP, D], F32, kind="ExternalInput")
    out = P.dram("out", [TP - 128, D], F32, kind="ExternalOutput")
    w_in = P.dram("w_in", [L, D, IN_W], F32, kind="ExternalInput")
    w_qksw = P.dram("w_qksw", [L, D, 1024], F32, kind="ExternalInput")
    w_branch = P.dram("w_branch", [L, 4, D, D], F32, kind="ExternalInput")
    w_out = P.dram("w_out", [L, D, D], F32, kind="ExternalInput")
    w_ffn_in = P.dram("w_ffn_in", [L, D, 2 * FFH], F32, kind="ExternalInput")
    w_ffn_out = P.dram("w_ffn_out", [L, FFH, D], F32, kind="ExternalInput")
    fmpack = P.dram("fmpack", [128, L * NFM + 8], F32, kind="ExternalInput")
    rowpack = P.dram("rowpack", [L, NROW], F32, kind="ExternalInput")
    lru_wg = P.dram("lru_wg", [L, 2, 8, 128, 128], F32, kind="ExternalInput")
    consts = P.dram("consts", [128, 1280], F32, kind="ExternalInput")
    ropetab = P.dram("ropetab", [128, 2, TP], F32, kind="ExternalInput")
    rettab = P.dram("rettab", [128, 1548], F32, kind="ExternalInput")
    rw_w2 = P.dram("rw_w2", [L, 64, D], F32, kind="ExternalInput")
    rw_a2 = P.dram("rw_a2", [L, 64, D], F32, kind="ExternalInput")
    rw_g2 = P.dram("rw_g2", [L, 128, D], F32, kind="ExternalInput")
    rw_v1 = P.dram("rw_v1", [max(L - 1, 1), D, 32], F32, kind="ExternalInput")
    rw_v2 = P.dram("rw_v2", [max(L - 1, 1), 32, D], F32, kind="ExternalInput")

    def cast_weight(name, src_tile, idx, shape):
        nblk = max(1, shape[0] // 256)
        t = P.dram(name, list(shape), BF16, kind="Internal", parts=nblk)
        rb = shape[0] // nblk
        for i in range(nblk):
            src_ap = src_tile.h[idx + (slice(i * rb, (i + 1) * rb), slice(None))]
            P.dma(View(t, t.h[i * rb:(i + 1) * rb, :], i, i + 1), View(src_tile, src_ap, 0, 1), eng="pool")
        return t

    wb_in, wb_qksw, wb_branch, wb_out, wb_ffi, wb_ffo = {}, {}, {}, {}, {}, {}

    def cast_layer(l):
        wb_in[l] = cast_weight(f"wb_in{l}", w_in, (l,), [D, IN_W])
        wb_qksw[l] = cast_weight(f"wb_qksw{l}", w_qksw, (l,), [D, 1024])
        wb_branch[l] = [cast_weight(f"wb_br{l}_{n}", w_branch, (l, n), [D, D]) for n in range(4)]
        wb_out[l] = cast_weight(f"wb_out{l}", w_out, (l,), [D, D])
        wb_ffi[l] = cast_weight(f"wb_ffi{l}", w_ffn_in, (l,), [D, 2 * FFH])
        wb_ffo[l] = cast_weight(f"wb_ffo{l}", w_ffn_out, (l,), [FFH, D])

    for l_ in range(L):
        cast_layer(l_)

    cst = P.sb("cst", [128, 1280], F32)
    P.dma(cst[:], consts[:])
    ident = cst[:, 0:128]
    identb_t = P.sb("identb", [128, 128], BF16)
    P.copy(identb_t[:], ident)
    identb = identb_t[:]
    onesb_t = P.sb("onesb", [128, 128], BF16)
    P.memset(onesb_t[:], 1.0)
    onesb = onesb_t[:]
    fm = P.sb("fm", [128, L * NFM + 8], F32)
    P.dma(fm[:], fmpack[:])

    def fmc(l, off, n=1):
        return fm[:, l * NFM + off: l * NFM + off + n]

    psA = [P.ps(f"psA{i}", [128, 512]) for i in range(4)]
    psB = [P.ps(f"psB{i}", [128, 512]) for i in range(3)]
    psT = P.ps("psT", [128, 1024], BF16)
    cnt = {"A": 0, "B": 0, "ev": 0, "w": 0}

    def nextA():
        cnt["A"] += 1
        return psA[cnt["A"] % 4]

    def nextB():
        cnt["B"] += 1
        return psB[cnt["B"] % 3]

    def ev_eng():
        cnt["ev"] += 1
        return "dve" if cnt["ev"] % 3 == 0 else "act"

    s = P.sb("s", [128, 8, GMAX], F32, parts=8)
    hn = P.sb("hn", [128, 8, GMAX], BF16, parts=8)
    rstd = P.sb("rstd", [128, GMAX], F32)
    ys = [P.sb(f"ys{n}", [128, 8, GMAX], BF16, parts=8) for n in range(4)]
    mfm = P.sb("mfm", [128, 8, GMAX], BF16, parts=8)
    NW = 3
    wbuf = [P.sb(f"wbuf{i}", [128, 8, 512], BF16) for i in range(NW)]
    arena_t = P.sb("arena", [128, arena_kb * 256], F32)
    ar = {"off": 0}

    def ar_reset():
        ar["off"] = 0

    def ar_alloc(name, shape, dtype=F32, parts=1):
        n = 1
        for d_ in shape[1:]:
            n *= d_
        nf = n if dtype == F32 else (n + 1) // 2
        nf = (nf + 7) // 8 * 8
        off = ar["off"]
        assert off + nf <= arena_kb * 256, (name, off, nf)
        ar["off"] = off + nf
        ap = arena_t.h[:, off:off + nf]
        if dtype != F32:
            ap = ap.bitcast(dtype)[:, 0:n]
        else:
            ap = ap[:, 0:n]
        if len(shape) == 3:
            ap = ap.rearrange("p (a b) -> p a b", a=shape[1])
        return Tile(name, ap, parts)

    def rowload(dst, l, off, n):
        P.dma(dst, View(rowpack, rowpack.h[l:l + 1, off:off + n].partition_broadcast(128), 0, 1))
    tmpf = [P.sb(f"tmpf{i}", [128, GMAX], F32) for i in range(8)]
    tcnt = [0]

    def tmp():
        tcnt[0] += 1
        return tmpf[tcnt[0] % 8]

    def wload(dram_view):
        cnt["w"] += 1
        wt = wbuf[cnt["w"] % NW]
        kt, n = dram_view.ap.shape[1], dram_view.ap.shape[2]
        P.dma(wt[:, 0:kt, 0:n], dram_view, eng="sp")
        return wt

    def wview(t, l_idx, c0, ncols, r0=0, nrows=1024):
        ap = t.h[l_idx + (slice(r0, r0 + nrows), slice(c0, c0 + ncols))]
        ap = ap.rearrange("(kt p) n -> p kt n", p=128)
        return View(t, ap, 0, t.nparts)

    ar_reset()
    sq = ar_alloc("sq", [128, 8, GMAX], F32, parts=8)
    sqb = ar_alloc("sqb", [128, 8, GMAX], BF16, parts=8)
    xtm = [ar_alloc(f"xtm{i}", [128, D], F32) for i in range(2)]
    macc = [ar_alloc(f"macc{i}", [128, GMAX], F32) for i in range(4)]

    def rmsnorm(G, wcol, outt, eps=1e-6):
        for kt in range(8):
            P.act(sqb[:, kt, 0:G], s[:, kt, 0:G], AF.Square)
        pp = nextA()
        for kt in range(8):
            P.mm(pp[:, 0:G], onesb, sqb[:, kt, 0:G], start=(kt == 0), stop=(kt == 7))
        P.act(rstd[:, 0:G], pp[:, 0:G], AF.Ln, scale=1.0 / D, bias=eps_t[:, 0:1])
        P.act(rstd[:, 0:G], rstd[:, 0:G], AF.Exp, scale=-0.5)
        for kt in range(8):
            P.stt(outt[:, kt, 0:G], s[:, kt, 0:G], wcol[:, kt:kt + 1], rstd[:, 0:G], ALU.mult, ALU.mult)

    onesf_t = P.sb("onesf", [128, 128], F32)
    P.memset(onesf_t[:], 1.0)
    onesb_f = onesf_t
    eps_t = P.sb("eps", [128, 2], F32)
    P.memset(eps_t[:, 0:1], 1e-6)
    P.memset(eps_t[:, 1:2], 1.0)

    def proj_fm(src, KT, wt_fn, ncols, G, evac):
        nblk = (ncols + 511) // 512
        for b in range(nblk):
            nb = min(512, ncols - b * 512)
            wt = wt_fn(b * 512, nb)
            for jj in range(nb // 128):
                pp = nextA()
                for kt in range(KT):
                    P.mm(pp[:, 0:G], wt[:, kt, jj * 128:(jj + 1) * 128], src[:, kt, 0:G],
                         start=(kt == 0), stop=(kt == KT - 1))
                evac(b * 4 + jj, pp[:, 0:G])

    def proj_tm(src, KT, wt_fn, ncols, G, evac):
        nblk = (ncols + 511) // 512
        for b in range(nblk):
            nb = min(512, ncols - b * 512)
            wt = wt_fn(b * 512, nb)
            for c in range(G // 128):
                pp = nextA()
                for kt in range(KT):
                    P.mm(pp[:, 0:nb], src[:, kt, c * 128:(c + 1) * 128], wt[:, kt, 0:nb],
                         start=(kt == 0), stop=(kt == KT - 1))
                evac(c, b, pp[:, 0:nb], nb)

    k.__dict__.update(locals())

    lru_tail = [P.sb(f"lru_tail{l}", [128, 8, 3], F32) for l in range(L)]
    lru_h = [P.sb(f"lru_h{l}", [128, 8], F32) for l in range(L)]
    lru_nsp = [P.sb(f"lru_nsp{l}", [128, 8], F32) for l in range(L)]
    ar_reset()
    lru_wgs = ar_alloc("lru_wgs", [128, 16, 128], BF16)
    lru_xcb = ar_alloc("lru_xcb", [128, GMAX], BF16)
    pin = ar_alloc("lru_pin", [128, 16, GMAX + 4], F32, parts=16)
    for l in range(L):
        P.memset(lru_tail[l][:], 0.0)
        P.memset(lru_h[l][:], 0.0)
        P.act(lru_nsp[l][:], fmc(l, C_LAM, 8), AF.Exp, scale=-1.0)
        P.act(lru_nsp[l][:], lru_nsp[l][:], AF.Ln, bias=eps_t[:, 1:2])
        P.ts(lru_nsp[l][:], lru_nsp[l][:], -8.0, ALU.mult)

    def gelu_tanh(outv, xv, G):
        t1 = tmp()
        P.act(t1[:, 0:G], xv, AF.Square)
        P.ts(t1[:, 0:G], t1[:, 0:G], 0.044715, ALU.mult, 1.0, ALU.add)
        P.tt(t1[:, 0:G], t1[:, 0:G], xv, ALU.mult)
        P.act(t1[:, 0:G], t1[:, 0:G], AF.Sigmoid, scale=1.5957691216057308)
        P.tt(outv, t1[:, 0:G], xv, ALU.mult)

    def lru_mix(l, g, G):
        def wt_fn(c0, nb):
            return wload(wview(wb_in[l], (), OFF_D + c0, nb))

        def evac(j, pp):
            if j < 8:
                P.copy(pin[:, j, 0:G], pp, eng=ev_eng())
            else:
                P.copy(pin[:, j, 3:3 + G], pp, eng=ev_eng())
        proj_fm(hn, 8, wt_fn, 2048, G, evac)
        P.dma(lru_wgs[:], View(lru_wg, lru_wg.h[l].rearrange("k n c e -> c (k n) e"), 0, 1), eng="pool")
        for n in range(8):
            xe = pin[:, 8 + n, :]
            P.copy(xe[:, 0:3], lru_tail[l][:, n, :])
            xc = tmp()
            lw = C_LCW + n * 4
            P.ts(xc[:, 0:G], xe[:, 0:G], fmc(l, lw), ALU.mult, fmc(l, C_LCB + n), ALU.add)
            for j in range(1, 4):
                P.stt(xc[:, 0:G], xe[:, j:j + G], fmc(l, lw + j), xc[:, 0:G], ALU.mult, ALU.add)
            P.copy(lru_tail[l][:, n, :], xe[:, G:G + 3])
            xcb16 = lru_xcb
            P.copy(xcb16[:, 0:G], xc[:, 0:G])
            gates = []
            for kk_ in range(2):
                pp = nextB()
                P.mm(pp[:, 0:G], lru_wgs[:, kk_ * 8 + n, :], xcb16[:, 0:G])
                gt = tmp()
                P.act(gt[:, 0:G], pp[:, 0:G], AF.Sigmoid, bias=fmc(l, C_BG + kk_ * 8 + n))
                gates.append(gt)
            rg, ig = gates
            la = tmp()
            P.ts(la[:, 0:G], rg[:, 0:G], lru_nsp[l][:, n:n + 1], ALU.mult)
            av = tmp()
            P.act(av[:, 0:G], la[:, 0:G], AF.Exp)
            P.act(la[:, 0:G], la[:, 0:G], AF.Exp, scale=2.0)
            P.ts(la[:, 0:G], la[:, 0:G], -1.0, ALU.mult, 1.0, ALU.add)
            P.act(la[:, 0:G], la[:, 0:G], AF.Sqrt)
            P.tt(ig[:, 0:G], ig[:, 0:G], xc[:, 0:G], ALU.mult)
            P.tt(ig[:, 0:G], ig[:, 0:G], la[:, 0:G], ALU.mult)
            if g == 0:
                P.memset(ig[:, 0:PAD], 0.0)
            P.scan(rg[:, 0:G], av[:, 0:G], ig[:, 0:G], lru_h[l][:, n:n + 1])
            P.copy(lru_h[l][:, n:n + 1], rg[:, G - 1:G])
            gelu_tanh(la[:, 0:G], pin[:, n, 0:G], G)
            P.tt(ys[3][:, n, 0:G], rg[:, 0:G], la[:, 0:G], ALU.mult)


    def zero_y(n, G):
        for kt in range(8):
            P.memset(ys[n][:, kt, 0:G], 0.0, eng="pool")

    mixers = {"a": None, "b": None, "c": None, "d": lru_mix}
    k.__dict__.update(locals())
    for nm, modfn in MIXER_BUILDERS.items():
        ar_reset()
        mixers[nm] = modfn(k)

    def layer(l, g, G):
        rmsnorm(G, fmc(l, C_NM, 8), hn)
        for n, nm in enumerate("abcd"):
            if nm in enable and mixers[nm] is not None:
                P.fence()
                mixers[nm](l, g, G)
            else:
                zero_y(n, G)
        P.fence()
        for db in range(2):
            for n in range(4):
                wb = wload(wview(wb_branch[l][n], (), db * 512, 512))
                wg = wload(wview(wb_in[l], (), OFF_G + n * 1024 + db * 512, 512))
                for dj in range(4):
                    pz = nextA()
                    for kt in range(8):
                        P.mm(pz[:, 0:G], wb[:, kt, dj * 128:(dj + 1) * 128], ys[n][:, kt, 0:G],
                             start=(kt == 0), stop=(kt == 7))
                    pg = nextA()
                    for kt in range(8):
                        P.mm(pg[:, 0:G], wg[:, kt, dj * 128:(dj + 1) * 128], hn[:, kt, 0:G],
                             start=(kt == 0), stop=(kt == 7))
                    sg = tmp()
                    P.act(sg[:, 0:G], pg[:, 0:G], AF.Sigmoid)
                    if n == 0:
                        P.tt(macc[dj][:, 0:G], sg[:, 0:G], pz[:, 0:G], ALU.mult)
                    else:
                        P.tt(sg[:, 0:G], sg[:, 0:G], pz[:, 0:G], ALU.mult)
                        P.tt(macc[dj][:, 0:G], macc[dj][:, 0:G], sg[:, 0:G], ALU.add, eng="pool")
            for dj in range(4):
                P.copy(mfm[:, db * 4 + dj, 0:G], macc[dj][:, 0:G])

        def evac_out(j, pp):
            P.tt(s[:, j, 0:G], s[:, j, 0:G], pp, ALU.add)
        proj_fm(mfm, 8, lambda c0, nb: wload(wview(wb_out[l], (), c0, nb)), 1024, G, evac_out)
        rmsnorm(G, fmc(l, C_NF, 8), hn)
        for hb in range(6):
            nb = min(512, FFH - hb * 512)
            wg_ = wload(wview(wb_ffi[l], (), hb * 512, nb))
            wu_ = wload(wview(wb_ffi[l], (), FFH + hb * 512, nb))
            for jj in range(nb // 128):
                pg = nextA()
                pu = nextA()
                for kt in range(8):
                    P.mm(pg[:, 0:G], wg_[:, kt, jj * 128:(jj + 1) * 128], hn[:, kt, 0:G], start=(kt == 0), stop=(kt == 7))
                for kt in range(8):
                    P.mm(pu[:, 0:G], wu_[:, kt, jj * 128:(jj + 1) * 128], hn[:, kt, 0:G], start=(kt == 0), stop=(kt == 7))
                sg = tmp()
                P.act(sg[:, 0:G], pg[:, 0:G], AF.Silu)
                ht_ = hb * 4 + jj
                P.tt(ys[ht_ // 8][:, ht_ % 8, 0:G], sg[:, 0:G], pu[:, 0:G], ALU.mult)
        for db in range(2):
            accs = [psA[i] for i in range(4)]
            for hb3, (h0, nh) in enumerate(((0, 8), (8, 8), (16, 6))):
                wt = wload(wview(wb_ffo[l], (), db * 512, 512, r0=h0 * 128, nrows=nh * 128))
                for dj in range(4):
                    for hh in range(nh):
                        ht = h0 + hh
                        P.mm(accs[dj][:, 0:G], wt[:, hh, dj * 128:(dj + 1) * 128], ys[ht // 8][:, ht % 8, 0:G],
                             start=(ht == 0), stop=(ht == 21))
            for dj in range(4):
                P.tt(s[:, db * 4 + dj, 0:G], s[:, db * 4 + dj, 0:G], accs[dj][:, 0:G], ALU.add)
        if g == 0:
            for kt in range(8):
                P.memset(s[:, kt, 0:PAD], 0.0, eng="pool")

    ngroups = (nchunks + nch_group - 1) // nch_group
    otm = xtm
    fin = sq
    for g in range(ngroups):
        c0 = g * nch_group
        c1 = min(nchunks, c0 + nch_group)
        G = (c1 - c0) * 128
        P.fence()
        for c in range(c0, c1):
            xt = xtm[c % 2]
            P.dma(xt[:], xin[c * 128:(c + 1) * 128, :])
            for kt in range(8):
                pp = nextB()
                P.tr(pp[:, 0:128], xt[:, kt * 128:(kt + 1) * 128], ident)
                P.copy(s[:, kt, (c - c0) * 128:(c - c0 + 1) * 128], pp[:, 0:128], eng=ev_eng())
        for l in range(L):
            layer(l, g, G)
        P.fence()
        rmsnorm(G, fm[:, L * NFM:L * NFM + 8], fin)
        for c in range(c0, c1):
            if c == 0:
                continue
            ot = otm[c % 2]
            for kt in range(8):
                pp = nextB()
                P.tr(pp[:, 0:128], fin[:, kt, (c - c0) * 128:(c - c0 + 1) * 128], ident)
                P.copy(ot[:, kt * 128:(kt + 1) * 128], pp[:, 0:128], eng=ev_eng())
            P.dma(out[(c - 1) * 128:c * 128, :], ot[:])
    if BUILD_ONLY:
        return nc
    P.emit()
    return nc


BUILD_ONLY = False
MIXER_BUILDERS = {}


def _fmcol(v):
    v = np.asarray(v, np.float32).reshape(-1, 128)
    return np.ascontiguousarray(v.T)


def prep_shared(inp, L, TP):
    f32 = np.float32
    sh = {}
    sh["w_in"] = np.ascontiguousarray(inp["w_in"][:L], f32)
    qk = inp["w_in"][:L, :, OFF_C:OFF_C + 1024].reshape(L, D, 16, 2, 32)
    sh["w_qksw"] = np.ascontiguousarray(qk[:, :, :, ::-1, :].reshape(L, D, 1024), f32)
    sh["w_branch"] = np.ascontiguousarray(inp["w_branch"][:L], f32)
    sh["w_out"] = np.ascontiguousarray(inp["w_out"][:L], f32)
    sh["w_ffn_in"] = np.ascontiguousarray(inp["w_ffn_in"][:L], f32)
    sh["w_ffn_out"] = np.ascontiguousarray(inp["w_ffn_out"][:L], f32)
    cols = []
    for l in range(L):
        vec = inp["rwkv_vec"][l]
        v0 = inp["rwkv_v0"][l - 1] if l > 0 else np.zeros(D, f32)
        parts = [_fmcol(inp["norm_mix"][l]), _fmcol(inp["norm_ffn"][l]), _fmcol(inp["rwkv_mu"][l]),
                 _fmcol(vec[0]), _fmcol(vec[1]), _fmcol(vec[2]), _fmcol(vec[3]),
                 _fmcol(inp["rwkv_rk"][l].reshape(-1)), _fmcol(v0)]
        scw = inp["ssm_conv_w"][l]
        parts.append(np.ascontiguousarray(scw.reshape(4, 16, 128).transpose(2, 1, 0)).reshape(128, 64))
        parts.append(_fmcol(inp["ssm_conv_b"][l]))
        lcw = inp["lru_conv_w"][l]
        parts.append(np.ascontiguousarray(lcw.reshape(4, 8, 128).transpose(2, 1, 0)).reshape(128, 32))
        parts.append(_fmcol(inp["lru_conv_b"][l]))
        parts.append(_fmcol(inp["lru_b_gate"][l].reshape(-1)))
        parts.append(_fmcol(inp["lru_lambda"][l]))
        parts.append(_fmcol(vec[4]))
        parts.append(_fmcol(vec[5]))
        cols.append(np.concatenate(parts, axis=1))
    cols.append(_fmcol(inp["final_norm"]))
    sh["fmpack"] = np.ascontiguousarray(np.concatenate(cols, axis=1), f32)
    assert sh["fmpack"].shape[1] == L * NFM + 8
    rows = []
    for l in range(L):
        vec = inp["rwkv_vec"][l]
        rows.append(np.concatenate([vec[4], vec[5], inp["ssm_norm"][l], inp["ssm_dt_bias"][l],
                                    inp["ssm_a_log"][l], inp["ssm_d"][l]]))
    sh["rowpack"] = np.ascontiguousarray(np.stack(rows), f32)
    sh["lru_wg"] = np.ascontiguousarray(inp["lru_w_gate"][:L], f32)
    i = np.arange(128)
    ident = np.eye(128, dtype=f32)
    triu = (i[:, None] <= i[None, :]).astype(f32)
    mstr = (i[:, None] > i[None, :]).astype(f32)
    spare = np.zeros((128, 128), f32)
    spare[:, 0] = (i >= PAD)
    sut = (i[:, None] < i[None, :]).astype(f32)
    blk = ((i[:, None] // 64) == (i[None, :] // 64)).astype(f32)
    sh["consts"] = np.ascontiguousarray(np.concatenate([ident, triu, mstr, spare, sut, blk, sut, triu, sut, triu], axis=1))
    pos = (np.arange(TP) - PAD).astype(f32)
    half = 32
    freq = np.power(np.float32(10000.0), -np.arange(half, dtype=f32) / half).astype(f32)
    ang = pos[None, :] * freq[:, None]
    cosf = np.cos(ang).astype(f32)
    sinf = np.sin(ang).astype(f32)
    ch = np.arange(128) % 64
    cos_t = cosf[ch % 32]
    sin_t = np.where((ch < 32)[:, None], -sinf[ch % 32], sinf[ch % 32])
    sh["ropetab"] = np.ascontiguousarray(np.stack([cos_t, sin_t], axis=1), f32)
    H = 8
    log_g = np.log1p(-np.exp2(-5.0 - np.arange(H, dtype=f32))).astype(f32)
    idx = np.arange(128, dtype=f32)
    rel = idx[None, :] - idx[:, None]
    intraT = np.where(rel >= 0, np.exp(np.where(rel >= 0, rel, 0.0)[None] * log_g[:, None, None]), 0.0)
    qdec = np.exp((idx + 1.0)[None, :] * log_g[:, None])
    kdec = np.exp((127.0 - idx)[None, :] * log_g[:, None])
    cdec = np.exp(128.0 * log_g)
    rt = np.zeros((128, 1548), f32)
    rt[:, 0:1024] = intraT.transpose(1, 0, 2).reshape(128, 1024)
    pp_ = np.arange(128) // 64
    for j in range(4):
        rt[:, 1024 + j * 128:1024 + (j + 1) * 128] = qdec[2 * j + pp_, :]
        rt[:, 1544 + j] = cdec[2 * j + pp_]
    rt[:, 1536:1544] = kdec.T
    sh["rettab"] = rt
    sh["rw_w2"] = np.ascontiguousarray(inp["rwkv_w2"][:L], f32)
    sh["rw_a2"] = np.ascontiguousarray(inp["rwkv_a2"][:L], f32)
    sh["rw_g2"] = np.ascontiguousarray(inp["rwkv_g2"][:L], f32)
    n1 = max(L - 1, 1)
    sh["rw_v1"] = np.ascontiguousarray(inp["rwkv_v1"][:n1], f32)
    sh["rw_v2"] = np.ascontiguousarray(inp["rwkv_v2"][:n1], f32)
    return sh


def run_module(inp, depth, nch_group, enable=("a", "b", "c", "d"), ncores=None, arena_kb=82):
    x = np.asarray(inp["x"], np.float32)
    B, S, _ = x.shape
    T = S + NMETA
    assert (T + PAD) % 128 == 0
    TP = T + PAD
    nchunks = TP // 128
    nc = build(nchunks, depth, nch_group, enable, arena_kb)
    sh = prep_shared(inp, depth, TP)
    meta = np.asarray(inp["meta"], np.float32)
    ncores = ncores or B
    in_maps = []
    for c in range(ncores):
        b = c % B
        xin = np.concatenate([np.zeros((PAD, D), np.float32), meta, x[b]], axis=0)
        m = dict(sh)
        m["xin"] = np.ascontiguousarray(xin)
        in_maps.append(m)
    res = run_bass_kernel_spmd(nc, in_maps, core_ids=list(range(ncores)))
    outs = [res.results[b]["out"] for b in range(B)]
    return np.stack(outs, axis=0)


def kernel(**inputs):
    return run_module(inputs, 4, NCH_GROUP, ncores=4, arena_kb=82)


NCH_GROUP = 2

RT_INTRA, RT_Q, RT_KDEC, RT_C, RT_N = 0, 1024, 1536, 1544, 1548


def build_ret(k):
    P, L, GMAX, NCHG = k.P, k.L, k.GMAX, k.nch_group
    hn, ys, w_in, w_qksw = k.hn, k.ys, k.w_in, k.w_qksw
    A = k.ar_alloc
    rt = A("rt", [128, RT_N], F32)
    rope_c = P.sb("rope_c", [128, GMAX], F32)
    rope_s = P.sb("rope_s", [128, GMAX], F32)
    pin = A("ret_pin", [128, 16, GMAX], F32, parts=16)
    qrT = A("qrT", [128, 8, GMAX], BF16, parts=8)
    qdT = A("qdT", [128, 4, GMAX], BF16, parts=4)
    vtm = A("ret_vtm", [128, NCHG, 1024], BF16, parts=NCHG)
    gtm = A("ret_gtm", [128, NCHG, 1024], F32, parts=NCHG)
    ktm = A("ret_ktm", [128, NCHG, 512], BF16, parts=NCHG)
    scm = [A(f"ret_scm{i}", [128, 8, 128], BF16) for i in range(2)]
    ysb = [A(f"ret_ysb{i}", [128, 8, 128], F32) for i in range(2)]
    ysq = A("ret_ysq", [128, 8, 128], F32)
    ssq = P.sb("ret_ssq", [128, 8], F32)
    Rst = [P.sb(f"ret_R{l}", [128, 4, 128], F32) for l in range(L)]
    Rb = P.sb("ret_Rb", [128, 4, 128], BF16)
    for l in range(L):
        P.memset(Rst[l][:], 0.0)

    def mix(l, g, G):
        c0 = g * NCHG
        nchk = G // 128
        P.dma(rt[:], k.rettab[:])
        if l == 0:
            P.dma(rope_c[:, 0:G], k.ropetab[:, 0, c0 * 128:c0 * 128 + G])
            P.dma(rope_s[:, 0:G], k.ropetab[:, 1, c0 * 128:c0 * 128 + G])

        def ev1(j, pp):
            P.copy(pin[:, j, 0:G], pp, eng=k.ev_eng())

        def ev2(j, pp):
            P.copy(pin[:, 8 + j, 0:G], pp, eng=k.ev_eng())
        k.proj_fm(hn, 8, lambda c, nb: k.wload(k.wview(k.wb_in[l], (), OFF_C + c, nb)), 1024, G, ev1)
        k.proj_fm(hn, 8, lambda c, nb: k.wload(k.wview(k.wb_qksw[l], (), c, nb)), 1024, G, ev2)
        for j in range(8):
            cc, ss = rope_c, rope_s
            t1 = k.tmp()
            t2 = k.tmp()
            P.tt(t1[:, 0:G], pin[:, j, 0:G], cc[:, 0:G], ALU.mult)
            P.tt(t2[:, 0:G], pin[:, 8 + j, 0:G], ss[:, 0:G], ALU.mult, eng="pool")
            if j < 4:
                P.tt(t1[:, 0:G], t1[:, 0:G], t2[:, 0:G], ALU.add)
                P.copy(qrT[:, j, 0:G], t1[:, 0:G], eng="act")
                for c in range(nchk):
                    P.tt(qdT[:, j, c * 128:(c + 1) * 128], t1[:, c * 128:(c + 1) * 128],
                         rt[:, RT_Q + j * 128:RT_Q + (j + 1) * 128], ALU.mult)
            else:
                P.tt(t1[:, 0:G], t1[:, 0:G], t2[:, 0:G], ALU.add)
                P.act(qrT[:, j, 0:G], t1[:, 0:G], AF.Copy, scale=0.125)

        def ev3(c, b, pp, nb):
            if b < 2:
                P.copy(vtm[:, c, b * 512:(b + 1) * 512], pp, eng=k.ev_eng())
            else:
                P.act(gtm[:, c, (b - 2) * 512:(b - 1) * 512], pp, AF.Silu)
        k.proj_tm(hn, 8, lambda c, nb: k.wload(k.wview(k.wb_in[l], (), OFF_C + 1024 + c, nb)), 2048, G, ev3)

        for c in range(nchk):
            cs = slice(c * 128, (c + 1) * 128)
            for j in range(4):
                pt = k.psT
                P.tr(pt[:, j * 128:(j + 1) * 128], qrT[:, 4 + j, cs], k.identb)
            for h in range(8):
                P.ts(ktm[:, c, h * 64:(h + 1) * 64], k.psT[:, h * 64:(h + 1) * 64],
                     rt[:, RT_KDEC + h:RT_KDEC + h + 1], ALU.mult, eng="dve")
            sc = scm[c % 2]
            for hb in range(2):
                pp = k.nextB()
                for hh in range(4):
                    h = hb * 4 + hh
                    j, pb = h // 2, 64 * (h % 2)
                    P.mm(pp[:, hh * 128:(hh + 1) * 128], qrT[pb:pb + 64, 4 + j, cs], qrT[pb:pb + 64, j, cs])
                P.tt(sc[:, hb * 4:(hb + 1) * 4, :], pp[:, :].re(lambda a: a.rearrange("p (h l) -> p h l", h=4)),
                     rt[:, RT_INTRA + hb * 512:RT_INTRA + (hb + 1) * 512].re(
                         lambda a: a.rearrange("p (h l) -> p h l", h=4)), ALU.mult)
            P.copy(Rb[:], Rst[l][:], eng="act")
            yb = ysb[c % 2]
            for hb in range(2):
                pp = k.nextB()
                for hh in range(4):
                    h = hb * 4 + hh
                    j, pb = h // 2, 64 * (h % 2)
                    P.mm(pp[:, hh * 128:(hh + 1) * 128], sc[:, h, :], vtm[:, c, h * 128:(h + 1) * 128],
                         start=True, stop=False)
                    P.mm(pp[:, hh * 128:(hh + 1) * 128], qdT[pb:pb + 64, j, cs], Rb[pb:pb + 64, j, :],
                         start=False, stop=True)
                P.copy(yb[:, hb * 4:(hb + 1) * 4, :], pp[:, :].re(lambda a: a.rearrange("p (h l) -> p h l", h=4)),
                       eng="act")
            for j in range(4):
                pp = k.nextB()
                for hp in range(2):
                    h = 2 * j + hp
                    P.mm(pp[:, hp * 128:(hp + 1) * 128], ktm[:, c, j * 128:(j + 1) * 128],
                         vtm[:, c, h * 128:(h + 1) * 128])
                for hp in range(2):
                    pb = 64 * hp
                    P.stt(Rst[l][pb:pb + 64, j, :], Rst[l][pb:pb + 64, j, :], rt[pb:pb + 64, RT_C + j:RT_C + j + 1],
                          pp[pb:pb + 64, hp * 128:(hp + 1) * 128], ALU.mult, ALU.add)
            P.tt(ysq[:], yb[:], yb[:], ALU.mult, eng="pool")
            P.red(ssq[:], ysq[:])
            P.ts(ssq[:], ssq[:], 1.0 / 128, ALU.mult, 1e-6, ALU.add)
            P.act(ssq[:], ssq[:], AF.Ln)
            P.act(ssq[:], ssq[:], AF.Exp, scale=-0.5)
            P.tt(yb[:], yb[:], ssq[:].re(lambda a: a.unsqueeze(2).broadcast_to([128, 8, 128])), ALU.mult)
            P.tt(yb[:], yb[:], gtm[:, c, :].re(lambda a: a.rearrange("p (h e) -> p h e", h=8)), ALU.mult)
            for kt in range(8):
                pp = k.nextB()
                P.tr(pp[:, 0:128], yb[:, kt, :], k.ident)
                P.copy(ys[2][:, kt, cs], pp[:, 0:128], eng=k.ev_eng())
    return mix


MIXER_BUILDERS["c"] = build_ret

def build_ssd(k):
    P, L, GMAX, NCHG = k.P, k.L, k.GMAX, k.nch_group
    hn, ys, w_in = k.hn, k.ys, k.w_in
    A = k.ar_alloc
    triU = k.cst[:, 128:256]
    mstr = k.cst[:, 256:384]
    xe = A("ssd_xe", [128, 16, GMAX + 4], F32, parts=16)
    ztm = A("ssd_ztm", [128, NCHG, 1024], F32, parts=NCHG)
    dtm = A("ssd_dtm", [128, NCHG, 16], F32, parts=NCHG)
    rowb = A("ssd_rowb", [128, 1072], F32)
    xsf = A("ssd_xsf", [128, 8, GMAX], F32, parts=8)
    bfm = A("ssd_bfm", [128, 4, GMAX], BF16, parts=4)
    cfm = A("ssd_cfm", [128, 4, GMAX], BF16, parts=4)
    btm = A("ssd_btm", [128, 4, 128], BF16)
    xs_tm = A("ssd_xstm", [128, 16, 64], F32)
    xdt = A("ssd_xdt", [128, 16, 64], BF16)
    xdt2 = A("ssd_xdt2", [128, 16, 64], BF16)
    rhsL = A("ssd_rhsL", [128, 16, 128], F32)
    Lm = A("ssd_Lm", [128, 16, 128], F32)
    Wm = A("ssd_Wm", [128, 16, 128], BF16)
    cbm = A("ssd_cbm", [128, 4, 128], F32)
    yt = A("ssd_yt", [128, 16, 64], F32)
    yt2 = A("ssd_yt2", [128, 16, 64], F32)
    sm = P.sb("ssd_small", [128, 8, 16], F32)
    ssq = P.sb("ssd_ssq", [128, 4], F32)
    St = [P.sb(f"ssd_S{l}", [128, 16, 64], F32) for l in range(L)]
    Sb = P.sb("ssd_Sb", [128, 16, 64], BF16)
    tails = [P.sb(f"ssd_tail{l}", [128, 16, 3], F32) for l in range(L)]
    for l in range(L):
        P.memset(St[l][:], 0.0)
        P.memset(tails[l][:], 0.0)

    def bc(v, n):
        return v.re(lambda a: a.unsqueeze(2).broadcast_to([128, a.shape[1], n]))

    def mix(l, g, G):
        nchk = G // 128
        k.rowload(rowb[:], l, R_SSMN, 1072)
        nw_b = rowb[:, 0:1024]
        dtb_b = rowb[:, 1024:1040]
        alog_b = rowb[:, 1040:1056]
        dsk_b = rowb[:, 1056:1072]
        P.act(sm[:, 6, :], alog_b, AF.Exp)
        P.ts(sm[:, 6, :], sm[:, 6, :], -1.0, ALU.mult)

        def ev_z(c, b, pp, nb):
            P.act(ztm[:, c, b * 512:(b + 1) * 512], pp, AF.Silu)
        k.proj_tm(hn, 8, lambda c, nb: k.wload(k.wview(k.wb_in[l], (), OFF_B + c, nb)), 1024, G, ev_z)

        def ev_x(j, pp):
            P.copy(xe[:, j, 3:3 + G], pp, eng=k.ev_eng())
        k.proj_fm(hn, 8, lambda c, nb: k.wload(k.wview(k.wb_in[l], (), OFF_B + 1024 + c, nb)), 2048, G, ev_x)

        def ev_dt(c, b, pp, nb):
            P.tt(dtm[:, c, :], pp, dtb_b, ALU.add)
        k.proj_tm(hn, 8, lambda c, nb: k.wload(k.wview(k.wb_in[l], (), OFF_B + 3072 + c, nb)), 16, G, ev_dt)
        for j in range(16):
            P.copy(xe[:, j, 0:3], tails[l][:, j, :])
            xc = k.tmp()
            cw = C_SCW + j * 4
            P.ts(xc[:, 0:G], xe[:, j, 0:G], k.fmc(l, cw), ALU.mult, k.fmc(l, C_SCB + j), ALU.add)
            for t in range(1, 4):
                P.stt(xc[:, 0:G], xe[:, j, t:t + G], k.fmc(l, cw + t), xc[:, 0:G], ALU.mult, ALU.add)
            P.copy(tails[l][:, j, :], xe[:, j, G:G + 3])
            if j < 8:
                P.act(xsf[:, j, 0:G], xc[:, 0:G], AF.Silu)
            elif j < 12:
                P.act(bfm[:, j - 8, 0:G], xc[:, 0:G], AF.Silu)
            else:
                P.act(cfm[:, j - 12, 0:G], xc[:, 0:G], AF.Silu)

        for c in range(nchk):
            cs = slice(c * 128, (c + 1) * 128)
            dt = sm[:, 0, :]
            dA = sm[:, 1, :]
            ea = sm[:, 2, :]
            dec = sm[:, 3, :]
            etot = sm[:, 4, :]
            dtdec = sm[:, 5, :]
            P.act(dt, dtm[:, c, :], AF.Exp)
            P.act(dt, dt, AF.Ln, bias=k.eps_t[:, 1:2])
            if g == 0 and c == 0:
                P.ts(dt, dt, k.cst[:, 384:385], ALU.mult)
            P.tt(dA, dt, sm[:, 6, :], ALU.mult)
            pc = k.nextB()
            P.mm(pc[:, 0:16], triU, dA)
            P.mm(pc[:, 16:32], k.onesf_t[:], dA)
            P.act(ea, pc[:, 0:16], AF.Exp)
            P.copy(sm[:, 7, :], pc[:, 0:16])
            P.tt(dec, pc[:, 16:32], sm[:, 7, :], ALU.subtract)
            P.act(dec, dec, AF.Exp)
            P.act(etot, pc[:, 16:32], AF.Exp)
            P.tt(dtdec, dt, dec, ALU.mult)
            for hb in range(2):
                pp = k.nextA()
                for jj in range(4):
                    P.tr(pp[:, jj * 128:(jj + 1) * 128], xsf[:, hb * 4 + jj, cs], k.ident)
                P.copy(xs_tm[:, hb * 8:(hb + 1) * 8, :], pp[:, :].re(lambda a: a.rearrange("p (h e) -> p h e", h=8)),
                       eng="act")
            P.tt(xdt[:], xs_tm[:], bc(dt, 64), ALU.mult)
            P.tt(xdt2[:], xs_tm[:], bc(dtdec, 64), ALU.mult, eng="pool")
            for gi in range(4):
                P.tr(k.psT[:, gi * 128:(gi + 1) * 128], bfm[:, gi, cs], k.identb)
            P.copy(btm[:], k.psT[:, 0:512].re(lambda a: a.rearrange("p (g n) -> p g n", g=4)), eng="act")
            P.tt(rhsL[:], triU.re(lambda a: a.unsqueeze(1).broadcast_to([128, 16, 128])), bc(dA, 128), ALU.mult)
            pcb = k.nextB()
            for gi in range(4):
                P.mm(pcb[:, gi * 128:(gi + 1) * 128], bfm[:, gi, cs], cfm[:, gi, cs])
            P.tt(cbm[:], pcb[:, :].re(lambda a: a.rearrange("p (g l) -> p g l", g=4)),
                 triU.re(lambda a: a.unsqueeze(1).broadcast_to([128, 4, 128])), ALU.mult)
            for gi in range(4):
                pl = k.nextA()
                P.mm(pl[:, :], mstr, rhsL[:, gi * 4:(gi + 1) * 4, :].re(lambda a: a.rearrange("p h l -> p (h l)")))
                P.act(Lm[:, gi * 4:(gi + 1) * 4, :], pl[:, :].re(lambda a: a.rearrange("p (h l) -> p h l", h=4)), AF.Exp)
                P.tt(Wm[:, gi * 4:(gi + 1) * 4, :], Lm[:, gi * 4:(gi + 1) * 4, :],
                     cbm[:, gi, :].re(lambda a: a.unsqueeze(1).broadcast_to([128, 4, 128])), ALU.mult)
            P.copy(Sb[:], St[l][:], eng="act")
            for hb in range(2):
                py = k.nextA()
                for hh in range(8):
                    h = hb * 8 + hh
                    P.mm(py[:, hh * 64:(hh + 1) * 64], Wm[:, h, :], xdt[:, h, :])
                po = k.nextB()
                for gg in range(2):
                    gi = hb * 2 + gg
                    P.mm(po[:, gg * 256:(gg + 1) * 256], cfm[:, gi, cs],
                         Sb[:, gi * 4:(gi + 1) * 4, :].re(lambda a: a.rearrange("p h e -> p (h e)")))
                hs = slice(hb * 8, (hb + 1) * 8)
                P.tt(yt[:, hs, :], po[:, :].re(lambda a: a.rearrange("p (h e) -> p h e", h=8)),
                     bc(sm[:, 2, hb * 8:(hb + 1) * 8], 64), ALU.mult)
                P.tt(yt[:, hs, :], yt[:, hs, :], py[:, :].re(lambda a: a.rearrange("p (h e) -> p h e", h=8)), ALU.add)
            P.tt(yt2[:], xs_tm[:], bc(dsk_b, 64), ALU.mult, eng="pool")
            P.tt(yt[:], yt[:], yt2[:], ALU.add)
            P.tt(yt[:], yt[:], ztm[:, c, :].re(lambda a: a.rearrange("p (h e) -> p h e", h=16)), ALU.mult)
            for hb in range(2):
                pst = k.nextB()
                for gg in range(2):
                    gi = hb * 2 + gg
                    P.mm(pst[:, gg * 256:(gg + 1) * 256], btm[:, gi, :],
                         xdt2[:, gi * 4:(gi + 1) * 4, :].re(lambda a: a.rearrange("p h e -> p (h e)")))
                hs = slice(hb * 8, (hb + 1) * 8)
                P.tt(St[l][:, hs, :], St[l][:, hs, :], bc(sm[:, 4, hb * 8:(hb + 1) * 8], 64), ALU.mult)
                P.tt(St[l][:, hs, :], St[l][:, hs, :],
                     pst[:, :].re(lambda a: a.rearrange("p (h e) -> p h e", h=8)), ALU.add)
            P.tt(yt2[:], yt[:], yt[:], ALU.mult, eng="pool")
            P.red(ssq[:], yt2[:].re(lambda a: a.rearrange("p (g e) n -> p g (e n)", g=4)))
            P.ts(ssq[:], ssq[:], 1.0 / 256, ALU.mult, 1e-5, ALU.add)
            P.act(ssq[:], ssq[:], AF.Ln)
            P.act(ssq[:], ssq[:], AF.Exp, scale=-0.5)
            ytg = yt[:].re(lambda a: a.rearrange("p (g e) n -> p g (e n)", g=4))
            P.tt(ytg, ytg, bc(ssq[:], 256), ALU.mult)
            ytf = yt[:].re(lambda a: a.rearrange("p h e -> p (h e)"))
            P.tt(ytf, ytf, nw_b, ALU.mult)
            for kt in range(8):
                pp = k.nextA()
                P.tr(pp[:, 0:128], ytf[:, kt * 128:(kt + 1) * 128], k.ident)
                P.copy(ys[1][:, kt, cs], pp[:, 0:128], eng=k.ev_eng())
    return mix


MIXER_BUILDERS["b"] = build_ssd

def build_rwkv(k):
    P, L, GMAX, NCHG = k.P, k.L, k.GMAX, k.nch_group
    hn, ys, w_in = k.hn, k.ys, k.w_in
    A = k.ar_alloc
    cst = k.cst
    ident, identb = k.ident, k.identb
    mstr = cst[:, 256:384]
    blkb_t = P.sb("rw_blkb", [128, 128], BF16)
    P.copy(blkb_t[:], cst[:, 640:768])
    blk = blkb_t[:]
    sqb2 = [A(f"rw_sqb{i}", [128, 2, 128], BF16) for i in range(2)]
    mask4 = cst[:, 768:1280]
    pa = A("rw_pa", [128, 26, GMAX + 1], F32, parts=26)
    w2s = A("rw_w2s", [128, 1024], BF16)
    a2s = A("rw_a2s", [128, 1024], BF16)
    g2s = A("rw_g2s", [128, 1024], BF16)
    v1s = A("rw_v1s", [128, 8, 32], BF16)
    v2s = A("rw_v2s", [128, 1024], BF16)
    twb = A("rw_twb", [128, 128], BF16)
    adb = A("rw_adb", [128, 128], BF16)
    sgb = A("rw_sgb", [128, 128], BF16)
    bt3 = [A(f"rw_bt3{i}", [128, 3, 128], BF16) for i in range(2)]
    gfm = A("rw_gfm", [128, 8, 128], BF16, parts=8)
    bonus = A("rw_bonus", [128, 8, 128], BF16, parts=8)
    off_alias = k.ar["off"]
    vb16 = A("rw_vb16", [128, 8, GMAX], BF16, parts=8)
    vvb = A("rw_vvb", [128, GMAX], BF16)
    k.ar["off"] = off_alias
    ar = A("rw_ar", [128, 8, 256], BF16, parts=8)
    bet = A("rw_bet", [128, 8, 128], BF16, parts=8)
    ktl = A("rw_ktl", [128, 8, 128], BF16, parts=8)
    tm3 = A("rw_tm3", [128, 3, 1024], BF16)
    btm, ktm, vtm = tm3[:, 0, :], tm3[:, 1, :], tm3[:, 2, :]
    AT = [A(f"rw_AT{i}", [128, 4, 512], BF16, parts=4) for i in range(1)] * 2
    P0 = [A(f"rw_P0{i}", [128, 4, 128], BF16) for i in range(1)] * 2
    QP = [A(f"rw_QP{i}", [128, 4, 256], BF16) for i in range(2)]
    Xb = [A(f"rw_Xb{i}", [128, 4, 64], BF16) for i in range(2)]
    Ub = A("rw_Ub", [128, 16, 64], BF16)
    ysb = A("rw_ysb", [128, 16, 64], F32)
    ysq = A("rw_ysq", [128, 16, 64], F32)
    elast = P.sb("rw_elast", [128, 8], F32)
    st4 = P.sb("rw_st4", [128, 4, 16], F32)
    nw0 = P.sb("rw_nw0", [128, 8], F32)
    na0 = P.sb("rw_na0", [128, 8], F32)
    nv0 = P.sb("rw_nv0", [128, 8], F32)
    cm = P.sb("rw_cm", [128, 2], F32)
    P.memset(cm[:, 0:1], -0.5)
    P.memset(cm[:, 1:2], 1e-24)
    vfirst = P.sb("rw_vfirst", [128, 8, GMAX], F32, parts=8)
    Hst = [P.sb(f"rw_H{l}", [128, 8, 64], F32) for l in range(L)]
    H0b = P.sb("rw_H0b", [128, 8, 64], BF16)
    shtail = [P.sb(f"rw_sht{l}", [128, 26], F32) for l in range(L)]
    for l in range(L):
        P.memset(Hst[l][:], 0.0)
        P.memset(shtail[l][:], 0.0)

    def bc(v, n):
        return v.re(lambda a: a.unsqueeze(2).broadcast_to([128, a.shape[1], n]))

    rtp = [A(f"rw_tmp{i}", [128, 128], F32) for i in range(12)]
    rtc = [0]

    def ctmp():
        rtc[0] += 1
        return rtp[rtc[0] % 12]

    def mix(l, g, G):
        nchk = G // 128
        fmc = k.fmc

        def ev(j, pp):
            P.copy(pa[:, j, 1:1 + G], pp, eng=k.ev_eng())
        k.proj_fm(hn, 8, lambda c, nb: k.wload(k.wview(k.wb_in[l], (), OFF_A + c, nb)), 3328, G, ev)
        P.dma(w2s[0:64, :], View(k.rw_w2, k.rw_w2.h[l], 0, 1), eng="pool")
        P.dma(a2s[64:128, :], View(k.rw_a2, k.rw_a2.h[l], 0, 1), eng="pool")
        P.dma(g2s[:, :], View(k.rw_g2, k.rw_g2.h[l], 0, 1), eng="pool")
        if l > 0:
            P.dma(v1s[:], View(k.rw_v1, k.rw_v1.h[l - 1].rearrange("(kt p) r -> p kt r", p=128), 0, 1), eng="pool")
            P.dma(v2s[0:32, :], View(k.rw_v2, k.rw_v2.h[l - 1], 0, 1), eng="pool")
        P.ts(nw0[:], fmc(l, C_W0, 8), -1.0, ALU.mult)
        P.ts(na0[:], fmc(l, C_A0, 8), -1.0, ALU.mult)
        P.ts(nv0[:], fmc(l, C_V0, 8), -1.0, ALU.mult)
        P.copy(pa[:, :, 0], shtail[l][:])
        P.copy(shtail[l][:], pa[:, :, G])
        for j in range(26):
            d = k.tmp()
            P.tt(d[:, 0:G], pa[:, j, 0:G], pa[:, j, 1:G + 1], ALU.subtract, eng="pool")
            P.stt(pa[:, j, 1:G + 1], d[:, 0:G], fmc(l, C_MU + j), pa[:, j, 1:G + 1], ALU.mult, ALU.add)
        if l == 0:
            for j in range(8):
                P.copy(vfirst[:, j, 0:G], pa[:, 16 + j, 1:G + 1], eng=k.ev_eng())
        else:
            for j in range(8):
                P.copy(vb16[:, j, 0:G], pa[:, 16 + j, 1:G + 1], eng=k.ev_eng())
            pv = k.nextA()
            for kt in range(8):
                P.mm(pv[0:32, 0:G], v1s[:, kt, :], vb16[:, kt, 0:G], start=(kt == 0), stop=(kt == 7))
            P.copy(vvb[0:32, 0:G], pv[0:32, 0:G])
            for j in range(8):
                pp = k.nextA()
                P.mm(pp[:, 0:G], v2s[0:32, j * 128:(j + 1) * 128], vvb[0:32, 0:G])
                sg = k.tmp()
                P.act(sg[:, 0:G], pp[:, 0:G], AF.Exp, scale=-1.0, bias=nv0[:, j:j + 1])
                P.ts(sg[:, 0:G], sg[:, 0:G], 1.0, ALU.add)
                P.recip(sg[:, 0:G], sg[:, 0:G])
                d = k.tmp()
                vcur = pa[:, 16 + j, 1:G + 1]
                P.tt(d[:, 0:G], vfirst[:, j, 0:G], vcur, ALU.subtract)
                P.tt(d[:, 0:G], d[:, 0:G], sg[:, 0:G], ALU.mult)
                P.tt(vcur, vcur, d[:, 0:G], ALU.add)

        P.fence()
        for c in range(nchk):
            cs = slice(c * 128, (c + 1) * 128)
            c1 = slice(1 + c * 128, 1 + (c + 1) * 128)
            tq = ctmp()
            P.act(tq[0:64, :], pa[0:64, 24, c1], AF.Exp, scale=2.0)
            P.ts(tq[0:64, :], tq[0:64, :], 1.0, ALU.add)
            P.recip(tq[0:64, :], tq[0:64, :])
            P.ts(twb[0:64, :], tq[0:64, :], -2.0, ALU.mult, 1.0, ALU.add)
            P.copy(adb[64:128, :], pa[64:128, 24, c1])
            tq2 = ctmp()
            P.act(tq2[:, :], pa[:, 25, c1], AF.Exp, scale=-1.0)
            P.ts(tq2[:, :], tq2[:, :], 1.0, ALU.add)
            P.recip(tq2[:, :], tq2[:, :])
            P.copy(sgb[:, :], tq2[:, :], eng="pool")
            for j in range(8):
                js = slice(j * 128, (j + 1) * 128)
                ed_j, av_j, kk_j = ctmp()[:, 0:128], ctmp()[:, 0:128], ctmp()[:, 0:128]
                b3 = bt3[j % 2]
                r_, k_, v_ = pa[:, j, c1], pa[:, 8 + j, c1], pa[:, 16 + j, c1]
                pw = k.nextA()
                P.mm(pw[:, 0:128], w2s[0:64, js], twb[0:64, :])
                P.mm(pw[:, 128:256], a2s[64:128, js], adb[64:128, :])
                P.mm(pw[:, 256:384], g2s[:, js], sgb[:, :])
                t1 = ctmp()
                P.act(t1[:, 0:128], pw[:, 0:128], AF.Exp, scale=-1.0, bias=nw0[:, j:j + 1])
                P.act(t1[:, 0:128], t1[:, 0:128], AF.Ln, bias=k.eps_t[:, 1:2])
                P.act(ed_j, t1[:, 0:128], AF.Exp, scale=-1.0, bias=cm[:, 0:1])
                P.act(av_j, pw[:, 128:256], AF.Exp, scale=-1.0, bias=na0[:, j:j + 1])
                P.ts(av_j, av_j, 1.0, ALU.add)
                P.recip(av_j, av_j)
                P.copy(gfm[:, j, :], pw[:, 256:384])
                kq = ctmp()
                P.ts(kq[:, 0:128], k_, fmc(l, C_KK + j), ALU.mult)
                sq2 = sqb2[j % 2]
                P.tt(sq2[:, 0, :], kq[:, 0:128], kq[:, 0:128], ALU.mult, eng="pool")
                pn = k.nextB()
                P.mm(pn[:, 0:128], blk, sq2[:, 0, :])
                rn = ctmp()
                P.act(rn[:, 0:128], pn[:, 0:128], AF.Ln, bias=cm[:, 1:2])
                P.act(rn[:, 0:128], rn[:, 0:128], AF.Exp, scale=-0.5)
                P.tt(kk_j, kq[:, 0:128], rn[:, 0:128], ALU.mult)
                t2 = ctmp()
                P.ts(t2[:, 0:128], av_j, -1.0, ALU.add, fmc(l, C_KA + j), ALU.mult)
                P.stt(k_, t2[:, 0:128], 1.0, k_, ALU.add, ALU.mult)
                P.stt(sq2[:, 1, :], r_, fmc(l, C_RK + j), k_, ALU.mult, ALU.mult)
                P.mm(pn[:, 128:256], blk, sq2[:, 1, :])
                P.tt(bonus[:, j, :], pn[:, 128:256], v_, ALU.mult)
                P.copy(b3[:, 2, :], v_, eng="act")
                cw = ctmp()
                P.scan(cw[:, 0:128], k.onesf_t[:, 0:128], ed_j, 0.0)
                e_in = ctmp()
                P.act(e_in[:, 0:128], cw[:, 0:128], AF.Exp, scale=-1.0)
                P.copy(elast[:, j:j + 1], e_in[:, 127:128])
                e_inv = ctmp()
                P.act(e_inv[:, 0:128], cw[:, 0:128], AF.Exp)
                e_ex = ctmp()
                P.tt(e_ex[:, 0:128], cw[:, 0:128], ed_j, ALU.subtract)
                P.act(e_ex[:, 0:128], e_ex[:, 0:128], AF.Exp, scale=-1.0)
                P.stt(ar[:, j, 0:128], kk_j, -1.0, e_ex[:, 0:128], ALU.mult, ALU.mult)
                P.tt(ar[:, j, 128:256], r_, e_in[:, 0:128], ALU.mult)
                P.tt(t2[:, 0:128], kk_j, av_j, ALU.mult, eng="pool")
                P.tt(bet[:, j, :], t2[:, 0:128], e_inv[:, 0:128], ALU.mult)
                P.tt(ktl[:, j, :], k_, e_inv[:, 0:128], ALU.mult)
                P.ts(b3[:, 0, :], bet[:, j, :], elast[:, j:j + 1], ALU.mult)
                P.ts(b3[:, 1, :], ktl[:, j, :], elast[:, j:j + 1], ALU.mult)
                for q in range(3):
                    P.tr(k.psT[:, q * 128:(q + 1) * 128], b3[:, q, :], identb)
                P.copy(tm3[:, :, js], k.psT[:, 0:384].re(lambda a: a.rearrange("p (q c) -> p q c", q=3)),
                       eng=k.ev_eng())
            P.copy(H0b[:], Hst[l][:])
            for hb in range(4):
                at = AT[hb % 2]
                p0 = P0[hb % 2]
                pp0 = k.nextB()
                for hh in range(4):
                    h = hb * 4 + hh
                    j, pb = h // 2, 64 * (h % 2)
                    b1 = k.nextA()
                    P.mm(b1[:, 0:256], bet[pb:pb + 64, j, :], ar[pb:pb + 64, j, :])
                    P.mm(b1[:, 256:512], ktl[pb:pb + 64, j, :], ar[pb:pb + 64, j, :])
                    P.tt(at[:, hh, :], b1[:, :], mask4, ALU.mult)
                    P.mm(pp0[:, hh * 128:(hh + 1) * 128], ar[pb:pb + 64, j, 0:128], bet[pb:pb + 64, j, :])
                P.tt(p0[:], pp0[:, :].re(lambda a: a.rearrange("p (h s) -> p h s", h=4)),
                     mstr.re(lambda a: a.unsqueeze(1).broadcast_to([128, 4, 128])), ALU.mult)
                px = k.nextB()
                for hh in range(4):
                    h = hb * 4 + hh
                    j, pb = h // 2, 64 * (h % 2)
                    P.mm(px[:, hh * 64:(hh + 1) * 64], ar[pb:pb + 64, j, 0:128], H0b[pb:pb + 64, j, :],
                         start=True, stop=False)
                    P.mm(px[:, hh * 64:(hh + 1) * 64], at[:, hh, 256:384], vtm[:, h * 64:(h + 1) * 64],
                         start=False, stop=True)
                xcur = Xb[0]
                P.copy(xcur[:], px[:, 0:256].re(lambda a: a.rearrange("p (h v) -> p h v", h=4)), eng="act")
                for step in range(7):
                    def Q(hh):
                        return at[:, hh, 0:128] if step == 0 else QP[(step - 1) % 2][:, hh, 0:128]

                    def Pm(hh):
                        return p0[:, hh, :] if step == 0 else QP[(step - 1) % 2][:, hh, 128:256]
                    px = k.nextB()
                    for hh in range(4):
                        P.mm(px[:, hh * 64:(hh + 1) * 64], Q(hh), xcur[:, hh, :])
                    if step < 6:
                        xn = Xb[(step + 1) % 2]
                        P.tt(xn[:], px[:, 0:256].re(lambda a: a.rearrange("p (h v) -> p h v", h=4)), xcur[:], ALU.add)
                        xcur = xn
                        qn = QP[step % 2]
                        for half in range(2):
                            pq = k.nextA()
                            for h2 in range(2):
                                hh = half * 2 + h2
                                P.mm(pq[:, h2 * 256:h2 * 256 + 128], Pm(hh), Q(hh))
                                P.mm(pq[:, h2 * 256 + 128:h2 * 256 + 256], Q(hh), Pm(hh))
                            P.copy(qn[:, half * 2:half * 2 + 2, :],
                                   pq[:, :].re(lambda a: a.rearrange("p (h s) -> p h s", h=2)), eng="act")
                    else:
                        P.tt(Ub[:, hb * 4:(hb + 1) * 4, :],
                             px[:, 0:256].re(lambda a: a.rearrange("p (h v) -> p h v", h=4)), xcur[:], ALU.add)
                py = k.nextB()
                for hh in range(4):
                    h = hb * 4 + hh
                    j, pb = h // 2, 64 * (h % 2)
                    P.mm(py[:, hh * 64:(hh + 1) * 64], ar[pb:pb + 64, j, 128:256], H0b[pb:pb + 64, j, :],
                         start=True, stop=False)
                    P.mm(py[:, hh * 64:(hh + 1) * 64], at[:, hh, 128:256], Ub[:, h, :], start=False, stop=False)
                    P.mm(py[:, hh * 64:(hh + 1) * 64], at[:, hh, 384:512], vtm[:, h * 64:(h + 1) * 64],
                         start=False, stop=True)
                P.copy(ysb[:, hb * 4:(hb + 1) * 4, :], py[:, 0:256].re(lambda a: a.rearrange("p (h v) -> p h v", h=4)),
                       eng="dve")
            for j in range(8):
                ph = k.nextA()
                P.mm(ph[:, 0:128], btm[:, j * 128:(j + 1) * 128],
                     Ub[:, 2 * j:2 * j + 2, :].re(lambda a: a.rearrange("p h v -> p (h v)")), start=True, stop=False)
                P.mm(ph[:, 0:128], ktm[:, j * 128:(j + 1) * 128], vtm[:, j * 128:(j + 1) * 128], start=False, stop=True)
                for hp in range(2):
                    pb = 64 * hp
                    P.stt(Hst[l][pb:pb + 64, j, :], Hst[l][pb:pb + 64, j, :], elast[pb:pb + 64, j:j + 1],
                          ph[pb:pb + 64, hp * 64:(hp + 1) * 64], ALU.mult, ALU.add)
            s1, s2, mean, var = st4[:, 0, :], st4[:, 1, :], st4[:, 2, :], st4[:, 3, :]
            P.red(s1, ysb[:])
            P.tt(ysq[:], ysb[:], ysb[:], ALU.mult, eng="pool")
            P.red(s2, ysq[:])
            P.ts(mean, s1, 1.0 / 64, ALU.mult)
            P.tt(var, mean, mean, ALU.mult)
            P.stt(var, s2, 1.0 / 64, var, ALU.mult, ALU.subtract)
            P.ts(var, var, 64e-5, ALU.add)
            P.act(var, var, AF.Ln)
            P.act(var, var, AF.Exp, scale=-0.5)
            P.tt(ysb[:], ysb[:], bc(mean, 64), ALU.subtract)
            P.tt(ysb[:], ysb[:], bc(var, 64), ALU.mult)
            for kt in range(8):
                pp = k.nextA()
                P.tr(pp[:, 0:128], ysb[:, 2 * kt:2 * kt + 2, :].re(lambda a: a.rearrange("p h v -> p (h v)")), ident)
                t3 = ctmp()
                P.ts(t3[:, 0:128], pp[:, 0:128], fmc(l, C_LNW + kt), ALU.mult, fmc(l, C_LNB + kt), ALU.add)
                P.tt(t3[:, 0:128], t3[:, 0:128], bonus[:, kt, :], ALU.add)
                P.tt(ys[0][:, kt, cs], t3[:, 0:128], gfm[:, kt, :], ALU.mult)
    return mix


MIXER_BUILDERS["a"] = build_rwkv
```
